# Optimizing a Trainium2 kernel written in Bass

```python
import math
import jax, jax.numpy as jnp
from jax import lax
import numpy as np

D_MODEL = 2048
BATCH = 4
SEQ = 4096
DEPTH = 2

D_FOURIER = D_MODEL // 2
FOURIER_GROUPS = 8
FOURIER_GROUP_DIM = D_FOURIER // FOURIER_GROUPS
V_HEAD_DIM = 128
D_ATTN = D_MODEL // 2
N_HEADS = D_ATTN // V_HEAD_DIM
QK_NOPE_DIM = 128
QK_ROPE_DIM = 64
QK_HEAD_DIM = QK_NOPE_DIM + QK_ROPE_DIM
Q_LORA_RANK = D_MODEL // 4
KV_LORA_RANK = D_MODEL // 8
D_MIX = D_FOURIER + D_ATTN
IN_SPLITS = (D_FOURIER, D_FOURIER, Q_LORA_RANK, KV_LORA_RANK, QK_ROPE_DIM, D_ATTN)
D_IN = sum(IN_SPLITS)
ROPE_THETA = 10000.0
Q_BLOCK = 128
NORM_EPS = 1e-6
DEEPNORM_ALPHA = (2 * DEPTH) ** 0.25
DEEPNORM_BETA = (8 * DEPTH) ** -0.25

kernel_name = "hybrid_fnet_mla_deepnorm_adaln"


def _layernorm(x, g=None, b=None):
    xf = x.astype(jnp.float32)
    mu = jnp.mean(xf, axis=-1, keepdims=True)
    var = jnp.mean(jnp.square(xf - mu), axis=-1, keepdims=True)
    y = (xf - mu) * lax.rsqrt(var + NORM_EPS)
    if g is not None:
        y = y * g.astype(jnp.float32) + b.astype(jnp.float32)
    return y.astype(x.dtype)


def _rmsnorm(x, g):
    xf = x.astype(jnp.float32)
    y = xf * lax.rsqrt(jnp.mean(jnp.square(xf), axis=-1, keepdims=True) + NORM_EPS)
    return (y * g.astype(jnp.float32)).astype(x.dtype)


def _rope_tables(positions, dtype):
    inv_freq = ROPE_THETA ** (-jnp.arange(0, QK_ROPE_DIM, 2, dtype=jnp.float32) / QK_ROPE_DIM)
    ang = positions.astype(jnp.float32)[..., None] * inv_freq
    return jnp.cos(ang).astype(dtype), jnp.sin(ang).astype(dtype)


def _apply_rope(x, cos, sin):
    x1, x2 = jnp.split(x, 2, axis=-1)
    return jnp.concatenate([x1 * cos - x2 * sin, x2 * cos + x1 * sin], axis=-1)


def _fourier_mix(u, w_fmix):
    B, S, _ = u.shape
    ug = u.reshape(B, S, FOURIER_GROUPS, FOURIER_GROUP_DIM).astype(jnp.float32)
    f = jnp.real(jnp.fft.fftn(ug, axes=(1, 3), norm="ortho")).astype(u.dtype)
    y = jnp.einsum("bsgc,gcd->bsgd", f, w_fmix)
    return y.reshape(B, S, D_FOURIER)


def _mla_attention(q_nope, q_rope, k_nope, k_rope, v):
    B, S, H, _ = q_nope.shape
    nb = S // Q_BLOCK
    scale = 1.0 / math.sqrt(QK_HEAD_DIM)
    qn = q_nope.reshape(B, nb, Q_BLOCK, H, QK_NOPE_DIM).transpose(1, 0, 2, 3, 4)
    qr = q_rope.reshape(B, nb, Q_BLOCK, H, QK_ROPE_DIM).transpose(1, 0, 2, 3, 4)

    def block(args):
        qn_b, qr_b = args
        s = (jnp.einsum("bqhd,bkhd->bhqk", qn_b, k_nope)
             + jnp.einsum("bqhr,bkr->bhqk", qr_b, k_rope)).astype(jnp.float32) * scale
        p = jax.nn.softmax(s, axis=-1).astype(v.dtype)
        return jnp.einsum("bhqk,bkhd->bqhd", p, v)

    o = lax.map(block, (qn, qr))
    return o.transpose(1, 0, 2, 3, 4).reshape(B, S, H * V_HEAD_DIM)


def setup_inputs(seed: int = 0) -> dict:
    key = jax.random.key(seed)
    ks = jax.random.split(key, 16)
    f32 = jnp.float32
    x = jax.random.normal(ks[0], (BATCH, SEQ, D_MODEL), f32)
    c = jax.random.normal(ks[1], (BATCH, D_MODEL), f32)
    offs = jax.random.randint(ks[2], (BATCH, 1), 0, SEQ, dtype=jnp.int32)
    positions = (jnp.arange(SEQ, dtype=jnp.int32)[None, :] + offs).astype(jnp.int32)
    w_ada = jax.random.normal(ks[3], (DEPTH, D_MODEL, 3 * D_MODEL), f32) * D_MODEL ** -0.5
    b_ada = 0.02 * jax.random.normal(ks[4], (DEPTH, 3 * D_MODEL), f32)
    w_in = jax.random.normal(ks[5], (DEPTH, D_MODEL, D_IN), f32) * D_MODEL ** -0.5
    q_norm = 1.0 + 0.02 * jax.random.normal(ks[6], (DEPTH, Q_LORA_RANK), f32)
    w_q_b = jax.random.normal(ks[7], (DEPTH, Q_LORA_RANK, N_HEADS * QK_HEAD_DIM), f32) * Q_LORA_RANK ** -0.5
    kv_norm = 1.0 + 0.02 * jax.random.normal(ks[8], (DEPTH, KV_LORA_RANK), f32)
    w_kv_b = jax.random.normal(ks[9], (DEPTH, KV_LORA_RANK, N_HEADS * (QK_NOPE_DIM + V_HEAD_DIM)), f32) * KV_LORA_RANK ** -0.5
    w_fmix = jax.random.normal(ks[10], (DEPTH, FOURIER_GROUPS, FOURIER_GROUP_DIM, FOURIER_GROUP_DIM), f32) * FOURIER_GROUP_DIM ** -0.5
    w_out = jax.random.normal(ks[11], (DEPTH, D_MIX, D_MODEL), f32) * (D_MIX ** -0.5 * DEEPNORM_BETA)
    ln_g = 1.0 + 0.02 * jax.random.normal(ks[12], (DEPTH, D_MODEL), f32)
    ln_b = 0.02 * jax.random.normal(ks[13], (DEPTH, D_MODEL), f32)
    return {"x": x, "c": c, "positions": positions, "w_ada": w_ada, "b_ada": b_ada,
            "w_in": w_in, "q_norm": q_norm, "w_q_b": w_q_b, "kv_norm": kv_norm,
            "w_kv_b": w_kv_b, "w_fmix": w_fmix, "w_out": w_out, "ln_g": ln_g, "ln_b": ln_b}


def reference(x, c, positions, w_ada, b_ada, w_in, q_norm, w_q_b, kv_norm, w_kv_b,
              w_fmix, w_out, ln_g, ln_b):
    B, S, D = x.shape
    cos, sin = _rope_tables(positions, x.dtype)
    cos_q, sin_q = cos[:, :, None, :], sin[:, :, None, :]
    c_act = jax.nn.silu(c)
    split_pts = list(np.cumsum(IN_SPLITS)[:-1])
    for l in range(DEPTH):
        mod = c_act @ w_ada[l] + b_ada[l]
        shift, scale, gate = jnp.split(mod, 3, axis=-1)
        h = _layernorm(x) * (1.0 + scale[:, None, :]) + shift[:, None, :]
        p = h @ w_in[l]
        u_f, z_f, cq, ckv, k_r, z_a = jnp.split(p, split_pts, axis=-1)
        y_f = _fourier_mix(u_f, w_fmix[l]) * jax.nn.silu(z_f)
        q = (_rmsnorm(cq, q_norm[l]) @ w_q_b[l]).reshape(B, S, N_HEADS, QK_HEAD_DIM)
        q_nope, q_rope = q[..., :QK_NOPE_DIM], _apply_rope(q[..., QK_NOPE_DIM:], cos_q, sin_q)
        kv = (_rmsnorm(ckv, kv_norm[l]) @ w_kv_b[l]).reshape(B, S, N_HEADS, QK_NOPE_DIM + V_HEAD_DIM)
        k_nope, v = kv[..., :QK_NOPE_DIM], kv[..., QK_NOPE_DIM:]
        k_rope = _apply_rope(k_r, cos, sin)
        y_a = _mla_attention(q_nope, q_rope, k_nope, k_rope, v) * jax.nn.silu(z_a)
        y = jnp.concatenate([y_f, y_a], axis=-1) @ w_out[l]
        x = _layernorm(DEEPNORM_ALPHA * x + gate[:, None, :] * y, ln_g[l], ln_b[l])
    return x
```

```python
import math
from contextlib import ExitStack

import numpy as np
import ml_dtypes

import concourse.bass as bass
import concourse.mybir as mybir
from concourse.bass_utils import run_bass_kernel_spmd

F32 = mybir.dt.float32
BF16 = mybir.dt.bfloat16
I32 = mybir.dt.int32
AF = mybir.ActivationFunctionType
ALU = mybir.AluOpType

D = 2048
SEQ = 4096
T = 2048
NT = 16
DEPTH = 2
DIN = 3904
NH = 8
EPS = 1e-6
ALPHA = (2 * DEPTH) ** 0.25
SM_SCALE = 1.0 / math.sqrt(192.0)
FNORM = 1.0 / math.sqrt(4096.0 * 128.0)
PI = math.pi
C1 = 6.28125
C2 = 2.0 * math.pi - 6.28125

import os
NO_CC = bool(os.environ.get("NO_CC"))
LITE = bool(os.environ.get("LITE"))
WL = 1 if LITE else 2
NTAB = 1 if LITE else 8
ENGS = ["sync", "scalar", "vector", "gpsimd", "tensor"]
CENGS = ["scalar", "vector", "gpsimd", "tensor"]


class Sched:
    def __init__(self, nc, es):
        self.nc = nc
        self.es = es
        self.q = {e: [] for e in ENGS}
        self.sem = {}
        self.cnt = {}
        for e in CENGS:
            self.sem[e] = es.enter_context(nc.semaphore("s_" + e))
            self.cnt[e] = 0
        self.waited = {e: {} for e in ENGS}
        self.chans = []
        self.named = {}
        self.tmp_i = {}

    def chan(self, name):
        if name in self.named:
            return self.named[name]
        sem = self.es.enter_context(self.nc.semaphore(name))
        ch = [sem, 0]
        self.chans.append(ch)
        self.named[name] = ch
        return ch

    def tmpchan(self, eng="sync"):
        self.tmp_i[eng] = self.tmp_i.get(eng, 0) + 1
        return self.chan(f"tmp_{eng}{self.tmp_i[eng]}")

    def _waits(self, eng, deps):
        flat = []
        for d in deps:
            if d is None:
                continue
            if isinstance(d, list):
                flat.extend(x for x in d if x is not None)
            else:
                flat.append(d)
        for d in flat:
            sem, val = d
            key = id(sem)
            if self.waited[eng].get(key, 0) >= val:
                continue
            self.waited[eng][key] = val
            self.q[eng].append(lambda e, sem=sem, val=val: e.wait_ge(sem, val))

    def op(self, eng, fn, deps=(), ev=True):
        self._waits(eng, deps)
        if ev:
            self.cnt[eng] += 1
            sem = self.sem[eng]
            self.q[eng].append(lambda e, fn=fn, sem=sem: fn(e).then_inc(sem, 1))
            return (sem, self.cnt[eng])
        self.q[eng].append(lambda e, fn=fn: fn(e))
        return None

    def dma(self, eng, out, in_, chan, deps=()):
        self._waits(eng, deps)
        chan[1] += 16
        sem = chan[0]
        self.q[eng].append(lambda e, out=out, in_=in_, sem=sem: e.dma_start(out=out, in_=in_).then_inc(sem, 16))
        return (sem, chan[1])

    def raw(self, eng, fn, deps=()):
        self._waits(eng, deps)
        self.q[eng].append(fn)

    def wait(self, eng, deps):
        self._waits(eng, deps)

    def barrier(self, extra=()):
        evs = [(self.sem[e], self.cnt[e]) for e in CENGS if self.cnt[e] > 0]
        evs += [(c[0], c[1]) for c in self.chans if c[1] > 0]
        evs += list(extra)
        for e in ENGS:
            self._waits(e, evs)
        self.tmp_i = {}

    def flush(self):
        nc = self.nc
        with nc.Block() as block:
            for e in ENGS:
                if not self.q[e]:
                    continue

                def f(engobj, e=e):
                    for c in self.q[e]:
                        c(engobj)

                getattr(block, e)(f)
        self.q = {e: [] for e in ENGS}


class Ring:
    def __init__(self, items):
        self.items = list(items)
        self.free = [None] * len(self.items)
        self.i = 0

    def get(self):
        j = self.i % len(self.items)
        self.i += 1
        return j, self.items[j], self.free[j]

    def release(self, j, ev):
        self.free[j] = ev


def build_program(stop=None, debug=()):
    nc = bass.Bass("TRN2", target_bir_lowering=False)
    dt = nc.dram_tensor
    x_in = dt("x", [T, D], F32, kind="ExternalInput")
    cvec = dt("cvec", [128, 16], F32, kind="ExternalInput")
    pos = dt("pos", [1, T], I32, kind="ExternalInput")
    w_ada = dt("w_ada", [1, 128, 512] if LITE else [WL, D, 3 * D], F32, kind="ExternalInput")
    b_ada = dt("b_ada", [WL, 3 * D], F32, kind="ExternalInput")
    w_in = dt("w_in", [WL, D, DIN], F32, kind="ExternalInput")
    qn_t = dt("qn_t", [WL, 128, 4], F32, kind="ExternalInput")
    w_q_b = dt("w_q_b", [WL, 512, 1536], F32, kind="ExternalInput")
    kvn_t = dt("kvn_t", [WL, 128, 2], F32, kind="ExternalInput")
    w_kv_b = dt("w_kv_b", [WL, 256, 2048], F32, kind="ExternalInput")
    w_fmix = dt("w_fmix", [WL, 8, 128, 128], F32, kind="ExternalInput")
    w_out = dt("w_out", [WL, D, D], F32, kind="ExternalInput")
    ln_g = dt("ln_g", [WL, D], F32, kind="ExternalInput")
    ln_b = dt("ln_b", [WL, D], F32, kind="ExternalInput")
    tab = dt("tab", [NTAB, SEQ, 512], BF16, kind="ExternalInput")
    cc_in = dt("cc", [128, 2, 128], BF16, kind="ExternalInput")
    sc_in = dt("sc", [128, 2, 128], BF16, kind="ExternalInput")
    invf = dt("invf", [64, 1], F32, kind="ExternalInput")
    ident_in = dt("ident", [128, 128], BF16, kind="ExternalInput")
    out = dt("out", [T, D], F32, kind="ExternalOutput")

    modv = dt("modv", [DEPTH, 3 * D], F32)
    uTokA = [dt(f"uTokA{l}", [1024, 1024], BF16) for l in range(DEPTH)]
    uTokB = [dt(f"uTokB{l}", [1024, 1024], BF16) for l in range(DEPTH)]
    uAllA = [dt(f"uAllA{l}", [2048, 1024], BF16) for l in range(DEPTH)]
    uAllB = [dt(f"uAllB{l}", [2048, 1024], BF16) for l in range(DEPTH)]
    kvT = [dt(f"kvT{l}", [320, T], BF16) for l in range(DEPTH)]
    kvAll = [dt(f"kvAll{l}", [640, T], BF16) for l in range(DEPTH)]
    szT = [dt(f"szT{l}", [2048, T], BF16) for l in range(DEPTH)]
    qT = [dt(f"qT{l}", [NH, 192, T], BF16) for l in range(DEPTH)]
    yT = [dt(f"yT{l}", [2048, T], BF16) for l in range(DEPTH)]
    x1 = dt("x1", [T, D], F32)

    dbg = {}
    for name, shape, dty in debug:
        dbg[name] = dt("dbg_" + name, shape, dty, kind="ExternalOutput")

    PAIRS = [[0, 1]] if os.environ.get('SIM2') else [[0, 1], [2, 3], [4, 5], [6, 7]]

    with ExitStack() as es:
        S = Sched(nc, es)
        cc_sem = es.enter_context(nc.semaphore("cc_sem"))
        cc_cnt = [0]
        ps = es.enter_context(nc.psum_tensor("ps", [128, 8, 512], F32))
        cos2 = es.enter_context(nc.sbuf_tensor("sb_cos2", [64, T], F32))
        sin2 = es.enter_context(nc.sbuf_tensor("sb_sin2", [64, T], F32))
        ident = es.enter_context(nc.sbuf_tensor("sb_ident", [128, 128], BF16))
        ones_bf = es.enter_context(nc.sbuf_tensor("sb_ones_bf", [128, 128], BF16))
        ch_st = S.chan("ch_st")
        cact = es.enter_context(nc.sbuf_tensor("sb_cact", [128, 16], BF16))
        ch_wa = [S.chan("ch_wa0"), S.chan("ch_wa1")]
        ch_bc = [S.chan("ch_bc0"), S.chan("ch_bc1")]
        ch_mc = [S.chan("ch_mc0"), S.chan("ch_mc1")]
        mod_state = {"it": 0, "mfree": [None, None], "e_ca": None}

        def mod_load(l, c0, ncol, wa_ring):
            j, wt, fr = wa_ring.get()
            src = w_ada[l % WL, :, c0:c0 + ncol].rearrange("(kc p) n -> p kc n", p=128)
            e_w = S.dma("gpsimd", wt[:, :, 0:ncol], src, ch_wa[j], deps=[fr])
            return (l, c0, ncol, j, wt, e_w)

        def mod_chunk(l, c0, ncol, wa_ring, bank_ring, badac, modc):
            mod_compute(mod_load(l, c0, ncol, wa_ring), wa_ring, bank_ring, badac, modc)

        def mod_compute(hnd, wa_ring, bank_ring, badac, modc):
            l, c0, ncol, j, wt, e_w = hnd
            mfree = mod_state["mfree"]
            bj, bank, bfree = bank_ring.get()
            i2 = mod_state["it"] % 2
            mod_state["it"] += 1
            e_bc = S.dma("sync", badac[i2][:, 0:ncol], b_ada[l % WL:l % WL + 1, c0:c0 + ncol], ch_bc[i2], deps=[mfree[i2]])
            em = None
            for kc in range(16):
                em = S.op("tensor", lambda t, bank=bank, wt=wt, kc=kc: t.matmul(
                    ps[0:1, bank, 0:ncol], cact[:, kc:kc + 1], wt[:, kc, 0:ncol], start=(kc == 0), stop=(kc == 15)),
                    deps=[e_w, mod_state["e_ca"], bfree] if kc == 0 else [], ev=(kc == 15))
            wa_ring.release(j, em)
            ee = S.op("vector", lambda v, bank=bank, i2=i2: v.tensor_tensor(
                out=modc[i2][0:1, 0:ncol], in0=ps[0:1, bank, 0:ncol], in1=badac[i2][0:1, 0:ncol], op=ALU.add),
                deps=[em, e_bc, mfree[i2]])
            bank_ring.release(bj, ee)
            e_m = S.dma("sync", modv[l:l + 1, c0:c0 + ncol], modc[i2][:, 0:ncol], ch_mc[i2], deps=[ee])
            if "mod" in dbg:
                e_m = S.dma("sync", dbg["mod"][l:l + 1, c0:c0 + ncol], modc[i2][:, 0:ncol], ch_mc[i2], deps=[ee])
            mfree[i2] = e_m

        def collective(src, dst, deps):
            S.wait("gpsimd", deps)
            if NO_CC:
                return None
            cc_cnt[0] += 1

            def f(g, src=src, dst=dst):
                g.collective_compute("AllGather", ALU.bypass, replica_groups=PAIRS,
                                     ins=[src.ap().opt()], outs=[dst.ap().opt()]).then_inc(cc_sem)
            S.q["gpsimd"].append(f)
            return (cc_sem, cc_cnt[0])

        with ExitStack() as p0:
            sb = lambda name, shape, dty: p0.enter_context(nc.sbuf_tensor("p0_" + name, shape, dty))
            cv = sb("cv", [128, 16], F32)
            badac = [sb(f"badac{i}", [1, 512], F32) for i in range(2)]
            modc = [sb(f"modc{i}", [1, 512], F32) for i in range(2)]
            wa = [sb(f"wa{i}", [128, 16, 512], BF16) for i in range(2)]
            posi = sb("posi", [64, T], I32)
            posf = sb("posf", [64, T], F32)
            ang = sb("ang", [64, T], F32)
            ki = sb("ki", [64, T], I32)
            kf = sb("kf", [64, T], F32)
            rr = sb("rr", [64, T], F32)
            tmpa = sb("tmpa", [64, T], F32)
            tmpb = sb("tmpb", [64, T], F32)
            invf_sb = sb("invf_sb", [64, 1], F32)

            e_id = S.dma("sync", ident[:], ident_in[:, :], S.tmpchan())
            e_cv = S.dma("sync", cv[:], cvec[:, :], S.tmpchan())
            e_ps = S.dma("sync", posi[:], pos.ap().partition_broadcast(64), S.tmpchan())
            e_if = S.dma("sync", invf_sb[:], invf[:, :], S.tmpchan())
            e_c0 = S.op("vector", lambda v: v.memset(ones_bf[:], 1.0))
            e_ca = S.op("scalar", lambda a: a.activation(out=cact[:], in_=cv[:], func=AF.Silu), deps=[e_cv])

            e = S.op("vector", lambda v: v.tensor_copy(out=posf[:], in_=posi[:]), deps=[e_ps])
            e = S.op("vector", lambda v: v.tensor_scalar(out=ang[:], in0=posf[:], scalar1=invf_sb[:, 0:1], scalar2=None, op0=ALU.mult), deps=[e, e_if])
            e = S.op("vector", lambda v: v.tensor_scalar(out=ki[:], in0=ang[:], scalar1=1.0 / (2 * PI), scalar2=None, op0=ALU.mult), deps=[e])
            e = S.op("vector", lambda v: v.tensor_copy(out=kf[:], in_=ki[:]), deps=[e])
            e = S.op("vector", lambda v: v.scalar_tensor_tensor(out=tmpa[:], in0=kf[:], scalar=-C1, in1=ang[:], op0=ALU.mult, op1=ALU.add), deps=[e])
            e_r = S.op("vector", lambda v: v.scalar_tensor_tensor(out=rr[:], in0=kf[:], scalar=-C2, in1=tmpa[:], op0=ALU.mult, op1=ALU.add), deps=[e])

            def wrap_sin(src_ap, dst_tab, shift, dep):
                e = S.op("vector", lambda v: v.tensor_scalar(out=tmpa[:], in0=src_ap, scalar1=shift, scalar2=None, op0=ALU.add), deps=[dep])
                e = S.op("vector", lambda v: v.tensor_scalar(out=tmpb[:], in0=tmpa[:], scalar1=PI, scalar2=2 * PI, op0=ALU.is_gt, op1=ALU.mult), deps=[e])
                e = S.op("vector", lambda v: v.tensor_tensor(out=tmpa[:], in0=tmpa[:], in1=tmpb[:], op=ALU.subtract), deps=[e])
                e = S.op("vector", lambda v: v.tensor_scalar(out=tmpb[:], in0=tmpa[:], scalar1=-PI, scalar2=2 * PI, op0=ALU.is_lt, op1=ALU.mult), deps=[e])
                e = S.op("vector", lambda v: v.tensor_tensor(out=tmpa[:], in0=tmpa[:], in1=tmpb[:], op=ALU.add), deps=[e])
                e = S.op("vector", lambda v: v.tensor_scalar(out=tmpa[:], in0=tmpa[:], scalar1=PI, scalar2=-PI, op0=ALU.min, op1=ALU.max), deps=[e])
                e = S.op("scalar", lambda a: a.activation(out=dst_tab, in_=tmpa[:], func=AF.Sin), deps=[e])
                return e

            e_s = wrap_sin(rr[:], sin2[:], 0.0, e_r)
            e_c = wrap_sin(rr[:], cos2[:], PI / 2, e_s)

            wa_ring = Ring(wa)
            bank_ring = Ring([0, 1])
            mod_state["e_ca"] = e_ca
            if LITE:
                e_z = S.op("vector", lambda v: v.memset(modc[0][:], 0.1))
                for l in range(DEPTH):
                    for nb in range(12):
                        S.dma("sync", modv[l:l + 1, nb * 512:(nb + 1) * 512], modc[0][:], ch_st, deps=[e_z])
            else:
                for nb in range(12):
                    mod_chunk(0, nb * 512, 512, wa_ring, bank_ring, badac, modc)
                if "mod" in dbg:
                    for nb in range(12):
                        mod_chunk(1, nb * 512, 512, wa_ring, bank_ring, badac, modc)
            if "rope" in dbg:
                S.dma("sync", dbg["rope"][0], cos2[:], ch_st, deps=[e_c])
                S.dma("sync", dbg["rope"][1], sin2[:], ch_st, deps=[e_c])
            S.barrier()
            S.flush()

        x_src = x_in
        for l in range(DEPTH):
            lw = l % WL
            if stop == "p0":
                break
            x_dst = x1 if l == 0 else out
            with ExitStack() as pa:
                sbA = lambda name, shape, dty: pa.enter_context(nc.sbuf_tensor(f"pa{l}_" + name, shape, dty))
                hT = sbA("hT", [128, 16, T], BF16)
                with ExitStack() as pa1:
                    sb = lambda name, shape, dty: pa1.enter_context(nc.sbuf_tensor(f"pa1{l}_" + name, shape, dty))
                    xt = [sb(f"xt{i}", [128, D], F32) for i in range(2)]
                    hb = [sb(f"hb{i}", [128, D], BF16) for i in range(2)]
                    sc_rep = sb("sc_rep", [128, D], F32)
                    sh_rep = sb("sh_rep", [128, D], F32)
                    st = [sb(f"st{i}", [128, 4, 6], F32) for i in range(2)]
                    mv = [sb(f"mv{i}", [128, 2], F32) for i in range(2)]
                    sm = [sb(f"sm{i}", [128, 4], F32) for i in range(2)]
                    e_sh = S.dma("sync", sh_rep[:], modv[l:l + 1, 0:D].partition_broadcast(128), S.tmpchan())
                    e_sc = S.dma("sync", sc_rep[:], modv[l:l + 1, D:2 * D].partition_broadcast(128), S.tmpchan())
                    e_sc = S.op("vector", lambda v: v.tensor_scalar(out=sc_rep[:], in0=sc_rep[:], scalar1=1.0, scalar2=None, op0=ALU.add), deps=[e_sc])
                    ch_x = [S.chan(f"ch_x{i}") for i in range(2)]
                    xfree = [None, None]
                    hfree = [None, None]
                    tp_ring = Ring([0, 1, 2, 3])
                    for t in range(NT):
                        i = t % 2
                        e_x = S.dma("sync", xt[i][:], x_src[t * 128:(t + 1) * 128, :], ch_x[i], deps=[xfree[i]])
                        e_st = None
                        for k in range(4):
                            e_st = S.op("vector", lambda v, i=i, k=k: v.bn_stats(out=st[i][:, k, :], in_=xt[i][:, k * 512:(k + 1) * 512]),
                                        deps=[e_x], ev=(k == 3))
                        e_ag = S.op("vector", lambda v, i=i: v.bn_aggr(out=mv[i][:], in_=st[i][:].rearrange("p a b -> p (a b)")), deps=[e_st])
                        e1 = S.op("vector", lambda v, i=i: v.tensor_scalar(out=sm[i][:, 0:1], in0=mv[i][:, 1:2], scalar1=EPS, scalar2=None, op0=ALU.add), deps=[e_ag])
                        e2 = S.op("scalar", lambda a, i=i: a.activation(out=sm[i][:, 1:2], in_=sm[i][:, 0:1], func=AF.Sqrt), deps=[e1])
                        e3 = S.op("vector", lambda v, i=i: v.reciprocal(out=sm[i][:, 2:3], in_=sm[i][:, 1:2]), deps=[e2])
                        e4 = S.op("vector", lambda v, i=i: v.tensor_scalar(out=sm[i][:, 3:4], in0=mv[i][:, 0:1], scalar1=sm[i][:, 2:3], scalar2=-1.0, op0=ALU.mult, op1=ALU.mult), deps=[e3])
                        e5 = S.op("scalar", lambda a, i=i: a.activation(out=xt[i][:], in_=xt[i][:], func=AF.Identity, bias=sm[i][:, 3:4], scale=sm[i][:, 2:3]), deps=[e4])
                        e6 = S.op("vector", lambda v, i=i: v.tensor_tensor(out=xt[i][:], in0=xt[i][:], in1=sc_rep[:], op=ALU.mult), deps=[e5, e_sc])
                        e7 = S.op("vector", lambda v, i=i: v.tensor_tensor(out=hb[i][:], in0=xt[i][:], in1=sh_rep[:], op=ALU.add), deps=[e6, e_sh, hfree[i]])
                        xfree[i] = e7
                        e_tp = None
                        for g in range(4):
                            bj, bank, bfree = tp_ring.get()
                            pview = ps[:, bank, :].bitcast(BF16)
                            for jj in range(4):
                                kc = 4 * g + jj
                                e_tp = S.op("tensor", lambda tt, pview=pview, i=i, kc=kc, jj=jj: tt.transpose(
                                    pview[:, jj * 128:(jj + 1) * 128], hb[i][:, kc * 128:(kc + 1) * 128], ident[:]),
                                    deps=[e7, bfree, e_id] if jj == 0 else [], ev=(jj == 3))
                            e_ev = S.op("scalar", lambda a, pview=pview, g=g, t=t: a.copy(
                                out=hT[:, 4 * g:4 * g + 4, t * 128:(t + 1) * 128],
                                in_=pview[:, 0:512].rearrange("p (a b) -> p a b", a=4)), deps=[e_tp])
                            tp_ring.release(bj, e_ev)
                        hfree[i] = e_tp
                    if f"hT{l}" in dbg:
                        S.barrier()
                        S.dma("sync", dbg[f"hT{l}"].ap().rearrange("(kc p) t -> p kc t", p=128), hT[:], ch_st)
                    S.barrier()
                    S.flush()
                if stop == f"a1_{l}":
                    break
                with ExitStack() as pa2:
                    sb = lambda name, shape, dty: pa2.enter_context(nc.sbuf_tensor(f"pa2{l}_" + name, shape, dty))
                    wbuf = [sb(f"wbuf{i}", [128, 16, 512], BF16) for i in range(2)]
                    cqT = sb("cqT", [128, 4, T], BF16)
                    ckv_raw = sb("ckv_raw", [128, 2, T], F32)
                    rq_rep = sb("rq_rep", [128, T], F32)
                    rkv_rep = sb("rkv_rep", [128, T], F32)
                    sq = [sb(f"sq{i}", [128, 512], BF16) for i in range(2)]
                    evb = [sb(f"evb{i}", [128, 512], BF16) for i in range(4)]
                    tf = [sb(f"tf{i}", [128, 512], F32) for i in range(3)]
                    wq = sb("wq", [128, 4, 1536], BF16)
                    wqrot = sb("wqrot", [128, 4, 8, 64], BF16)
                    wkrot = sb("wkrot", [128, 16, 64], BF16)
                    gq = sb("gq", [128, 4], F32)
                    gkv = sb("gkv", [128, 2], F32)

                    e_gq = e_gkv = e_wq = None
                    if not os.environ.get("NO_G"):
                        e_gq = S.dma("sync", gq[:], qn_t[lw], S.tmpchan())
                        e_gkv = S.dma("sync", gkv[:], kvn_t[lw], S.tmpchan())
                    if not os.environ.get("NO_WQ"):
                        e_wq = S.dma("gpsimd", wq[:], w_q_b[lw].rearrange("(kc p) n -> p kc n", p=128), S.tmpchan("gpsimd"))

                    w_ring = Ring(wbuf)
                    ch_w = [S.chan("ch_w0"), S.chan("ch_w1")]
                    ev_ring = Ring(evb)
                    ch_ev = [S.chan(f"ch_ev{i}") for i in range(4)]
                    bank_ring = Ring([0, 1, 2, 3])
                    sq_ring = Ring(sq)
                    tf_ring = Ring(tf)

                    GROUPS = [("kv", 2560, 320), ("u", 0, 512), ("u", 512, 512), ("cq", 2048, 512),
                              ("z", 1024, 512), ("z", 1536, 512), ("z", 2880, 512), ("z", 3392, 512)]
                    SZROW = {1024: 0, 1536: 512, 2880: 1024, 3392: 1536}
                    loaded = {}

                    def load_group(gi):
                        kind, c0, ncol = GROUPS[gi]
                        j, wt, fr = w_ring.get()
                        src = w_in[lw, :, c0:c0 + ncol].rearrange("(kc p) n -> p kc n", p=128)
                        e = S.dma("gpsimd", wt[:, :, 0:ncol], src, ch_w[j], deps=[fr])
                        loaded[gi] = (j, wt, e)

                    def store(dst, j, ev_t, e_ev, npart=128, eng="gpsimd"):
                        if os.environ.get("NO_STORE"):
                            ev_ring.release(j, e_ev)
                            return e_ev
                        eng = os.environ.get("STORE_ENG", eng)
                        e_st = S.dma(eng, dst, ev_t[0:npart, :], ch_ev[j], deps=[e_ev])
                        ev_ring.release(j, e_st)
                        return e_st

                    def mm_group(bank, lhs_fn, rhs_fn, nk, deps, mpart=128, ncols=512):
                        em = None
                        for kc in range(nk):
                            em = S.op("tensor", lambda t, bank=bank, kc=kc: t.matmul(
                                ps[0:mpart, bank, 0:ncols], lhs_fn(kc), rhs_fn(kc), start=(kc == 0), stop=(kc == nk - 1)),
                                deps=deps if kc == 0 else [], ev=(kc == nk - 1))
                        return em

                    def rms_chunks(wt, e_w, nchunk, gvec, e_g, raw_dst, r_rep, dim, tb, ssq_bank, last_mm):
                        tsl = slice(tb * 512, (tb + 1) * 512)
                        e_ss = None
                        for c in range(nchunk):
                            bj, bank, bfree = bank_ring.get()
                            em = mm_group(bank, lambda kc, c=c: wt[:, kc, c * 128:(c + 1) * 128], lambda kc: hT[:, kc, tsl], 16, [e_w, bfree])
                            sj, sqt, sfree = sq_ring.get()
                            if os.environ.get("NO_SQ"):
                                e_sq = em
                            else:
                                e_sq = S.op("scalar", lambda a, bank=bank, sqt=sqt: a.activation(out=sqt[:], in_=ps[:, bank, :], func=AF.Square), deps=[em, sfree])
                            if os.environ.get("NO_RAW"):
                                e_raw = em
                            else:
                                rm = os.environ.get("RAWMODE", "")
                                if rm == "const":
                                    e_raw = S.op("vector", lambda v, bank=bank, c=c: v.tensor_scalar(
                                        out=raw_dst[:, c, tsl], in0=ps[:, bank, :], scalar1=1.0, scalar2=None, op0=ALU.mult), deps=[em, e_g])
                                elif rm == "nodep":
                                    e_raw = S.op("vector", lambda v, bank=bank, c=c: v.tensor_scalar(
                                        out=raw_dst[:, c, tsl], in0=ps[:, bank, :], scalar1=gvec[:, c:c + 1], scalar2=None, op0=ALU.mult), deps=[em])
                                elif rm == "act":
                                    e_raw = S.op("scalar", lambda a, bank=bank, c=c: a.activation(
                                        out=raw_dst[:, c, tsl], in_=ps[:, bank, :], func=AF.Copy, scale=gvec[:, c:c + 1]), deps=[em, e_g])
                                else:
                                    e_raw = S.op("vector", lambda v, bank=bank, c=c: v.tensor_scalar(
                                        out=raw_dst[:, c, tsl], in0=ps[:, bank, :], scalar1=gvec[:, c:c + 1], scalar2=None, op0=ALU.mult), deps=[em, e_g, e_sq])
                            bank_ring.release(bj, [e_sq, e_raw])
                            if os.environ.get("NO_SSQ"):
                                e_ss = e_sq
                            else:
                                e_ss = S.op("tensor", lambda t, sqt=sqt, c=c: t.matmul(
                                    ps[:, ssq_bank, :], ones_bf[:], sqt[:], start=(c == 0), stop=(c == nchunk - 1)),
                                    deps=[e_sq, ssq_free[ssq_bank]] if c == 0 else [e_sq])
                            sq_ring.release(sj, e_ss)
                            last_mm[0] = em
                        if os.environ.get("NO_RTAIL"):
                            return e_ss, e_raw
                        tj, tft, tfree = tf_ring.get()
                        e1 = S.op("vector", lambda v, tft=tft: v.tensor_scalar(out=tft[:], in0=ps[:, ssq_bank, :], scalar1=1.0 / dim, scalar2=EPS, op0=ALU.mult, op1=ALU.add), deps=[e_ss, tfree])
                        ssq_free[ssq_bank] = e1
                        e2 = S.op("scalar", lambda a, tft=tft: a.activation(out=tft[:], in_=tft[:], func=AF.Sqrt), deps=[e1])
                        e3 = S.op("vector", lambda v, tft=tft: v.reciprocal(out=r_rep[:, tsl], in_=tft[:]), deps=[e2])
                        tf_ring.release(tj, e3)
                        return e3, e_raw

                    def rope_pair(bA, bB, e_mm, r_rep, e_r, tb, dst):
                        tsl = slice(tb * 512, (tb + 1) * 512)
                        t1j, t1, f1 = tf_ring.get()
                        t2j, t2, f2 = tf_ring.get()
                        ea = S.op("vector", lambda v: v.tensor_tensor(out=t1[0:64, :], in0=ps[0:64, bA[1], :], in1=cos2[:, tsl], op=ALU.mult), deps=[e_mm, f1])
                        eb = S.op("vector", lambda v: v.tensor_tensor(out=t2[0:64, :], in0=ps[0:64, bB[1], :], in1=sin2[:, tsl], op=ALU.mult), deps=[e_mm, f2])
                        bank_ring.release(bA[0], ea)
                        bank_ring.release(bB[0], eb)
                        j, ev_t, fr = ev_ring.get()
                        if r_rep is None:
                            ec = S.op("vector", lambda v: v.tensor_tensor(out=ev_t[0:64, :], in0=t1[0:64, :], in1=t2[0:64, :], op=ALU.add), deps=[ea, eb, fr])
                        else:
                            ec0 = S.op("vector", lambda v: v.tensor_tensor(out=t1[0:64, :], in0=t1[0:64, :], in1=t2[0:64, :], op=ALU.add), deps=[ea, eb])
                            ec = S.op("vector", lambda v: v.tensor_tensor(out=ev_t[0:64, :], in0=t1[0:64, :], in1=r_rep[0:64, tsl], op=ALU.mult), deps=[ec0, e_r, fr])
                        tf_ring.release(t1j, ec)
                        tf_ring.release(t2j, ec)
                        return store(dst, j, ev_t, ec, npart=64)

                    ssq_free = {4: None, 5: None}
                    kv_stores, u_stores = [], []
                    load_group(0)
                    A2N = int(os.environ.get('A2N', '99'))
                    GROUPS = GROUPS[:A2N]
                    for gi, (kind, c0, ncol) in enumerate(GROUPS):
                        if gi + 1 < len(GROUPS):
                            load_group(gi + 1)
                        j_w, wt, e_w = loaded[gi]
                        last_mm = [None]
                        if kind == "kv":
                            if os.environ.get("NO_WKROT"):
                                continue
                            e_r1 = S.op("vector", lambda v, wt=wt: v.tensor_scalar(out=wkrot[:, :, 0:32], in0=wt[:, :, 288:320], scalar1=-1.0, scalar2=None, op0=ALU.mult), deps=[e_w])
                            e_r2 = S.op("vector", lambda v, wt=wt: v.tensor_copy(out=wkrot[:, :, 32:64], in_=wt[:, :, 256:288]), deps=[e_w])
                            for tb in range(int(os.environ.get("KV_TB", "4"))):
                                tsl = slice(tb * 512, (tb + 1) * 512)
                                e_r, e_raw = rms_chunks(wt, e_w, 2, gkv, e_gkv, ckv_raw, rkv_rep, 256.0, tb, 4 + tb % 2, last_mm)
                                for c in range(2):
                                    j, ev_t, fr = ev_ring.get()
                                    e_n = S.op("vector", lambda v, c=c, ev_t=ev_t, tsl=tsl: v.tensor_tensor(
                                        out=ev_t[:], in0=ckv_raw[:, c, tsl], in1=rkv_rep[:, tsl], op=ALU.mult), deps=[e_r, e_raw, fr])
                                    kv_stores.append(store(kvT[l][c * 128:(c + 1) * 128, tsl], j, ev_t, e_n))
                                if os.environ.get("NO_KR"):
                                    continue
                                bA = bank_ring.get()
                                emA = mm_group(bA[1], lambda kc, wt=wt: wt[:, kc, 256:320], lambda kc, tsl=tsl: hT[:, kc, tsl], 16, [e_w, bA[2]], mpart=64)
                                bB = bank_ring.get()
                                emB = mm_group(bB[1], lambda kc: wkrot[:, kc, :], lambda kc, tsl=tsl: hT[:, kc, tsl], 16, [e_r1, e_r2, bB[2]], mpart=64)
                                last_mm[0] = emB
                                kv_stores.append(rope_pair(bA, bB, [emA, emB], None, None, tb, kvT[l][256:320, tsl]))
                            e_cc_kv = collective(kvT[l], kvAll[l], kv_stores)
                        elif kind == "u":
                            for t in range(NT):
                                bj, bank, bfree = bank_ring.get()
                                em = mm_group(bank, lambda kc, t=t: hT[:, kc, t * 128:(t + 1) * 128], lambda kc, wt=wt: wt[:, kc, 0:512], 16, [e_w, bfree])
                                last_mm[0] = em
                                j, ev_t, fr = ev_ring.get()
                                e_ev = S.op("scalar", lambda a, bank=bank, ev_t=ev_t: a.copy(out=ev_t[:], in_=ps[:, bank, :]), deps=[em, fr])
                                bank_ring.release(bj, e_ev)
                                dstT = uTokA[l] if t < 8 else uTokB[l]
                                tt = t % 8
                                u_stores.append(store(dstT[tt * 128:(tt + 1) * 128, c0:c0 + 512], j, ev_t, e_ev))
                            if c0 == 512:
                                e_cc_uA = collective(uTokA[l], uAllA[l], u_stores)
                                e_cc_uB = collective(uTokB[l], uAllB[l], u_stores)
                        elif kind == "cq":
                            for tb in range(4):
                                e_rq, e_cq = rms_chunks(wt, e_w, 4, gq, e_gq, cqT, rq_rep, 512.0, tb, 4 + tb % 2, last_mm)
                            e_rq_all, e_cq_all = e_rq, e_cq
                        else:
                            row0 = SZROW[c0]
                            for c in range(4):
                                for tb in range(4):
                                    tsl = slice(tb * 512, (tb + 1) * 512)
                                    bj, bank, bfree = bank_ring.get()
                                    em = mm_group(bank, lambda kc, c=c, wt=wt: wt[:, kc, c * 128:(c + 1) * 128], lambda kc, tsl=tsl: hT[:, kc, tsl], 16, [e_w, bfree])
                                    last_mm[0] = em
                                    j, ev_t, fr = ev_ring.get()
                                    e_ev = S.op("scalar", lambda a, bank=bank, ev_t=ev_t: a.activation(out=ev_t[:], in_=ps[:, bank, :], func=AF.Silu), deps=[em, fr])
                                    bank_ring.release(bj, e_ev)
                                    store(szT[l][row0 + c * 128:row0 + (c + 1) * 128, tsl], j, ev_t, e_ev)
                        w_ring.release(j_w, last_mm[0])

                        if kind == "cq" and not os.environ.get('NO_A3'):
                            e_rots = []
                            for kc in range(4):
                                wqv = wq[:, kc, :].rearrange("p (h d) -> p h d", h=8)
                                e_rots.append(S.op("vector", lambda v, kc=kc, wqv=wqv: v.tensor_scalar(out=wqrot[:, kc, :, 0:32], in0=wqv[:, :, 160:192], scalar1=-1.0, scalar2=None, op0=ALU.mult), deps=[e_wq]))
                                e_rots.append(S.op("vector", lambda v, kc=kc, wqv=wqv: v.tensor_copy(out=wqrot[:, kc, :, 32:64], in_=wqv[:, :, 128:160]), deps=[e_wq]))
                            for h in range(NH):
                                for tb in range(4):
                                    tsl = slice(tb * 512, (tb + 1) * 512)
                                    bj, bank, bfree = bank_ring.get()
                                    em = mm_group(bank, lambda kc, h=h: wq[:, kc, 192 * h:192 * h + 128], lambda kc, tsl=tsl: cqT[:, kc, tsl], 4, [e_wq, e_cq_all, bfree])
                                    j, ev_t, fr = ev_ring.get()
                                    e_ev = S.op("vector", lambda v, bank=bank, ev_t=ev_t, tsl=tsl: v.tensor_tensor(
                                        out=ev_t[:], in0=ps[:, bank, :], in1=rq_rep[:, tsl], op=ALU.mult), deps=[em, e_rq_all, fr])
                                    bank_ring.release(bj, e_ev)
                                    store(qT[l][h, 0:128, tsl], j, ev_t, e_ev)
                                    bA = bank_ring.get()
                                    emA = mm_group(bA[1], lambda kc, h=h: wq[:, kc, 192 * h + 128:192 * h + 192], lambda kc, tsl=tsl: cqT[:, kc, tsl], 4, [e_wq, e_cq_all, bA[2]], mpart=64)
                                    bB = bank_ring.get()
                                    emB = mm_group(bB[1], lambda kc, h=h: wqrot[:, kc, h, :], lambda kc, tsl=tsl: cqT[:, kc, tsl], 4, e_rots + [e_cq_all, bB[2]], mpart=64)
                                    rope_pair(bA, bB, [emA, emB], rq_rep, e_rq_all, tb, qT[l][h, 128:192, tsl])

                    S.barrier(extra=[(cc_sem, cc_cnt[0])])
                    if f"uTok{l}" in dbg:
                        S.dma("sync", dbg[f"uTok{l}"][0:1024, :], uTokA[l][:, :], ch_st)
                        S.dma("sync", dbg[f"uTok{l}"][1024:2048, :], uTokB[l][:, :], ch_st)
                        S.dma("sync", dbg[f"kvT{l}"][:, :], kvT[l][:, :], ch_st)
                        S.dma("sync", dbg[f"szT{l}"][:, :], szT[l][:, :], ch_st)
                        S.dma("sync", dbg[f"qT{l}"].ap().rearrange("h d t -> (h d) t"), qT[l].ap().rearrange("h d t -> (h d) t"), ch_st)
                        S.barrier()
                    S.flush()
            if stop == f"a_{l}":
                break
            with ExitStack() as pb1:
                sb = lambda name, shape, dty: pb1.enter_context(nc.sbuf_tensor(f"pb1{l}_" + name, shape, dty))
                u = sb("u", [128, 32, 1024], BF16)
                tabb = [sb(f"tabb{i}", [128, 32, 512], BF16) for i in range(2)]
                ccs = sb("ccs", [128, 2, 128], BF16)
                scs = sb("scs", [128, 2, 128], BF16)
                wf = sb("wf", [128, 8, 128], F32)
                wfh = sb("wfh", [128, 8, 128], BF16)
                wfl = sb("wfl", [128, 8, 128], BF16)
                G1s = sb("G1s", [128, 8, 128], BF16)
                G2s = sb("G2s", [128, 8, 128], BF16)
                XT = [sb(f"XT{i}", [128, 512], BF16) for i in range(2)]
                szt = [sb(f"szt{i}", [128, 256], BF16) for i in range(3)]
                yev = [sb(f"yev{i}", [128, 256], BF16) for i in range(3)]

                e_uA = S.dma("sync", u[:, 0:16, :], uAllA[l].ap().rearrange("(st p) c -> p st c", p=128), S.tmpchan())
                e_uB = S.dma("sync", u[:, 16:32, :], uAllB[l].ap().rearrange("(st p) c -> p st c", p=128), S.tmpchan())
                e_cc = S.dma("sync", ccs[:], cc_in[:, :, :], S.tmpchan())
                e_sc2 = S.dma("sync", scs[:], sc_in[:, :, :], S.tmpchan())
                e_wf0 = S.dma("sync", wf[:], w_fmix[lw].rearrange("g m d -> m g d"), S.tmpchan())
                e_wfh = S.op("vector", lambda v: v.tensor_copy(out=wfh[:], in_=wf[:]), deps=[e_wf0])
                e_wfl0 = S.op("vector", lambda v: v.tensor_tensor(out=wf[:], in0=wf[:], in1=wfh[:], op=ALU.subtract), deps=[e_wfh])
                e_wf = S.op("vector", lambda v: v.tensor_copy(out=wfl[:], in_=wf[:]), deps=[e_wfl0])
                ch_tab = [S.chan("ch_tab0"), S.chan("ch_tab1")]
                tabfree = [None, None]

                def load_tab(kb):
                    i = kb % 2
                    srcv = tab[kb % NTAB].rearrange("(st p) n -> p st n", p=128)
                    S.dma("sync", tabb[i][:, 0:16, :], srcv[:, 0:16, :], ch_tab[i], deps=[tabfree[i]])
                    return S.dma("sync", tabb[i][:, 16:32, :], srcv[:, 16:32, :], ch_tab[i], deps=[tabfree[i]])

                gbank = Ring([6, 7])
                e_G = []
                for g in range(8):
                    for (tabm, dstG, scl) in ((ccs, G1s, FNORM), (scs, G2s, -FNORM)):
                        bj, bank, bfree = gbank.get()
                        S.op("tensor", lambda t, bank=bank, tabm=tabm, g=g: t.matmul(ps[:, bank, 0:128], tabm[:, 0, :], wfh[:, g, :], start=True, stop=False),
                             deps=[e_cc, e_sc2, e_wf, bfree], ev=False)
                        S.op("tensor", lambda t, bank=bank, tabm=tabm, g=g: t.matmul(ps[:, bank, 0:128], tabm[:, 0, :], wfl[:, g, :], start=False, stop=False), ev=False)
                        em = S.op("tensor", lambda t, bank=bank, tabm=tabm, g=g: t.matmul(ps[:, bank, 0:128], tabm[:, 1, :], wfh[:, g, :], start=False, stop=True))
                        ee = S.op("scalar", lambda a, bank=bank, dstG=dstG, g=g, scl=scl: a.activation(out=dstG[:, g, :], in_=ps[:, bank, 0:128], func=AF.Copy, scale=scl), deps=[em])
                        gbank.release(bj, ee)
                        e_G.append(ee)

                mbank = Ring([0, 1, 2])
                cbank = Ring([4, 5])
                xt_ring = Ring(XT)
                szt_ring = Ring(szt)
                yev_ring = Ring(yev)
                ch_szt = [S.chan(f"ch_szt{i}") for i in range(3)]
                ch_yev = [S.chan(f"ch_yev{i}") for i in range(3)]

                def channel_stage(kb, g, xj, xtt, e_x, sj, sztt, e_sz):
                    bj, bank, bfree = cbank.get()
                    S.op("tensor", lambda t: t.matmul(ps[:, bank, 0:256], G1s[:, g, :], xtt[:, 0:256], start=True, stop=False),
                         deps=[e_x, bfree] + e_G, ev=False)
                    em = S.op("tensor", lambda t: t.matmul(ps[:, bank, 0:256], G2s[:, g, :], xtt[:, 256:512], start=False, stop=True))
                    xt_ring.release(xj, em)
                    yj, yt, yfree = yev_ring.get()
                    ee = S.op("vector", lambda v: v.tensor_tensor(out=yt[:], in0=ps[:, bank, 0:256], in1=sztt[:], op=ALU.mult), deps=[em, e_sz, yfree])
                    cbank.release(bj, ee)
                    szt_ring.release(sj, ee)
                    e_st = S.dma("gpsimd", yT[l][g * 128:(g + 1) * 128, kb * 256:(kb + 1) * 256], yt[:], ch_yev[yj], deps=[ee])
                    yev_ring.release(yj, e_st)

                pending = None
                e_tab = load_tab(0)
                for kb in range(8):
                    i = kb % 2
                    e_tab_cur = e_tab
                    if kb + 1 < 8:
                        e_tab = load_tab(kb + 1)
                    em = None
                    for g in range(8):
                        sj, sztt, sfree = szt_ring.get()
                        e_sz = S.dma("sync", sztt[:], szT[l][g * 128:(g + 1) * 128, kb * 256:(kb + 1) * 256], ch_szt[sj], deps=[sfree])
                        bj, bank, bfree = mbank.get()
                        for st_ in range(32):
                            em = S.op("tensor", lambda t, bank=bank, st_=st_, g=g, i=i: t.matmul(
                                ps[:, bank, :], u[:, st_, g * 128:(g + 1) * 128], tabb[i][:, st_, :], start=(st_ == 0), stop=(st_ == 31)),
                                deps=[e_uA, e_uB, e_tab_cur, bfree] if st_ == 0 else [], ev=(st_ == 31))
                        xj, xtt, xfree = xt_ring.get()
                        e_x = S.op("scalar", lambda a, bank=bank, xtt=xtt: a.copy(out=xtt[:], in_=ps[:, bank, :]), deps=[em, xfree])
                        mbank.release(bj, e_x)
                        if pending is not None:
                            channel_stage(*pending)
                        pending = (kb, g, xj, xtt, e_x, sj, sztt, e_sz)
                    tabfree[i] = em
                channel_stage(*pending)
                S.barrier()
                S.flush()
            if stop == f"b1_{l}":
                if f"yT{l}" in dbg:
                    S.dma("sync", dbg[f"yT{l}"][0:1024, :], yT[l][0:1024, :], ch_st)
                break
            pbw = ExitStack()
            wo = pbw.enter_context(nc.sbuf_tensor(f"pbw{l}_wo", [128, 16, D], BF16))
            e_wo = []
            for cb in range(4):
                e_wo.append(S.dma("gpsimd", wo[:, :, cb * 512:(cb + 1) * 512], w_out[lw, :, cb * 512:(cb + 1) * 512].rearrange("(kc p) n -> p kc n", p=128), S.chan(f"ch_wo{cb}")))
            with ExitStack() as pb2:
                sb = lambda name, shape, dty: pb2.enter_context(nc.sbuf_tensor(f"pb2{l}_" + name, shape, dty))
                do_mod1 = (l == 0 and not LITE and "mod" not in dbg)
                if do_mod1:
                    wa2 = [sb(f"wa2_{i}", [128, 16, 256], BF16) for i in range(2)]
                    badac2 = [sb(f"badac2_{i}", [1, 512], F32) for i in range(2)]
                    modc2 = [sb(f"modc2_{i}", [1, 512], F32) for i in range(2)]
                    wa2_ring = Ring(wa2)
                    mod_pend = [None]
                ckvT = sb("ckvT", [128, 2, SEQ], BF16)
                krT = sb("krT", [128, SEQ], BF16)
                wkv = sb("wkv", [128, 2, 2048], BF16)
                kT = [sb(f"kT{i}", [128, SEQ], BF16) for i in range(2)]
                vS = [sb(f"vS{i}", [128, 32, 128], BF16) for i in range(2)]
                qn = [sb(f"qn{i}", [128, T], BF16) for i in range(2)]
                qr = [sb(f"qr{i}", [128, T], BF16) for i in range(2)]
                pT = [sb(f"pT{i}", [128, 512], BF16) for i in range(4)]
                rden = [sb(f"rden{i}", [128, 512], F32) for i in range(2)]
                acc = [sb(f"acc{i}", [128, 512], F32) for i in range(2)]
                accbf = [sb(f"accbf{i}", [128, 512], BF16) for i in range(2)]
                accP = [sb(f"accP{i}", [128, 512], F32) for i in range(2)]
                accfree = [None, None]
                accbf_free = [None, None]
                sza = [sb(f"sza{i}", [128, 512], BF16) for i in range(2)]
                yev = [sb(f"yev{i}", [128, 512], BF16) for i in range(2)]

                e_kvl = [S.op("vector", lambda v: v.memset(krT[64:128, :], 0.0)),
                         S.op("vector", lambda v: v.memset(qr[0][64:128, :], 0.0)),
                         S.op("vector", lambda v: v.memset(qr[1][64:128, :], 0.0))]
                for r in range(2):
                    for c in range(2):
                        e_kvl.append(S.dma("sync", ckvT[:, c, r * T:(r + 1) * T], kvAll[l][320 * r + 128 * c:320 * r + 128 * (c + 1), :], S.tmpchan()))
                    e_kvl.append(S.dma("sync", krT[0:64, r * T:(r + 1) * T], kvAll[l][320 * r + 256:320 * r + 320, :], S.tmpchan()))
                e_wkv = S.dma("gpsimd", wkv[:], w_kv_b[lw].rearrange("(kc p) n -> p kc n", p=128), S.tmpchan("gpsimd"))

                ch_q = [S.chan("ch_q0"), S.chan("ch_q1")]
                ch_sza = [S.chan("ch_sza0"), S.chan("ch_sza1")]
                ch_ya = [S.chan("ch_ya0"), S.chan("ch_ya1")]
                qfree = [None, None]
                kvfree = [None, None]
                sbank = Ring([0, 1, 2])
                obank = Ring([3, 4])
                dbank = Ring([5])
                bbank = Ring([6, 7])
                pt_ring = Ring(pT)
                sza_ring = Ring(sza)
                yev_ring = Ring(yev)
                fin_i = [0]
                kb_ev = {}
                q_ev = {}

                def build_kv(h):
                    i = h % 2
                    S.dma("sync", qn[i][:], qT[l][h, 0:128, :], ch_q[i], deps=[qfree[i]])
                    q_ev[h] = S.dma("sync", qr[i][0:64, :], qT[l][h, 128:192, :], ch_q[i], deps=[qfree[i]])
                    e_kb = []
                    for kblk in range(8):
                        bj, bank, bfree = bbank.get()
                        ksl = slice(kblk * 512, (kblk + 1) * 512)
                        for kc in range(2):
                            em = S.op("tensor", lambda t, bank=bank, kc=kc, h=h, ksl=ksl: t.matmul(
                                ps[:, bank, :], wkv[:, kc, 256 * h:256 * h + 128], ckvT[:, kc, ksl], start=(kc == 0), stop=(kc == 1)),
                                deps=[e_wkv, bfree] + e_kvl if kc == 0 else [], ev=(kc == 1))
                        ee = S.op("vector", lambda v, bank=bank, i=i, ksl=ksl: v.tensor_copy(out=kT[i][:, ksl], in_=ps[:, bank, :]), deps=[em, kvfree[i]])
                        bbank.release(bj, ee)
                        e_kb.append(ee)
                    for grp in range(8):
                        bj, bank, bfree = bbank.get()
                        for jj in range(4):
                            kt = 4 * grp + jj
                            for kc in range(2):
                                em = S.op("tensor", lambda t, bank=bank, kc=kc, h=h, kt=kt, jj=jj: t.matmul(
                                    ps[:, bank, jj * 128:(jj + 1) * 128], ckvT[:, kc, kt * 128:(kt + 1) * 128],
                                    wkv[:, kc, 256 * h + 128:256 * h + 256], start=(kc == 0), stop=(kc == 1)),
                                    deps=[e_wkv, bfree] + e_kvl if (jj == 0 and kc == 0) else [], ev=(jj == 3 and kc == 1))
                        ee = S.op("vector", lambda v, bank=bank, i=i, grp=grp: v.tensor_copy(
                            out=vS[i][:, 4 * grp:4 * grp + 4, :], in_=ps[:, bank, :].rearrange("p (a b) -> p a b", a=4)), deps=[em, kvfree[i]])
                        bbank.release(bj, ee)
                        e_kb.append(ee)
                    kb_ev[h] = e_kb

                build_kv(0)
                for h in range(NH):
                    i = h % 2
                    e_q = q_ev[h]
                    e_kb = kb_ev[h]
                    last_pv = None
                    for qb in range(4):
                        qsl = slice(qb * 512, (qb + 1) * 512)
                        zj, szat, zfree = sza_ring.get()
                        e_sza = S.dma("sync", szat[:], szT[l][1024 + 128 * h:1024 + 128 * (h + 1), qsl], ch_sza[zj], deps=[zfree])
                        oj, ob, ofree = obank.get()
                        dj, db, dfree = dbank.get()
                        s_ev = {}

                        def s_mm(kt, i=i, qsl=qsl):
                            bj, bank, bfree = sbank.get()
                            S.op("tensor", lambda t: t.matmul(ps[:, bank, :], kT[i][:, kt * 128:(kt + 1) * 128], qn[i][:, qsl], start=True, stop=False),
                                 deps=[e_q, bfree] + e_kb, ev=False)
                            em = S.op("tensor", lambda t: t.matmul(ps[:, bank, :], krT[:, kt * 128:(kt + 1) * 128], qr[i][:, qsl], start=False, stop=True))
                            s_ev[kt] = (bj, bank, em)

                        fa = fin_i[0] % 2
                        e_add = e_addD = e_addP = None
                        s_mm(0)
                        s_mm(1)
                        for kt in range(32):
                            bj, bank, em = s_ev.pop(kt)
                            pj, ptt, pfree = pt_ring.get()
                            e_p = S.op("scalar", lambda a, bank=bank, ptt=ptt: a.activation(out=ptt[:], in_=ps[:, bank, :], func=AF.Exp, scale=SM_SCALE), deps=[em, pfree])
                            sbank.release(bj, e_p)
                            if kt + 2 < 32:
                                s_mm(kt + 2)
                            pv_ev = S.op("tensor", lambda t, kt=kt, ptt=ptt, ob=ob, i=i: t.matmul(ps[:, ob, :], vS[i][:, kt, :], ptt[:], start=(kt == 0), stop=(kt == 31)),
                                         deps=[e_p, ofree] if kt == 0 else [e_p])
                            if kt % 2 == 0:
                                if kt == 0:
                                    e_addD = S.op("vector", lambda v, ptt=ptt, fa=fa: v.tensor_copy(out=acc[fa][:], in_=ptt[:]), deps=[e_p, accfree[fa]])
                                else:
                                    e_addD = S.op("vector", lambda v, ptt=ptt, fa=fa: v.tensor_tensor(out=acc[fa][:], in0=acc[fa][:], in1=ptt[:], op=ALU.add), deps=[e_p, e_addD])
                                e_add = e_addD
                            else:
                                if kt == 1:
                                    e_addP = S.op("gpsimd", lambda g, ptt=ptt, fa=fa: g.tensor_copy(out=accP[fa][:], in_=ptt[:]), deps=[e_p, accfree[fa]])
                                else:
                                    e_addP = S.op("gpsimd", lambda g, ptt=ptt, fa=fa: g.tensor_tensor(out=accP[fa][:], in0=accP[fa][:], in1=ptt[:], op=ALU.add), deps=[e_p, e_addP])
                                e_add = e_addP
                            pt_ring.release(pj, [pv_ev, e_add])
                        e_cv = S.op("vector", lambda v, fa=fa: v.tensor_tensor(out=accbf[fa][:], in0=acc[fa][:], in1=accP[fa][:], op=ALU.add), deps=[e_addD, e_addP, accbf_free[fa]])
                        accfree[fa] = e_cv
                        last_pv = S.op("tensor", lambda t, db=db, fa=fa: t.matmul(ps[:, db, :], ones_bf[:], accbf[fa][:], start=True, stop=True), deps=[e_cv, dfree])
                        accbf_free[fa] = last_pv
                        fi = fin_i[0] % 2
                        fin_i[0] += 1
                        e1 = S.op("vector", lambda v, fi=fi, db=db: v.reciprocal(out=rden[fi][:], in_=ps[:, db, :]), deps=[last_pv])
                        dbank.release(dj, e1)
                        e2 = S.op("vector", lambda v, fi=fi, ob=ob: v.tensor_tensor(out=rden[fi][:], in0=ps[:, ob, :], in1=rden[fi][:], op=ALU.mult), deps=[e1])
                        obank.release(oj, e2)
                        yj, yt, yfree = yev_ring.get()
                        e3 = S.op("vector", lambda v, fi=fi, yt=yt, szat=szat: v.tensor_tensor(out=yt[:], in0=rden[fi][:], in1=szat[:], op=ALU.mult), deps=[e2, e_sza, yfree])
                        sza_ring.release(zj, e3)
                        e_st = S.dma("gpsimd", yT[l][1024 + 128 * h:1024 + 128 * (h + 1), qsl], yt[:], ch_ya[yj], deps=[e3])
                        yev_ring.release(yj, e_st)
                        if qb == 1 and h + 1 < NH:
                            build_kv(h + 1)
                        if do_mod1:
                            slot = 4 * h + qb
                            if slot < 24:
                                if slot == 0:
                                    mod_pend[0] = mod_load(1, 0, 256, wa2_ring)
                                cur = mod_pend[0]
                                if slot + 1 < 24:
                                    mod_pend[0] = mod_load(1, (slot + 1) * 256, 256, wa2_ring)
                                mod_compute(cur, wa2_ring, bbank, badac2, modc2)
                    qfree[i] = last_pv
                    kvfree[i] = last_pv
                S.barrier()
                S.flush()
            if stop == f"b2_{l}":
                if f"yT{l}" in dbg:
                    S.dma("sync", dbg[f"yT{l}"][:, :], yT[l][:, :], ch_st)
                break
            with ExitStack() as pb3:
                sb = lambda name, shape, dty: pb3.enter_context(nc.sbuf_tensor(f"pb3{l}_" + name, shape, dty))
                yTs = sb("yTs", [128, 16, T], BF16)
                gate_rep = sb("gate_rep", [128, D], F32)
                g_rep = sb("g_rep", [128, D], F32)
                b_rep = sb("b_rep", [128, D], F32)
                xt = [sb(f"xt{i}", [128, D], F32) for i in range(2)]
                vv = [sb(f"vv{i}", [128, D], F32) for i in range(2)]
                st = [sb(f"st{i}", [128, 4, 6], F32) for i in range(2)]
                mv = [sb(f"mv{i}", [128, 2], F32) for i in range(2)]
                sm = [sb(f"sm{i}", [128, 4], F32) for i in range(2)]
                e_y = []
                for q4 in range(4):
                    e_y.append(S.dma("sync", yTs[:, 4 * q4:4 * q4 + 4, :], yT[l][512 * q4:512 * (q4 + 1), :].rearrange("(kc p) t -> p kc t", p=128), S.tmpchan()))
                e_gate = S.dma("sync", gate_rep[:], modv[l:l + 1, 2 * D:3 * D].partition_broadcast(128), S.tmpchan())
                e_g = S.dma("sync", g_rep[:], ln_g[lw:lw + 1, :].partition_broadcast(128), S.tmpchan())
                e_b = S.dma("sync", b_rep[:], ln_b[lw:lw + 1, :].partition_broadcast(128), S.tmpchan())
                ch_x = [S.chan(f"ch_x{i}") for i in range(2)]
                ch_o = [S.chan(f"ch_o{i}") for i in range(2)]
                xfree = [None, None]
                vfree = [None, None]
                obank = Ring(list(range(8)))
                for t in range(NT):
                    i = t % 2
                    e_x = S.dma("sync", xt[i][:], x_src[t * 128:(t + 1) * 128, :], ch_x[i], deps=[xfree[i]])
                    e_gm = []
                    for cb in range(4):
                        bj, bank, bfree = obank.get()
                        csl = slice(cb * 512, (cb + 1) * 512)
                        for kc in range(16):
                            em = S.op("tensor", lambda tt, bank=bank, kc=kc, t=t, csl=csl: tt.matmul(
                                ps[:, bank, :], yTs[:, kc, t * 128:(t + 1) * 128], wo[:, kc, csl], start=(kc == 0), stop=(kc == 15)),
                                deps=e_y + [e_wo[cb], bfree] if kc == 0 else [], ev=(kc == 15))
                        ee = S.op("vector", lambda v, bank=bank, i=i, csl=csl: v.tensor_tensor(out=vv[i][:, csl], in0=ps[:, bank, :], in1=gate_rep[:, csl], op=ALU.mult),
                                  deps=[em, e_gate, vfree[i]])
                        obank.release(bj, ee)
                        e_gm.append(ee)
                    e_v = S.op("vector", lambda v, i=i: v.scalar_tensor_tensor(out=vv[i][:], in0=xt[i][:], scalar=ALPHA, in1=vv[i][:], op0=ALU.mult, op1=ALU.add), deps=e_gm + [e_x])
                    xfree[i] = e_v
                    e_st = None
                    for k in range(4):
                        e_st = S.op("vector", lambda v, i=i, k=k: v.bn_stats(out=st[i][:, k, :], in_=vv[i][:, k * 512:(k + 1) * 512]), deps=[e_v], ev=(k == 3))
                    e_ag = S.op("vector", lambda v, i=i: v.bn_aggr(out=mv[i][:], in_=st[i][:].rearrange("p a b -> p (a b)")), deps=[e_st])
                    e1 = S.op("vector", lambda v, i=i: v.tensor_scalar(out=sm[i][:, 0:1], in0=mv[i][:, 1:2], scalar1=EPS, scalar2=None, op0=ALU.add), deps=[e_ag])
                    e2 = S.op("scalar", lambda a, i=i: a.activation(out=sm[i][:, 1:2], in_=sm[i][:, 0:1], func=AF.Sqrt), deps=[e1])
                    e3 = S.op("vector", lambda v, i=i: v.reciprocal(out=sm[i][:, 2:3], in_=sm[i][:, 1:2]), deps=[e2])
                    e4 = S.op("vector", lambda v, i=i: v.tensor_scalar(out=sm[i][:, 3:4], in0=mv[i][:, 0:1], scalar1=sm[i][:, 2:3], scalar2=-1.0, op0=ALU.mult, op1=ALU.mult), deps=[e3])
                    e5 = S.op("scalar", lambda a, i=i: a.activation(out=vv[i][:], in_=vv[i][:], func=AF.Identity, bias=sm[i][:, 3:4], scale=sm[i][:, 2:3]), deps=[e4])
                    e6 = S.op("vector", lambda v, i=i: v.tensor_tensor(out=vv[i][:], in0=vv[i][:], in1=g_rep[:], op=ALU.mult), deps=[e5, e_g])
                    e7 = S.op("vector", lambda v, i=i: v.tensor_tensor(out=vv[i][:], in0=vv[i][:], in1=b_rep[:], op=ALU.add), deps=[e6, e_b])
                    e_o = S.dma("gpsimd", x_dst[t * 128:(t + 1) * 128, :], vv[i][:], ch_o[i], deps=[e7])
                    vfree[i] = e_o
                S.barrier()
                S.flush()
            pbw.close()
            if stop == f"b3_{l}":
                if "x1" in dbg:
                    S.dma("sync", dbg["x1"][:, :], x1[:, :], ch_st)
                break
            x_src = x1

        S.barrier()
        S.flush()
    return nc


_PROGRAM_CACHE = {}


def _host_consts():
    m = np.arange(128, dtype=np.float64)
    angc = 2 * np.pi * np.outer(m, m) / 128.0
    def hilo(a):
        hi = a.astype(np.float32).astype(ml_dtypes.bfloat16)
        lo = (a - hi.astype(np.float64)).astype(np.float32).astype(ml_dtypes.bfloat16)
        return np.ascontiguousarray(np.stack([hi, lo], axis=1))
    cc = hilo(np.cos(angc))
    sc = hilo(np.sin(angc))
    invf = (10000.0 ** (-np.arange(0, 64, 2, dtype=np.float32) / np.float32(64))).astype(np.float32)
    invf2 = np.concatenate([invf, invf]).reshape(64, 1).astype(np.float32)
    ident = np.eye(128, dtype=np.float32).astype(ml_dtypes.bfloat16)
    return cc, sc, invf2, ident


def _token_lists():
    return [np.arange(0, 2048), np.arange(2048, 4096)]


def _tables(tok_lists):
    s_order = np.concatenate([tok_lists[0][:1024], tok_lists[1][:1024], tok_lists[0][1024:], tok_lists[1][1024:]])
    tabs = []
    for r in range(2):
        k = tok_lists[r].astype(np.int64)
        ph = (np.outer(s_order.astype(np.int64), k) % SEQ).astype(np.float64) * (2 * np.pi / SEQ)
        cosm = np.cos(ph).reshape(SEQ, 8, 256)
        sinm = np.sin(ph).reshape(SEQ, 8, 256)
        tb = np.concatenate([cosm, sinm], axis=2).transpose(1, 0, 2)
        tabs.append(np.ascontiguousarray(tb).astype(ml_dtypes.bfloat16))
    return tabs


def make_in_maps(x, c, positions, w_ada, b_ada, w_in, q_norm, w_q_b, kv_norm, w_kv_b, w_fmix, w_out, ln_g, ln_b):
    tl = _token_lists()
    tabs = _tables(tl)
    cc, sc, invf2, ident = _host_consts()
    f32 = lambda a: np.ascontiguousarray(np.asarray(a, dtype=np.float32))
    shared = {
        "w_ada": f32(w_ada), "b_ada": f32(b_ada), "w_in": f32(w_in), "w_q_b": f32(w_q_b),
        "w_kv_b": f32(w_kv_b), "w_fmix": f32(w_fmix), "w_out": f32(w_out), "ln_g": f32(ln_g), "ln_b": f32(ln_b),
        "qn_t": f32(np.asarray(q_norm).reshape(DEPTH, 4, 128).transpose(0, 2, 1)),
        "kvn_t": f32(np.asarray(kv_norm).reshape(DEPTH, 2, 128).transpose(0, 2, 1)),
        "cc": cc, "sc": sc, "invf": invf2, "ident": ident,
    }
    if LITE:
        for k in ["w_ada", "b_ada", "w_in", "w_q_b", "w_kv_b", "w_fmix", "w_out", "ln_g", "ln_b", "qn_t", "kvn_t"]:
            shared[k] = np.ascontiguousarray(shared[k][:1])
        shared["w_ada"] = np.ascontiguousarray(shared["w_ada"][:1, :128, :512])
        tabs = [np.ascontiguousarray(t[:1]) for t in tabs]
    in_maps = []
    for core in range(8):
        b, r = core // 2, core % 2
        m = dict(shared)
        m["x"] = f32(np.asarray(x)[b][tl[r]])
        m["cvec"] = f32(np.asarray(c)[b].reshape(16, 128).T)
        m["pos"] = np.ascontiguousarray(np.asarray(positions)[b][tl[r]].astype(np.int32).reshape(1, T))
        m["tab"] = tabs[r]
        in_maps.append(m)
    return in_maps, tl


def kernel(x, c, positions, w_ada, b_ada, w_in, q_norm, w_q_b, kv_norm, w_kv_b, w_fmix, w_out, ln_g, ln_b):
    in_maps, tl = make_in_maps(x, c, positions, w_ada, b_ada, w_in, q_norm, w_q_b, kv_norm, w_kv_b,
                               w_fmix, w_out, ln_g, ln_b)
    if "full" not in _PROGRAM_CACHE:
        _PROGRAM_CACHE["full"] = build_program()
    nc = _PROGRAM_CACHE["full"]
    res = run_bass_kernel_spmd(nc, in_maps, core_ids=list(range(8)))
    outp = np.empty((4, SEQ, D), dtype=np.float32)
    for core in range(8):
        b, r = core // 2, core % 2
        outp[b][tl[r]] = res.results[core]["out"]
    return outp
```

```python
import math
from contextlib import ExitStack

import numpy as np
import ml_dtypes

import concourse.bass as bass
import concourse.mybir as mybir
from concourse.bass_utils import run_bass_kernel_spmd

F32 = mybir.dt.float32
BF16 = mybir.dt.bfloat16
I32 = mybir.dt.int32
AF = mybir.ActivationFunctionType
ALU = mybir.AluOpType

D = 2048
SEQ = 4096
T = 2048
NT = 16
DEPTH = 2
DIN = 3904
NH = 8
EPS = 1e-6
ALPHA = (2 * DEPTH) ** 0.25
SM_SCALE = 1.0 / math.sqrt(192.0)
FNORM = 1.0 / math.sqrt(4096.0 * 128.0)
PI = math.pi
C1 = 6.28125
C2 = 2.0 * math.pi - 6.28125

import os
NO_CC = bool(os.environ.get("NO_CC"))
LITE = bool(os.environ.get("LITE"))
WL = 1 if LITE else 2
NTAB = 1 if LITE else 8
ENGS = ["sync", "scalar", "vector", "gpsimd", "tensor"]
CENGS = ["scalar", "vector", "gpsimd", "tensor"]


class Sched:
    def __init__(self, nc, es):
        self.nc = nc
        self.es = es
        self.q = {e: [] for e in ENGS}
        self.sem = {}
        self.cnt = {}
        for e in CENGS:
            self.sem[e] = es.enter_context(nc.semaphore("s_" + e))
            self.cnt[e] = 0
        self.waited = {e: {} for e in ENGS}
        self.chans = []
        self.named = {}
        self.tmp_i = {}

    def chan(self, name):
        if name in self.named:
            return self.named[name]
        sem = self.es.enter_context(self.nc.semaphore(name))
        ch = [sem, 0]
        self.chans.append(ch)
        self.named[name] = ch
        return ch

    def tmpchan(self, eng="sync"):
        self.tmp_i[eng] = self.tmp_i.get(eng, 0) + 1
        return self.chan(f"tmp_{eng}{self.tmp_i[eng]}")

    def _waits(self, eng, deps):
        flat = []
        for d in deps:
            if d is None:
                continue
            if isinstance(d, list):
                flat.extend(x for x in d if x is not None)
            else:
                flat.append(d)
        for d in flat:
            sem, val = d
            key = id(sem)
            if self.waited[eng].get(key, 0) >= val:
                continue
            self.waited[eng][key] = val
            self.q[eng].append(lambda e, sem=sem, val=val: e.wait_ge(sem, val))

    def op(self, eng, fn, deps=(), ev=True):
        self._waits(eng, deps)
        if ev:
            self.cnt[eng] += 1
            sem = self.sem[eng]
            self.q[eng].append(lambda e, fn=fn, sem=sem: fn(e).then_inc(sem, 1))
            return (sem, self.cnt[eng])
        self.q[eng].append(lambda e, fn=fn: fn(e))
        return None

    def dma(self, eng, out, in_, chan, deps=()):
        self._waits(eng, deps)
        chan[1] += 16
        sem = chan[0]
        self.q[eng].append(lambda e, out=out, in_=in_, sem=sem: e.dma_start(out=out, in_=in_).then_inc(sem, 16))
        return (sem, chan[1])

    def raw(self, eng, fn, deps=()):
        self._waits(eng, deps)
        self.q[eng].append(fn)

    def wait(self, eng, deps):
        self._waits(eng, deps)

    def barrier(self, extra=()):
        evs = [(self.sem[e], self.cnt[e]) for e in CENGS if self.cnt[e] > 0]
        evs += [(c[0], c[1]) for c in self.chans if c[1] > 0]
        evs += list(extra)
        for e in ENGS:
            self._waits(e, evs)
        self.tmp_i = {}

    def flush(self):
        nc = self.nc
        with nc.Block() as block:
            for e in ENGS:
                if not self.q[e]:
                    continue

                def f(engobj, e=e):
                    for c in self.q[e]:
                        c(engobj)

                getattr(block, e)(f)
        self.q = {e: [] for e in ENGS}


class Ring:
    def __init__(self, items):
        self.items = list(items)
        self.free = [None] * len(self.items)
        self.i = 0

    def get(self):
        j = self.i % len(self.items)
        self.i += 1
        return j, self.items[j], self.free[j]

    def release(self, j, ev):
        self.free[j] = ev


def build_program(stop=None, debug=()):
    nc = bass.Bass("TRN2", target_bir_lowering=False)
    dt = nc.dram_tensor
    x_in = dt("x", [T, D], F32, kind="ExternalInput")
    cvec = dt("cvec", [128, 16], F32, kind="ExternalInput")
    pos = dt("pos", [1, T], I32, kind="ExternalInput")
    w_ada = dt("w_ada", [1, 128, 512] if LITE else [WL, D, 3 * D], F32, kind="ExternalInput")
    b_ada = dt("b_ada", [WL, 3 * D], F32, kind="ExternalInput")
    w_in = dt("w_in", [WL, D, DIN], F32, kind="ExternalInput")
    qn_t = dt("qn_t", [WL, 128, 4], F32, kind="ExternalInput")
    w_q_b = dt("w_q_b", [WL, 512, 1536], F32, kind="ExternalInput")
    kvn_t = dt("kvn_t", [WL, 128, 2], F32, kind="ExternalInput")
    w_kv_b = dt("w_kv_b", [WL, 256, 2048], F32, kind="ExternalInput")
    w_fmix = dt("w_fmix", [WL, 8, 128, 128], F32, kind="ExternalInput")
    w_out = dt("w_out", [WL, D, D], F32, kind="ExternalInput")
    ln_g = dt("ln_g", [WL, D], F32, kind="ExternalInput")
    ln_b = dt("ln_b", [WL, D], F32, kind="ExternalInput")
    tab = dt("tab", [NTAB, SEQ, 512], BF16, kind="ExternalInput")
    cc_in = dt("cc", [128, 2, 128], BF16, kind="ExternalInput")
    sc_in = dt("sc", [128, 2, 128], BF16, kind="ExternalInput")
    invf = dt("invf", [64, 1], F32, kind="ExternalInput")
    ident_in = dt("ident", [128, 128], BF16, kind="ExternalInput")
    out = dt("out", [T, D], F32, kind="ExternalOutput")

    modv = dt("modv", [DEPTH, 3 * D], F32)
    uTokA = [dt(f"uTokA{l}", [1024, 1024], BF16) for l in range(DEPTH)]
    uTokB = [dt(f"uTokB{l}", [1024, 1024], BF16) for l in range(DEPTH)]
    uAllA = [dt(f"uAllA{l}", [2048, 1024], BF16) for l in range(DEPTH)]
    uAllB = [dt(f"uAllB{l}", [2048, 1024], BF16) for l in range(DEPTH)]
    kvT = [dt(f"kvT{l}", [320, T], BF16) for l in range(DEPTH)]
    kvAll = [dt(f"kvAll{l}", [640, T], BF16) for l in range(DEPTH)]
    szT = [dt(f"szT{l}", [2048, T], BF16) for l in range(DEPTH)]
    qT = [dt(f"qT{l}", [NH, 192, T], BF16) for l in range(DEPTH)]
    yT = [dt(f"yT{l}", [2048, T], BF16) for l in range(DEPTH)]
    x1 = dt("x1", [T, D], F32)

    dbg = {}
    for name, shape, dty in debug:
        dbg[name] = dt("dbg_" + name, shape, dty, kind="ExternalOutput")

    PAIRS = [[0, 1]] if os.environ.get('SIM2') else [[0, 1], [2, 3], [4, 5], [6, 7]]

    with ExitStack() as es:
        S = Sched(nc, es)
        cc_sem = es.enter_context(nc.semaphore("cc_sem"))
        cc_cnt = [0]
        ps = es.enter_context(nc.psum_tensor("ps", [128, 8, 512], F32))
        cos2 = es.enter_context(nc.sbuf_tensor("sb_cos2", [64, T], F32))
        sin2 = es.enter_context(nc.sbuf_tensor("sb_sin2", [64, T], F32))
        ident = es.enter_context(nc.sbuf_tensor("sb_ident", [128, 128], BF16))
        ones_bf = es.enter_context(nc.sbuf_tensor("sb_ones_bf", [128, 128], BF16))
        ch_st = S.chan("ch_st")
        cact = es.enter_context(nc.sbuf_tensor("sb_cact", [128, 16], BF16))
        ch_wa = [S.chan("ch_wa0"), S.chan("ch_wa1")]
        ch_bc = [S.chan("ch_bc0"), S.chan("ch_bc1")]
        ch_mc = [S.chan("ch_mc0"), S.chan("ch_mc1")]
        mod_state = {"it": 0, "mfree": [None, None], "e_ca": None}

        def mod_load(l, c0, ncol, wa_ring):
            j, wt, fr = wa_ring.get()
            src = w_ada[l % WL, :, c0:c0 + ncol].rearrange("(kc p) n -> p kc n", p=128)
            e_w = S.dma("gpsimd", wt[:, :, 0:ncol], src, ch_wa[j], deps=[fr])
            return (l, c0, ncol, j, wt, e_w)

        def mod_chunk(l, c0, ncol, wa_ring, bank_ring, badac, modc):
            mod_compute(mod_load(l, c0, ncol, wa_ring), wa_ring, bank_ring, badac, modc)

        def mod_compute(hnd, wa_ring, bank_ring, badac, modc):
            l, c0, ncol, j, wt, e_w = hnd
            mfree = mod_state["mfree"]
            bj, bank, bfree = bank_ring.get()
            i2 = mod_state["it"] % 2
            mod_state["it"] += 1
            e_bc = S.dma("sync", badac[i2][:, 0:ncol], b_ada[l % WL:l % WL + 1, c0:c0 + ncol], ch_bc[i2], deps=[mfree[i2]])
            em = None
            for kc in range(16):
                em = S.op("tensor", lambda t, bank=bank, wt=wt, kc=kc: t.matmul(
                    ps[0:1, bank, 0:ncol], cact[:, kc:kc + 1], wt[:, kc, 0:ncol], start=(kc == 0), stop=(kc == 15)),
                    deps=[e_w, mod_state["e_ca"], bfree] if kc == 0 else [], ev=(kc == 15))
            wa_ring.release(j, em)
            ee = S.op("vector", lambda v, bank=bank, i2=i2: v.tensor_tensor(
                out=modc[i2][0:1, 0:ncol], in0=ps[0:1, bank, 0:ncol], in1=badac[i2][0:1, 0:ncol], op=ALU.add),
                deps=[em, e_bc, mfree[i2]])
            bank_ring.release(bj, ee)
            e_m = S.dma("sync", modv[l:l + 1, c0:c0 + ncol], modc[i2][:, 0:ncol], ch_mc[i2], deps=[ee])
            if "mod" in dbg:
                e_m = S.dma("sync", dbg["mod"][l:l + 1, c0:c0 + ncol], modc[i2][:, 0:ncol], ch_mc[i2], deps=[ee])
            mfree[i2] = e_m

        def collective(src, dst, deps):
            S.wait("gpsimd", deps)
            if NO_CC:
                return None
            cc_cnt[0] += 1

            def f(g, src=src, dst=dst):
                g.collective_compute("AllGather", ALU.bypass, replica_groups=PAIRS,
                                     ins=[src.ap().opt()], outs=[dst.ap().opt()]).then_inc(cc_sem)
            S.q["gpsimd"].append(f)
            return (cc_sem, cc_cnt[0])

        with ExitStack() as p0:
            sb = lambda name, shape, dty: p0.enter_context(nc.sbuf_tensor("p0_" + name, shape, dty))
            cv = sb("cv", [128, 16], F32)
            badac = [sb(f"badac{i}", [1, 512], F32) for i in range(2)]
            modc = [sb(f"modc{i}", [1, 512], F32) for i in range(2)]
            wa = [sb(f"wa{i}", [128, 16, 512], BF16) for i in range(2)]
            posi = sb("posi", [64, T], I32)
            posf = sb("posf", [64, T], F32)
            ang = sb("ang", [64, T], F32)
            ki = sb("ki", [64, T], I32)
            kf = sb("kf", [64, T], F32)
            rr = sb("rr", [64, T], F32)
            tmpa = sb("tmpa", [64, T], F32)
            tmpb = sb("tmpb", [64, T], F32)
            invf_sb = sb("invf_sb", [64, 1], F32)

            e_id = S.dma("sync", ident[:], ident_in[:, :], S.tmpchan())
            e_cv = S.dma("sync", cv[:], cvec[:, :], S.tmpchan())
            e_ps = S.dma("sync", posi[:], pos.ap().partition_broadcast(64), S.tmpchan())
            e_if = S.dma("sync", invf_sb[:], invf[:, :], S.tmpchan())
            e_c0 = S.op("vector", lambda v: v.memset(ones_bf[:], 1.0))
            e_ca = S.op("scalar", lambda a: a.activation(out=cact[:], in_=cv[:], func=AF.Silu), deps=[e_cv])

            e = S.op("vector", lambda v: v.tensor_copy(out=posf[:], in_=posi[:]), deps=[e_ps])
            e = S.op("vector", lambda v: v.tensor_scalar(out=ang[:], in0=posf[:], scalar1=invf_sb[:, 0:1], scalar2=None, op0=ALU.mult), deps=[e, e_if])
            e = S.op("vector", lambda v: v.tensor_scalar(out=ki[:], in0=ang[:], scalar1=1.0 / (2 * PI), scalar2=None, op0=ALU.mult), deps=[e])
            e = S.op("vector", lambda v: v.tensor_copy(out=kf[:], in_=ki[:]), deps=[e])
            e = S.op("vector", lambda v: v.scalar_tensor_tensor(out=tmpa[:], in0=kf[:], scalar=-C1, in1=ang[:], op0=ALU.mult, op1=ALU.add), deps=[e])
            e_r = S.op("vector", lambda v: v.scalar_tensor_tensor(out=rr[:], in0=kf[:], scalar=-C2, in1=tmpa[:], op0=ALU.mult, op1=ALU.add), deps=[e])

            def wrap_sin(src_ap, dst_tab, shift, dep):
                e = S.op("vector", lambda v: v.tensor_scalar(out=tmpa[:], in0=src_ap, scalar1=shift, scalar2=None, op0=ALU.add), deps=[dep])
                e = S.op("vector", lambda v: v.tensor_scalar(out=tmpb[:], in0=tmpa[:], scalar1=PI, scalar2=2 * PI, op0=ALU.is_gt, op1=ALU.mult), deps=[e])
                e = S.op("vector", lambda v: v.tensor_tensor(out=tmpa[:], in0=tmpa[:], in1=tmpb[:], op=ALU.subtract), deps=[e])
                e = S.op("vector", lambda v: v.tensor_scalar(out=tmpb[:], in0=tmpa[:], scalar1=-PI, scalar2=2 * PI, op0=ALU.is_lt, op1=ALU.mult), deps=[e])
                e = S.op("vector", lambda v: v.tensor_tensor(out=tmpa[:], in0=tmpa[:], in1=tmpb[:], op=ALU.add), deps=[e])
                e = S.op("vector", lambda v: v.tensor_scalar(out=tmpa[:], in0=tmpa[:], scalar1=PI, scalar2=-PI, op0=ALU.min, op1=ALU.max), deps=[e])
                e = S.op("scalar", lambda a: a.activation(out=dst_tab, in_=tmpa[:], func=AF.Sin), deps=[e])
                return e

            e_s = wrap_sin(rr[:], sin2[:], 0.0, e_r)
            e_c = wrap_sin(rr[:], cos2[:], PI / 2, e_s)

            wa_ring = Ring(wa)
            bank_ring = Ring([0, 1])
            mod_state["e_ca"] = e_ca
            if LITE:
                e_z = S.op("vector", lambda v: v.memset(modc[0][:], 0.1))
                for l in range(DEPTH):
                    for nb in range(12):
                        S.dma("sync", modv[l:l + 1, nb * 512:(nb + 1) * 512], modc[0][:], ch_st, deps=[e_z])
            else:
                for nb in range(12 if "mod" in dbg else 8):
                    mod_chunk(0, nb * 512, 512, wa_ring, bank_ring, badac, modc)
                if "mod" in dbg:
                    for nb in range(12):
                        mod_chunk(1, nb * 512, 512, wa_ring, bank_ring, badac, modc)
            if "rope" in dbg:
                S.dma("sync", dbg["rope"][0], cos2[:], ch_st, deps=[e_c])
                S.dma("sync", dbg["rope"][1], sin2[:], ch_st, deps=[e_c])
            S.barrier()
            S.flush()

        x_src = x_in
        for l in range(DEPTH):
            lw = l % WL
            if stop == "p0":
                break
            x_dst = x1 if l == 0 else out
            with ExitStack() as pa:
                sbA = lambda name, shape, dty: pa.enter_context(nc.sbuf_tensor(f"pa{l}_" + name, shape, dty))
                hT = sbA("hT", [128, 16, T], BF16)
                with ExitStack() as pa1:
                    sb = lambda name, shape, dty: pa1.enter_context(nc.sbuf_tensor(f"pa1{l}_" + name, shape, dty))
                    xt = [sb(f"xt{i}", [128, D], F32) for i in range(2)]
                    hb = [sb(f"hb{i}", [128, D], BF16) for i in range(2)]
                    sc_rep = sb("sc_rep", [128, D], F32)
                    sh_rep = sb("sh_rep", [128, D], F32)
                    st = [sb(f"st{i}", [128, 4, 6], F32) for i in range(2)]
                    mv = [sb(f"mv{i}", [128, 2], F32) for i in range(2)]
                    sm = [sb(f"sm{i}", [128, 4], F32) for i in range(2)]
                    e_sh = S.dma("sync", sh_rep[:], modv[l:l + 1, 0:D].partition_broadcast(128), S.tmpchan())
                    e_sc = S.dma("sync", sc_rep[:], modv[l:l + 1, D:2 * D].partition_broadcast(128), S.tmpchan())
                    e_sc = S.op("vector", lambda v: v.tensor_scalar(out=sc_rep[:], in0=sc_rep[:], scalar1=1.0, scalar2=None, op0=ALU.add), deps=[e_sc])
                    ch_x = [S.chan(f"ch_x{i}") for i in range(2)]
                    xfree = [None, None]
                    hfree = [None, None]
                    tp_ring = Ring([0, 1, 2, 3])
                    for t in range(NT):
                        i = t % 2
                        e_x = S.dma("sync", xt[i][:], x_src[t * 128:(t + 1) * 128, :], ch_x[i], deps=[xfree[i]])
                        e_st = None
                        for k in range(4):
                            e_st = S.op("vector", lambda v, i=i, k=k: v.bn_stats(out=st[i][:, k, :], in_=xt[i][:, k * 512:(k + 1) * 512]),
                                        deps=[e_x], ev=(k == 3))
                        e_ag = S.op("vector", lambda v, i=i: v.bn_aggr(out=mv[i][:], in_=st[i][:].rearrange("p a b -> p (a b)")), deps=[e_st])
                        e1 = S.op("vector", lambda v, i=i: v.tensor_scalar(out=sm[i][:, 0:1], in0=mv[i][:, 1:2], scalar1=EPS, scalar2=None, op0=ALU.add), deps=[e_ag])
                        e2 = S.op("scalar", lambda a, i=i: a.activation(out=sm[i][:, 1:2], in_=sm[i][:, 0:1], func=AF.Sqrt), deps=[e1])
                        e3 = S.op("vector", lambda v, i=i: v.reciprocal(out=sm[i][:, 2:3], in_=sm[i][:, 1:2]), deps=[e2])
                        e4 = S.op("vector", lambda v, i=i: v.tensor_scalar(out=sm[i][:, 3:4], in0=mv[i][:, 0:1], scalar1=sm[i][:, 2:3], scalar2=-1.0, op0=ALU.mult, op1=ALU.mult), deps=[e3])
                        e5 = S.op("scalar", lambda a, i=i: a.activation(out=xt[i][:], in_=xt[i][:], func=AF.Identity, bias=sm[i][:, 3:4], scale=sm[i][:, 2:3]), deps=[e4])
                        e6 = S.op("vector", lambda v, i=i: v.tensor_tensor(out=xt[i][:], in0=xt[i][:], in1=sc_rep[:], op=ALU.mult), deps=[e5, e_sc])
                        e7 = S.op("vector", lambda v, i=i: v.tensor_tensor(out=hb[i][:], in0=xt[i][:], in1=sh_rep[:], op=ALU.add), deps=[e6, e_sh, hfree[i]])
                        xfree[i] = e7
                        e_tp = None
                        for g in range(4):
                            bj, bank, bfree = tp_ring.get()
                            pview = ps[:, bank, :].bitcast(BF16)
                            for jj in range(4):
                                kc = 4 * g + jj
                                e_tp = S.op("tensor", lambda tt, pview=pview, i=i, kc=kc, jj=jj: tt.transpose(
                                    pview[:, jj * 128:(jj + 1) * 128], hb[i][:, kc * 128:(kc + 1) * 128], ident[:]),
                                    deps=[e7, bfree, e_id] if jj == 0 else [], ev=(jj == 3))
                            e_ev = S.op("scalar", lambda a, pview=pview, g=g, t=t: a.copy(
                                out=hT[:, 4 * g:4 * g + 4, t * 128:(t + 1) * 128],
                                in_=pview[:, 0:512].rearrange("p (a b) -> p a b", a=4)), deps=[e_tp])
                            tp_ring.release(bj, e_ev)
                        hfree[i] = e_tp
                    if f"hT{l}" in dbg:
                        S.barrier()
                        S.dma("sync", dbg[f"hT{l}"].ap().rearrange("(kc p) t -> p kc t", p=128), hT[:], ch_st)
                    S.barrier()
                    S.flush()
                if stop == f"a1_{l}":
                    break
                with ExitStack() as pa2:
                    sb = lambda name, shape, dty: pa2.enter_context(nc.sbuf_tensor(f"pa2{l}_" + name, shape, dty))
                    wbuf = [sb(f"wbuf{i}", [128, 16, 512], BF16) for i in range(2)]
                    cqT = sb("cqT", [128, 4, T], BF16)
                    ckv_raw = sb("ckv_raw", [128, 2, T], F32)
                    rq_rep = sb("rq_rep", [128, T], F32)
                    rkv_rep = sb("rkv_rep", [128, T], F32)
                    sq = [sb(f"sq{i}", [128, 512], BF16) for i in range(2)]
                    evb = [sb(f"evb{i}", [128, 512], BF16) for i in range(4)]
                    tf = [sb(f"tf{i}", [128, 512], F32) for i in range(3)]
                    wq = sb("wq", [128, 4, 1536], BF16)
                    wqrot = sb("wqrot", [128, 4, 8, 64], BF16)
                    wkrot = sb("wkrot", [128, 16, 64], BF16)
                    gq = sb("gq", [128, 4], F32)
                    gkv = sb("gkv", [128, 2], F32)

                    e_gq = e_gkv = e_wq = None
                    if not os.environ.get("NO_G"):
                        e_gq = S.dma("sync", gq[:], qn_t[lw], S.tmpchan())
                        e_gkv = S.dma("sync", gkv[:], kvn_t[lw], S.tmpchan())
                    if not os.environ.get("NO_WQ"):
                        e_wq = S.dma("gpsimd", wq[:], w_q_b[lw].rearrange("(kc p) n -> p kc n", p=128), S.tmpchan("gpsimd"))

                    w_ring = Ring(wbuf)
                    ch_w = [S.chan("ch_w0"), S.chan("ch_w1")]
                    ev_ring = Ring(evb)
                    ch_ev = [S.chan(f"ch_ev{i}") for i in range(4)]
                    bank_ring = Ring([0, 1, 2, 3])
                    sq_ring = Ring(sq)
                    tf_ring = Ring(tf)

                    GROUPS = [("kv", 2560, 320), ("u", 0, 512), ("u", 512, 512), ("cq", 2048, 512),
                              ("z", 1024, 512), ("z", 1536, 512), ("z", 2880, 512), ("z", 3392, 512)]
                    SZROW = {1024: 0, 1536: 512, 2880: 1024, 3392: 1536}
                    loaded = {}

                    def load_group(gi):
                        kind, c0, ncol = GROUPS[gi]
                        j, wt, fr = w_ring.get()
                        src = w_in[lw, :, c0:c0 + ncol].rearrange("(kc p) n -> p kc n", p=128)
                        e = S.dma("gpsimd", wt[:, :, 0:ncol], src, ch_w[j], deps=[fr])
                        loaded[gi] = (j, wt, e)

                    def store(dst, j, ev_t, e_ev, npart=128, eng="gpsimd"):
                        if os.environ.get("NO_STORE"):
                            ev_ring.release(j, e_ev)
                            return e_ev
                        eng = os.environ.get("STORE_ENG", eng)
                        e_st = S.dma(eng, dst, ev_t[0:npart, :], ch_ev[j], deps=[e_ev])
                        ev_ring.release(j, e_st)
                        return e_st

                    def mm_group(bank, lhs_fn, rhs_fn, nk, deps, mpart=128, ncols=512):
                        em = None
                        for kc in range(nk):
                            em = S.op("tensor", lambda t, bank=bank, kc=kc: t.matmul(
                                ps[0:mpart, bank, 0:ncols], lhs_fn(kc), rhs_fn(kc), start=(kc == 0), stop=(kc == nk - 1)),
                                deps=deps if kc == 0 else [], ev=(kc == nk - 1))
                        return em

                    def rms_chunks(wt, e_w, nchunk, gvec, e_g, raw_dst, r_rep, dim, tb, ssq_bank, last_mm):
                        tsl = slice(tb * 512, (tb + 1) * 512)
                        e_ss = None
                        for c in range(nchunk):
                            bj, bank, bfree = bank_ring.get()
                            em = mm_group(bank, lambda kc, c=c: wt[:, kc, c * 128:(c + 1) * 128], lambda kc: hT[:, kc, tsl], 16, [e_w, bfree])
                            sj, sqt, sfree = sq_ring.get()
                            if os.environ.get("NO_SQ"):
                                e_sq = em
                            else:
                                e_sq = S.op("scalar", lambda a, bank=bank, sqt=sqt: a.activation(out=sqt[:], in_=ps[:, bank, :], func=AF.Square), deps=[em, sfree])
                            if os.environ.get("NO_RAW"):
                                e_raw = em
                            else:
                                rm = os.environ.get("RAWMODE", "")
                                if rm == "const":
                                    e_raw = S.op("vector", lambda v, bank=bank, c=c: v.tensor_scalar(
                                        out=raw_dst[:, c, tsl], in0=ps[:, bank, :], scalar1=1.0, scalar2=None, op0=ALU.mult), deps=[em, e_g])
                                elif rm == "nodep":
                                    e_raw = S.op("vector", lambda v, bank=bank, c=c: v.tensor_scalar(
                                        out=raw_dst[:, c, tsl], in0=ps[:, bank, :], scalar1=gvec[:, c:c + 1], scalar2=None, op0=ALU.mult), deps=[em])
                                elif rm == "act":
                                    e_raw = S.op("scalar", lambda a, bank=bank, c=c: a.activation(
                                        out=raw_dst[:, c, tsl], in_=ps[:, bank, :], func=AF.Copy, scale=gvec[:, c:c + 1]), deps=[em, e_g])
                                else:
                                    e_raw = S.op("vector", lambda v, bank=bank, c=c: v.tensor_scalar(
                                        out=raw_dst[:, c, tsl], in0=ps[:, bank, :], scalar1=gvec[:, c:c + 1], scalar2=None, op0=ALU.mult), deps=[em, e_g, e_sq])
                            bank_ring.release(bj, [e_sq, e_raw])
                            if os.environ.get("NO_SSQ"):
                                e_ss = e_sq
                            else:
                                e_ss = S.op("tensor", lambda t, sqt=sqt, c=c: t.matmul(
                                    ps[:, ssq_bank, :], ones_bf[:], sqt[:], start=(c == 0), stop=(c == nchunk - 1)),
                                    deps=[e_sq, ssq_free[ssq_bank]] if c == 0 else [e_sq])
                            sq_ring.release(sj, e_ss)
                            last_mm[0] = em
                        if os.environ.get("NO_RTAIL"):
                            return e_ss, e_raw
                        tj, tft, tfree = tf_ring.get()
                        e1 = S.op("vector", lambda v, tft=tft: v.tensor_scalar(out=tft[:], in0=ps[:, ssq_bank, :], scalar1=1.0 / dim, scalar2=EPS, op0=ALU.mult, op1=ALU.add), deps=[e_ss, tfree])
                        ssq_free[ssq_bank] = e1
                        e2 = S.op("scalar", lambda a, tft=tft: a.activation(out=tft[:], in_=tft[:], func=AF.Sqrt), deps=[e1])
                        e3 = S.op("vector", lambda v, tft=tft: v.reciprocal(out=r_rep[:, tsl], in_=tft[:]), deps=[e2])
                        tf_ring.release(tj, e3)
                        return e3, e_raw

                    def rope_pair(bA, bB, e_mm, r_rep, e_r, tb, dst):
                        tsl = slice(tb * 512, (tb + 1) * 512)
                        t1j, t1, f1 = tf_ring.get()
                        t2j, t2, f2 = tf_ring.get()
                        ea = S.op("vector", lambda v: v.tensor_tensor(out=t1[0:64, :], in0=ps[0:64, bA[1], :], in1=cos2[:, tsl], op=ALU.mult), deps=[e_mm, f1])
                        eb = S.op("vector", lambda v: v.tensor_tensor(out=t2[0:64, :], in0=ps[0:64, bB[1], :], in1=sin2[:, tsl], op=ALU.mult), deps=[e_mm, f2])
                        bank_ring.release(bA[0], ea)
                        bank_ring.release(bB[0], eb)
                        j, ev_t, fr = ev_ring.get()
                        if r_rep is None:
                            ec = S.op("vector", lambda v: v.tensor_tensor(out=ev_t[0:64, :], in0=t1[0:64, :], in1=t2[0:64, :], op=ALU.add), deps=[ea, eb, fr])
                        else:
                            ec0 = S.op("vector", lambda v: v.tensor_tensor(out=t1[0:64, :], in0=t1[0:64, :], in1=t2[0:64, :], op=ALU.add), deps=[ea, eb])
                            ec = S.op("vector", lambda v: v.tensor_tensor(out=ev_t[0:64, :], in0=t1[0:64, :], in1=r_rep[0:64, tsl], op=ALU.mult), deps=[ec0, e_r, fr])
                        tf_ring.release(t1j, ec)
                        tf_ring.release(t2j, ec)
                        return store(dst, j, ev_t, ec, npart=64)

                    ssq_free = {4: None, 5: None}
                    kv_stores, u_stores = [], []
                    load_group(0)
                    A2N = int(os.environ.get('A2N', '99'))
                    GROUPS = GROUPS[:A2N]
                    for gi, (kind, c0, ncol) in enumerate(GROUPS):
                        if gi + 1 < len(GROUPS):
                            load_group(gi + 1)
                        j_w, wt, e_w = loaded[gi]
                        last_mm = [None]
                        if kind == "kv":
                            if os.environ.get("NO_WKROT"):
                                continue
                            e_r1 = S.op("vector", lambda v, wt=wt: v.tensor_scalar(out=wkrot[:, :, 0:32], in0=wt[:, :, 288:320], scalar1=-1.0, scalar2=None, op0=ALU.mult), deps=[e_w])
                            e_r2 = S.op("vector", lambda v, wt=wt: v.tensor_copy(out=wkrot[:, :, 32:64], in_=wt[:, :, 256:288]), deps=[e_w])
                            for tb in range(int(os.environ.get("KV_TB", "4"))):
                                tsl = slice(tb * 512, (tb + 1) * 512)
                                e_r, e_raw = rms_chunks(wt, e_w, 2, gkv, e_gkv, ckv_raw, rkv_rep, 256.0, tb, 4 + tb % 2, last_mm)
                                for c in range(2):
                                    j, ev_t, fr = ev_ring.get()
                                    e_n = S.op("vector", lambda v, c=c, ev_t=ev_t, tsl=tsl: v.tensor_tensor(
                                        out=ev_t[:], in0=ckv_raw[:, c, tsl], in1=rkv_rep[:, tsl], op=ALU.mult), deps=[e_r, e_raw, fr])
                                    kv_stores.append(store(kvT[l][c * 128:(c + 1) * 128, tsl], j, ev_t, e_n))
                                if os.environ.get("NO_KR"):
                                    continue
                                bA = bank_ring.get()
                                emA = mm_group(bA[1], lambda kc, wt=wt: wt[:, kc, 256:320], lambda kc, tsl=tsl: hT[:, kc, tsl], 16, [e_w, bA[2]], mpart=64)
                                bB = bank_ring.get()
                                emB = mm_group(bB[1], lambda kc: wkrot[:, kc, :], lambda kc, tsl=tsl: hT[:, kc, tsl], 16, [e_r1, e_r2, bB[2]], mpart=64)
                                last_mm[0] = emB
                                kv_stores.append(rope_pair(bA, bB, [emA, emB], None, None, tb, kvT[l][256:320, tsl]))
                            e_cc_kv = collective(kvT[l], kvAll[l], kv_stores)
                        elif kind == "u":
                            for t in range(NT):
                                bj, bank, bfree = bank_ring.get()
                                em = mm_group(bank, lambda kc, t=t: hT[:, kc, t * 128:(t + 1) * 128], lambda kc, wt=wt: wt[:, kc, 0:512], 16, [e_w, bfree])
                                last_mm[0] = em
                                j, ev_t, fr = ev_ring.get()
                                e_ev = S.op("scalar", lambda a, bank=bank, ev_t=ev_t: a.copy(out=ev_t[:], in_=ps[:, bank, :]), deps=[em, fr])
                                bank_ring.release(bj, e_ev)
                                dstT = uTokA[l] if t < 8 else uTokB[l]
                                tt = t % 8
                                u_stores.append(store(dstT[tt * 128:(tt + 1) * 128, c0:c0 + 512], j, ev_t, e_ev))
                            if c0 == 512:
                                e_cc_uA = collective(uTokA[l], uAllA[l], u_stores)
                                e_cc_uB = collective(uTokB[l], uAllB[l], u_stores)
                        elif kind == "cq":
                            for tb in range(4):
                                e_rq, e_cq = rms_chunks(wt, e_w, 4, gq, e_gq, cqT, rq_rep, 512.0, tb, 4 + tb % 2, last_mm)
                            e_rq_all, e_cq_all = e_rq, e_cq
                        else:
                            row0 = SZROW[c0]
                            for c in range(4):
                                for tb in range(4):
                                    tsl = slice(tb * 512, (tb + 1) * 512)
                                    bj, bank, bfree = bank_ring.get()
                                    em = mm_group(bank, lambda kc, c=c, wt=wt: wt[:, kc, c * 128:(c + 1) * 128], lambda kc, tsl=tsl: hT[:, kc, tsl], 16, [e_w, bfree])
                                    last_mm[0] = em
                                    j, ev_t, fr = ev_ring.get()
                                    e_ev = S.op("scalar", lambda a, bank=bank, ev_t=ev_t: a.activation(out=ev_t[:], in_=ps[:, bank, :], func=AF.Silu), deps=[em, fr])
                                    bank_ring.release(bj, e_ev)
                                    store(szT[l][row0 + c * 128:row0 + (c + 1) * 128, tsl], j, ev_t, e_ev)
                        w_ring.release(j_w, last_mm[0])

                        if kind == "cq" and not os.environ.get('NO_A3'):
                            e_rots = []
                            for kc in range(4):
                                wqv = wq[:, kc, :].rearrange("p (h d) -> p h d", h=8)
                                e_rots.append(S.op("vector", lambda v, kc=kc, wqv=wqv: v.tensor_scalar(out=wqrot[:, kc, :, 0:32], in0=wqv[:, :, 160:192], scalar1=-1.0, scalar2=None, op0=ALU.mult), deps=[e_wq]))
                                e_rots.append(S.op("vector", lambda v, kc=kc, wqv=wqv: v.tensor_copy(out=wqrot[:, kc, :, 32:64], in_=wqv[:, :, 128:160]), deps=[e_wq]))
                            for h in range(NH):
                                for tb in range(4):
                                    tsl = slice(tb * 512, (tb + 1) * 512)
                                    bj, bank, bfree = bank_ring.get()
                                    em = mm_group(bank, lambda kc, h=h: wq[:, kc, 192 * h:192 * h + 128], lambda kc, tsl=tsl: cqT[:, kc, tsl], 4, [e_wq, e_cq_all, bfree])
                                    j, ev_t, fr = ev_ring.get()
                                    e_ev = S.op("vector", lambda v, bank=bank, ev_t=ev_t, tsl=tsl: v.tensor_tensor(
                                        out=ev_t[:], in0=ps[:, bank, :], in1=rq_rep[:, tsl], op=ALU.mult), deps=[em, e_rq_all, fr])
                                    bank_ring.release(bj, e_ev)
                                    store(qT[l][h, 0:128, tsl], j, ev_t, e_ev)
                                    bA = bank_ring.get()
                                    emA = mm_group(bA[1], lambda kc, h=h: wq[:, kc, 192 * h + 128:192 * h + 192], lambda kc, tsl=tsl: cqT[:, kc, tsl], 4, [e_wq, e_cq_all, bA[2]], mpart=64)
                                    bB = bank_ring.get()
                                    emB = mm_group(bB[1], lambda kc, h=h: wqrot[:, kc, h, :], lambda kc, tsl=tsl: cqT[:, kc, tsl], 4, e_rots + [e_cq_all, bB[2]], mpart=64)
                                    rope_pair(bA, bB, [emA, emB], rq_rep, e_rq_all, tb, qT[l][h, 128:192, tsl])

                    S.barrier(extra=[(cc_sem, cc_cnt[0])])
                    if f"uTok{l}" in dbg:
                        S.dma("sync", dbg[f"uTok{l}"][0:1024, :], uTokA[l][:, :], ch_st)
                        S.dma("sync", dbg[f"uTok{l}"][1024:2048, :], uTokB[l][:, :], ch_st)
                        S.dma("sync", dbg[f"kvT{l}"][:, :], kvT[l][:, :], ch_st)
                        S.dma("sync", dbg[f"szT{l}"][:, :], szT[l][:, :], ch_st)
                        S.dma("sync", dbg[f"qT{l}"].ap().rearrange("h d t -> (h d) t"), qT[l].ap().rearrange("h d t -> (h d) t"), ch_st)
                        S.barrier()
                    S.flush()
            if stop == f"a_{l}":
                break
            with ExitStack() as pb1:
                sb = lambda name, shape, dty: pb1.enter_context(nc.sbuf_tensor(f"pb1{l}_" + name, shape, dty))
                u = sb("u", [128, 32, 1024], BF16)
                tabb = [sb(f"tabb{i}", [128, 32, 512], BF16) for i in range(2)]
                ccs = sb("ccs", [128, 2, 128], BF16)
                scs = sb("scs", [128, 2, 128], BF16)
                wf = sb("wf", [128, 8, 128], F32)
                wfh = sb("wfh", [128, 8, 128], BF16)
                wfl = sb("wfl", [128, 8, 128], BF16)
                G1s = sb("G1s", [128, 8, 128], BF16)
                G2s = sb("G2s", [128, 8, 128], BF16)
                XT = [sb(f"XT{i}", [128, 512], BF16) for i in range(2)]
                szt = [sb(f"szt{i}", [128, 256], BF16) for i in range(3)]
                yev = [sb(f"yev{i}", [128, 256], BF16) for i in range(3)]

                e_uA = S.dma("sync", u[:, 0:16, :], uAllA[l].ap().rearrange("(st p) c -> p st c", p=128), S.tmpchan())
                e_uB = S.dma("sync", u[:, 16:32, :], uAllB[l].ap().rearrange("(st p) c -> p st c", p=128), S.tmpchan())
                e_cc = S.dma("sync", ccs[:], cc_in[:, :, :], S.tmpchan())
                e_sc2 = S.dma("sync", scs[:], sc_in[:, :, :], S.tmpchan())
                e_wf0 = S.dma("sync", wf[:], w_fmix[lw].rearrange("g m d -> m g d"), S.tmpchan())
                e_wfh = S.op("vector", lambda v: v.tensor_copy(out=wfh[:], in_=wf[:]), deps=[e_wf0])
                e_wfl0 = S.op("vector", lambda v: v.tensor_tensor(out=wf[:], in0=wf[:], in1=wfh[:], op=ALU.subtract), deps=[e_wfh])
                e_wf = S.op("vector", lambda v: v.tensor_copy(out=wfl[:], in_=wf[:]), deps=[e_wfl0])
                ch_tab = [S.chan("ch_tab0"), S.chan("ch_tab1")]
                tabfree = [None, None]

                def load_tab(kb):
                    i = kb % 2
                    srcv = tab[kb % NTAB].rearrange("(st p) n -> p st n", p=128)
                    S.dma("sync", tabb[i][:, 0:16, :], srcv[:, 0:16, :], ch_tab[i], deps=[tabfree[i]])
                    return S.dma("sync", tabb[i][:, 16:32, :], srcv[:, 16:32, :], ch_tab[i], deps=[tabfree[i]])

                gbank = Ring([6, 7])
                e_G = []
                for g in range(8):
                    for (tabm, dstG, scl) in ((ccs, G1s, FNORM), (scs, G2s, -FNORM)):
                        bj, bank, bfree = gbank.get()
                        S.op("tensor", lambda t, bank=bank, tabm=tabm, g=g: t.matmul(ps[:, bank, 0:128], tabm[:, 0, :], wfh[:, g, :], start=True, stop=False),
                             deps=[e_cc, e_sc2, e_wf, bfree], ev=False)
                        S.op("tensor", lambda t, bank=bank, tabm=tabm, g=g: t.matmul(ps[:, bank, 0:128], tabm[:, 0, :], wfl[:, g, :], start=False, stop=False), ev=False)
                        em = S.op("tensor", lambda t, bank=bank, tabm=tabm, g=g: t.matmul(ps[:, bank, 0:128], tabm[:, 1, :], wfh[:, g, :], start=False, stop=True))
                        ee = S.op("scalar", lambda a, bank=bank, dstG=dstG, g=g, scl=scl: a.activation(out=dstG[:, g, :], in_=ps[:, bank, 0:128], func=AF.Copy, scale=scl), deps=[em])
                        gbank.release(bj, ee)
                        e_G.append(ee)

                mbank = Ring([0, 1, 2])
                cbank = Ring([4, 5])
                xt_ring = Ring(XT)
                szt_ring = Ring(szt)
                yev_ring = Ring(yev)
                ch_szt = [S.chan(f"ch_szt{i}") for i in range(3)]
                ch_yev = [S.chan(f"ch_yev{i}") for i in range(3)]

                def channel_stage(kb, g, xj, xtt, e_x, sj, sztt, e_sz):
                    bj, bank, bfree = cbank.get()
                    S.op("tensor", lambda t: t.matmul(ps[:, bank, 0:256], G1s[:, g, :], xtt[:, 0:256], start=True, stop=False),
                         deps=[e_x, bfree] + e_G, ev=False)
                    em = S.op("tensor", lambda t: t.matmul(ps[:, bank, 0:256], G2s[:, g, :], xtt[:, 256:512], start=False, stop=True))
                    xt_ring.release(xj, em)
                    yj, yt, yfree = yev_ring.get()
                    ee = S.op("vector", lambda v: v.tensor_tensor(out=yt[:], in0=ps[:, bank, 0:256], in1=sztt[:], op=ALU.mult), deps=[em, e_sz, yfree])
                    cbank.release(bj, ee)
                    szt_ring.release(sj, ee)
                    e_st = S.dma("gpsimd", yT[l][g * 128:(g + 1) * 128, kb * 256:(kb + 1) * 256], yt[:], ch_yev[yj], deps=[ee])
                    yev_ring.release(yj, e_st)

                pending = None
                e_tab = load_tab(0)
                for kb in range(8):
                    i = kb % 2
                    e_tab_cur = e_tab
                    if kb + 1 < 8:
                        e_tab = load_tab(kb + 1)
                    em = None
                    for g in range(8):
                        sj, sztt, sfree = szt_ring.get()
                        e_sz = S.dma("sync", sztt[:], szT[l][g * 128:(g + 1) * 128, kb * 256:(kb + 1) * 256], ch_szt[sj], deps=[sfree])
                        bj, bank, bfree = mbank.get()
                        for st_ in range(32):
                            em = S.op("tensor", lambda t, bank=bank, st_=st_, g=g, i=i: t.matmul(
                                ps[:, bank, :], u[:, st_, g * 128:(g + 1) * 128], tabb[i][:, st_, :], start=(st_ == 0), stop=(st_ == 31)),
                                deps=[e_uA, e_uB, e_tab_cur, bfree] if st_ == 0 else [], ev=(st_ == 31))
                        xj, xtt, xfree = xt_ring.get()
                        e_x = S.op("scalar", lambda a, bank=bank, xtt=xtt: a.copy(out=xtt[:], in_=ps[:, bank, :]), deps=[em, xfree])
                        mbank.release(bj, e_x)
                        if pending is not None:
                            channel_stage(*pending)
                        pending = (kb, g, xj, xtt, e_x, sj, sztt, e_sz)
                    tabfree[i] = em
                channel_stage(*pending)
                S.barrier()
                S.flush()
            if stop == f"b1_{l}":
                if f"yT{l}" in dbg:
                    S.dma("sync", dbg[f"yT{l}"][0:1024, :], yT[l][0:1024, :], ch_st)
                break
            pbw = ExitStack()
            wo = pbw.enter_context(nc.sbuf_tensor(f"pbw{l}_wo", [128, 16, D], BF16))
            e_wo = []
            for cb in range(4):
                e_wo.append(S.dma("gpsimd", wo[:, :, cb * 512:(cb + 1) * 512], w_out[lw, :, cb * 512:(cb + 1) * 512].rearrange("(kc p) n -> p kc n", p=128), S.chan(f"ch_wo{cb}")))
            with ExitStack() as pb2:
                sb = lambda name, shape, dty: pb2.enter_context(nc.sbuf_tensor(f"pb2{l}_" + name, shape, dty))
                do_mod1 = (l == 0 and not LITE and "mod" not in dbg)
                if do_mod1:
                    wa2 = [sb(f"wa2_{i}", [128, 16, 256], BF16) for i in range(2)]
                    badac2 = [sb(f"badac2_{i}", [1, 512], F32) for i in range(2)]
                    modc2 = [sb(f"modc2_{i}", [1, 512], F32) for i in range(2)]
                    wa2_ring = Ring(wa2)
                    mod_pend = [None]
                ckvT = sb("ckvT", [128, 2, SEQ], BF16)
                krT = sb("krT", [128, SEQ], BF16)
                wkv = sb("wkv", [128, 2, 2048], BF16)
                kT = [sb(f"kT{i}", [128, SEQ], BF16) for i in range(2)]
                vS = [sb(f"vS{i}", [128, 32, 128], BF16) for i in range(2)]
                qn = [sb(f"qn{i}", [128, T], BF16) for i in range(2)]
                qr = [sb(f"qr{i}", [128, T], BF16) for i in range(2)]
                pT = [sb(f"pT{i}", [128, 512], BF16) for i in range(4)]
                rden = [sb(f"rden{i}", [128, 512], F32) for i in range(2)]
                sza = [sb(f"sza{i}", [128, 512], BF16) for i in range(2)]
                yev = [sb(f"yev{i}", [128, 512], BF16) for i in range(2)]

                e_kvl = [S.op("vector", lambda v: v.memset(krT[64:128, :], 0.0)),
                         S.op("vector", lambda v: v.memset(qr[0][64:128, :], 0.0)),
                         S.op("vector", lambda v: v.memset(qr[1][64:128, :], 0.0))]
                for r in range(2):
                    for c in range(2):
                        e_kvl.append(S.dma("sync", ckvT[:, c, r * T:(r + 1) * T], kvAll[l][320 * r + 128 * c:320 * r + 128 * (c + 1), :], S.tmpchan()))
                    e_kvl.append(S.dma("sync", krT[0:64, r * T:(r + 1) * T], kvAll[l][320 * r + 256:320 * r + 320, :], S.tmpchan()))
                e_wkv = S.dma("gpsimd", wkv[:], w_kv_b[lw].rearrange("(kc p) n -> p kc n", p=128), S.tmpchan("gpsimd"))

                ch_q = [S.chan("ch_q0"), S.chan("ch_q1")]
                ch_sza = [S.chan("ch_sza0"), S.chan("ch_sza1")]
                ch_ya = [S.chan("ch_ya0"), S.chan("ch_ya1")]
                qfree = [None, None]
                kvfree = [None, None]
                sbank = Ring([0, 1, 2])
                obank = Ring([3, 4])
                dbank = Ring([5])
                bbank = Ring([6, 7])
                pt_ring = Ring(pT)
                sza_ring = Ring(sza)
                yev_ring = Ring(yev)
                fin_i = [0]
                kb_ev = {}
                q_ev = {}

                def build_kv(h):
                    i = h % 2
                    S.dma("sync", qn[i][:], qT[l][h, 0:128, :], ch_q[i], deps=[qfree[i]])
                    q_ev[h] = S.dma("sync", qr[i][0:64, :], qT[l][h, 128:192, :], ch_q[i], deps=[qfree[i]])
                    e_kb = []
                    for kblk in range(8):
                        bj, bank, bfree = bbank.get()
                        ksl = slice(kblk * 512, (kblk + 1) * 512)
                        for kc in range(2):
                            em = S.op("tensor", lambda t, bank=bank, kc=kc, h=h, ksl=ksl: t.matmul(
                                ps[:, bank, :], wkv[:, kc, 256 * h:256 * h + 128], ckvT[:, kc, ksl], start=(kc == 0), stop=(kc == 1)),
                                deps=[e_wkv, bfree] + e_kvl if kc == 0 else [], ev=(kc == 1))
                        ee = S.op("vector", lambda v, bank=bank, i=i, ksl=ksl: v.tensor_copy(out=kT[i][:, ksl], in_=ps[:, bank, :]), deps=[em, kvfree[i]])
                        bbank.release(bj, ee)
                        e_kb.append(ee)
                    for grp in range(8):
                        bj, bank, bfree = bbank.get()
                        for jj in range(4):
                            kt = 4 * grp + jj
                            for kc in range(2):
                                em = S.op("tensor", lambda t, bank=bank, kc=kc, h=h, kt=kt, jj=jj: t.matmul(
                                    ps[:, bank, jj * 128:(jj + 1) * 128], ckvT[:, kc, kt * 128:(kt + 1) * 128],
                                    wkv[:, kc, 256 * h + 128:256 * h + 256], start=(kc == 0), stop=(kc == 1)),
                                    deps=[e_wkv, bfree] + e_kvl if (jj == 0 and kc == 0) else [], ev=(jj == 3 and kc == 1))
                        ee = S.op("vector", lambda v, bank=bank, i=i, grp=grp: v.tensor_copy(
                            out=vS[i][:, 4 * grp:4 * grp + 4, :], in_=ps[:, bank, :].rearrange("p (a b) -> p a b", a=4)), deps=[em, kvfree[i]])
                        bbank.release(bj, ee)
                        e_kb.append(ee)
                    kb_ev[h] = e_kb

                build_kv(0)
                for h in range(NH):
                    i = h % 2
                    e_q = q_ev[h]
                    e_kb = kb_ev[h]
                    last_pv = None
                    for qb in range(4):
                        qsl = slice(qb * 512, (qb + 1) * 512)
                        zj, szat, zfree = sza_ring.get()
                        e_sza = S.dma("sync", szat[:], szT[l][1024 + 128 * h:1024 + 128 * (h + 1), qsl], ch_sza[zj], deps=[zfree])
                        oj, ob, ofree = obank.get()
                        dj, db, dfree = dbank.get()
                        s_ev = {}

                        def s_mm(kt, i=i, qsl=qsl):
                            bj, bank, bfree = sbank.get()
                            S.op("tensor", lambda t: t.matmul(ps[:, bank, :], kT[i][:, kt * 128:(kt + 1) * 128], qn[i][:, qsl], start=True, stop=False),
                                 deps=[e_q, bfree] + e_kb, ev=False)
                            em = S.op("tensor", lambda t: t.matmul(ps[:, bank, :], krT[:, kt * 128:(kt + 1) * 128], qr[i][:, qsl], start=False, stop=True))
                            s_ev[kt] = (bj, bank, em)

                        s_mm(0)
                        s_mm(1)
                        for kt in range(32):
                            bj, bank, em = s_ev.pop(kt)
                            pj, ptt, pfree = pt_ring.get()
                            e_p = S.op("scalar", lambda a, bank=bank, ptt=ptt: a.activation(out=ptt[:], in_=ps[:, bank, :], func=AF.Exp, scale=SM_SCALE), deps=[em, pfree])
                            sbank.release(bj, e_p)
                            if kt + 2 < 32:
                                s_mm(kt + 2)
                            S.op("tensor", lambda t, kt=kt, ptt=ptt, ob=ob, i=i: t.matmul(ps[:, ob, :], vS[i][:, kt, :], ptt[:], start=(kt == 0), stop=(kt == 31)),
                                 deps=[e_p, ofree] if kt == 0 else [e_p], ev=False)
                            last_pv = S.op("tensor", lambda t, kt=kt, ptt=ptt, db=db: t.matmul(ps[:, db, :], ones_bf[:], ptt[:], start=(kt == 0), stop=(kt == 31)),
                                           deps=[dfree] if kt == 0 else [])
                            pt_ring.release(pj, last_pv)
                        fi = fin_i[0] % 2
                        fin_i[0] += 1
                        e1 = S.op("vector", lambda v, fi=fi, db=db: v.reciprocal(out=rden[fi][:], in_=ps[:, db, :]), deps=[last_pv])
                        dbank.release(dj, e1)
                        e2 = S.op("vector", lambda v, fi=fi, ob=ob: v.tensor_tensor(out=rden[fi][:], in0=ps[:, ob, :], in1=rden[fi][:], op=ALU.mult), deps=[e1])
                        obank.release(oj, e2)
                        yj, yt, yfree = yev_ring.get()
                        e3 = S.op("vector", lambda v, fi=fi, yt=yt, szat=szat: v.tensor_tensor(out=yt[:], in0=rden[fi][:], in1=szat[:], op=ALU.mult), deps=[e2, e_sza, yfree])
                        sza_ring.release(zj, e3)
                        e_st = S.dma("gpsimd", yT[l][1024 + 128 * h:1024 + 128 * (h + 1), qsl], yt[:], ch_ya[yj], deps=[e3])
                        yev_ring.release(yj, e_st)
                        if qb == 1 and h + 1 < NH:
                            build_kv(h + 1)
                        if do_mod1:
                            slot = 4 * h + qb
                            mod_job = lambda k: (1, k * 256) if k < 24 else (0, 4096 + (k - 24) * 256)
                            if slot == 0:
                                mod_pend[0] = mod_load(mod_job(0)[0], mod_job(0)[1], 256, wa2_ring)
                            cur = mod_pend[0]
                            if slot + 1 < 32:
                                mod_pend[0] = mod_load(mod_job(slot + 1)[0], mod_job(slot + 1)[1], 256, wa2_ring)
                            mod_compute(cur, wa2_ring, bbank, badac2, modc2)
                    qfree[i] = last_pv
                    kvfree[i] = last_pv
                S.barrier()
                S.flush()
            if stop == f"b2_{l}":
                if f"yT{l}" in dbg:
                    S.dma("sync", dbg[f"yT{l}"][:, :], yT[l][:, :], ch_st)
                break
            with ExitStack() as pb3:
                sb = lambda name, shape, dty: pb3.enter_context(nc.sbuf_tensor(f"pb3{l}_" + name, shape, dty))
                yTs = sb("yTs", [128, 16, T], BF16)
                gate_rep = sb("gate_rep", [128, D], F32)
                g_rep = sb("g_rep", [128, D], F32)
                b_rep = sb("b_rep", [128, D], F32)
                xt = [sb(f"xt{i}", [128, D], F32) for i in range(2)]
                vv = [sb(f"vv{i}", [128, D], F32) for i in range(2)]
                st = [sb(f"st{i}", [128, 4, 6], F32) for i in range(2)]
                mv = [sb(f"mv{i}", [128, 2], F32) for i in range(2)]
                sm = [sb(f"sm{i}", [128, 4], F32) for i in range(2)]
                e_y = []
                for q4 in range(4):
                    e_y.append(S.dma("sync", yTs[:, 4 * q4:4 * q4 + 4, :], yT[l][512 * q4:512 * (q4 + 1), :].rearrange("(kc p) t -> p kc t", p=128), S.tmpchan()))
                e_gate = S.dma("sync", gate_rep[:], modv[l:l + 1, 2 * D:3 * D].partition_broadcast(128), S.tmpchan())
                e_g = S.dma("sync", g_rep[:], ln_g[lw:lw + 1, :].partition_broadcast(128), S.tmpchan())
                e_b = S.dma("sync", b_rep[:], ln_b[lw:lw + 1, :].partition_broadcast(128), S.tmpchan())
                ch_x = [S.chan(f"ch_x{i}") for i in range(2)]
                ch_o = [S.chan(f"ch_o{i}") for i in range(2)]
                xfree = [None, None]
                vfree = [None, None]
                obank = Ring(list(range(8)))
                for t in range(NT):
                    i = t % 2
                    e_x = S.dma("sync", xt[i][:], x_src[t * 128:(t + 1) * 128, :], ch_x[i], deps=[xfree[i]])
                    e_gm = []
                    for cb in range(4):
                        bj, bank, bfree = obank.get()
                        csl = slice(cb * 512, (cb + 1) * 512)
                        for kc in range(16):
                            em = S.op("tensor", lambda tt, bank=bank, kc=kc, t=t, csl=csl: tt.matmul(
                                ps[:, bank, :], yTs[:, kc, t * 128:(t + 1) * 128], wo[:, kc, csl], start=(kc == 0), stop=(kc == 15)),
                                deps=e_y + [e_wo[cb], bfree] if kc == 0 else [], ev=(kc == 15))
                        ee = S.op("vector", lambda v, bank=bank, i=i, csl=csl: v.tensor_tensor(out=vv[i][:, csl], in0=ps[:, bank, :], in1=gate_rep[:, csl], op=ALU.mult),
                                  deps=[em, e_gate, vfree[i]])
                        obank.release(bj, ee)
                        e_gm.append(ee)
                    e_v = S.op("vector", lambda v, i=i: v.scalar_tensor_tensor(out=vv[i][:], in0=xt[i][:], scalar=ALPHA, in1=vv[i][:], op0=ALU.mult, op1=ALU.add), deps=e_gm + [e_x])
                    xfree[i] = e_v
                    e_st = None
                    for k in range(4):
                        e_st = S.op("vector", lambda v, i=i, k=k: v.bn_stats(out=st[i][:, k, :], in_=vv[i][:, k * 512:(k + 1) * 512]), deps=[e_v], ev=(k == 3))
                    e_ag = S.op("vector", lambda v, i=i: v.bn_aggr(out=mv[i][:], in_=st[i][:].rearrange("p a b -> p (a b)")), deps=[e_st])
                    e1 = S.op("vector", lambda v, i=i: v.tensor_scalar(out=sm[i][:, 0:1], in0=mv[i][:, 1:2], scalar1=EPS, scalar2=None, op0=ALU.add), deps=[e_ag])
                    e2 = S.op("scalar", lambda a, i=i: a.activation(out=sm[i][:, 1:2], in_=sm[i][:, 0:1], func=AF.Sqrt), deps=[e1])
                    e3 = S.op("vector", lambda v, i=i: v.reciprocal(out=sm[i][:, 2:3], in_=sm[i][:, 1:2]), deps=[e2])
                    e4 = S.op("vector", lambda v, i=i: v.tensor_scalar(out=sm[i][:, 3:4], in0=mv[i][:, 0:1], scalar1=sm[i][:, 2:3], scalar2=-1.0, op0=ALU.mult, op1=ALU.mult), deps=[e3])
                    e5 = S.op("scalar", lambda a, i=i: a.activation(out=vv[i][:], in_=vv[i][:], func=AF.Identity, bias=sm[i][:, 3:4], scale=sm[i][:, 2:3]), deps=[e4])
                    e6 = S.op("vector", lambda v, i=i: v.tensor_tensor(out=vv[i][:], in0=vv[i][:], in1=g_rep[:], op=ALU.mult), deps=[e5, e_g])
                    e7 = S.op("vector", lambda v, i=i: v.tensor_tensor(out=vv[i][:], in0=vv[i][:], in1=b_rep[:], op=ALU.add), deps=[e6, e_b])
                    e_o = S.dma("gpsimd", x_dst[t * 128:(t + 1) * 128, :], vv[i][:], ch_o[i], deps=[e7])
                    vfree[i] = e_o
                S.barrier()
                S.flush()
            pbw.close()
            if stop == f"b3_{l}":
                if "x1" in dbg:
                    S.dma("sync", dbg["x1"][:, :], x1[:, :], ch_st)
                break
            x_src = x1

        S.barrier()
        S.flush()
    return nc


_PROGRAM_CACHE = {}


def _host_consts():
    m = np.arange(128, dtype=np.float64)
    angc = 2 * np.pi * np.outer(m, m) / 128.0
    def hilo(a):
        hi = a.astype(np.float32).astype(ml_dtypes.bfloat16)
        lo = (a - hi.astype(np.float64)).astype(np.float32).astype(ml_dtypes.bfloat16)
        return np.ascontiguousarray(np.stack([hi, lo], axis=1))
    cc = hilo(np.cos(angc))
    sc = hilo(np.sin(angc))
    invf = (10000.0 ** (-np.arange(0, 64, 2, dtype=np.float32) / np.float32(64))).astype(np.float32)
    invf2 = np.concatenate([invf, invf]).reshape(64, 1).astype(np.float32)
    ident = np.eye(128, dtype=np.float32).astype(ml_dtypes.bfloat16)
    return cc, sc, invf2, ident


def _token_lists():
    return [np.arange(0, 2048), np.arange(2048, 4096)]


def _tables(tok_lists):
    s_order = np.concatenate([tok_lists[0][:1024], tok_lists[1][:1024], tok_lists[0][1024:], tok_lists[1][1024:]])
    tabs = []
    for r in range(2):
        k = tok_lists[r].astype(np.int64)
        ph = (np.outer(s_order.astype(np.int64), k) % SEQ).astype(np.float64) * (2 * np.pi / SEQ)
        cosm = np.cos(ph).reshape(SEQ, 8, 256)
        sinm = np.sin(ph).reshape(SEQ, 8, 256)
        tb = np.concatenate([cosm, sinm], axis=2).transpose(1, 0, 2)
        tabs.append(np.ascontiguousarray(tb).astype(ml_dtypes.bfloat16))
    return tabs


def make_in_maps(x, c, positions, w_ada, b_ada, w_in, q_norm, w_q_b, kv_norm, w_kv_b, w_fmix, w_out, ln_g, ln_b):
    tl = _token_lists()
    tabs = _tables(tl)
    cc, sc, invf2, ident = _host_consts()
    f32 = lambda a: np.ascontiguousarray(np.asarray(a, dtype=np.float32))
    shared = {
        "w_ada": f32(w_ada), "b_ada": f32(b_ada), "w_in": f32(w_in), "w_q_b": f32(w_q_b),
        "w_kv_b": f32(w_kv_b), "w_fmix": f32(w_fmix), "w_out": f32(w_out), "ln_g": f32(ln_g), "ln_b": f32(ln_b),
        "qn_t": f32(np.asarray(q_norm).reshape(DEPTH, 4, 128).transpose(0, 2, 1)),
        "kvn_t": f32(np.asarray(kv_norm).reshape(DEPTH, 2, 128).transpose(0, 2, 1)),
        "cc": cc, "sc": sc, "invf": invf2, "ident": ident,
    }
    if LITE:
        for k in ["w_ada", "b_ada", "w_in", "w_q_b", "w_kv_b", "w_fmix", "w_out", "ln_g", "ln_b", "qn_t", "kvn_t"]:
            shared[k] = np.ascontiguousarray(shared[k][:1])
        shared["w_ada"] = np.ascontiguousarray(shared["w_ada"][:1, :128, :512])
        tabs = [np.ascontiguousarray(t[:1]) for t in tabs]
    in_maps = []
    for core in range(8):
        b, r = core // 2, core % 2
        m = dict(shared)
        m["x"] = f32(np.asarray(x)[b][tl[r]])
        m["cvec"] = f32(np.asarray(c)[b].reshape(16, 128).T)
        m["pos"] = np.ascontiguousarray(np.asarray(positions)[b][tl[r]].astype(np.int32).reshape(1, T))
        m["tab"] = tabs[r]
        in_maps.append(m)
    return in_maps, tl


def kernel(x, c, positions, w_ada, b_ada, w_in, q_norm, w_q_b, kv_norm, w_kv_b, w_fmix, w_out, ln_g, ln_b):
    in_maps, tl = make_in_maps(x, c, positions, w_ada, b_ada, w_in, q_norm, w_q_b, kv_norm, w_kv_b,
                               w_fmix, w_out, ln_g, ln_b)
    if "full" not in _PROGRAM_CACHE:
        _PROGRAM_CACHE["full"] = build_program()
    nc = _PROGRAM_CACHE["full"]
    res = run_bass_kernel_spmd(nc, in_maps, core_ids=list(range(8)))
    outp = np.empty((4, SEQ, D), dtype=np.float32)
    for core in range(8):
        b, r = core // 2, core % 2
        outp[b][tl[r]] = res.results[core]["out"]
    return outp
```

```python
import math
from contextlib import ExitStack

import numpy as np
import ml_dtypes

import concourse.bass as bass
import concourse.mybir as mybir
from concourse.bass_utils import run_bass_kernel_spmd

F32 = mybir.dt.float32
BF16 = mybir.dt.bfloat16
I32 = mybir.dt.int32
AF = mybir.ActivationFunctionType
ALU = mybir.AluOpType

D = 2048
SEQ = 4096
T = 2048
NT = 16
DEPTH = 2
DIN = 3904
NH = 8
EPS = 1e-6
ALPHA = (2 * DEPTH) ** 0.25
SM_SCALE = 1.0 / math.sqrt(192.0)
FNORM = 1.0 / math.sqrt(4096.0 * 128.0)
PI = math.pi
C1 = 6.28125
C2 = 2.0 * math.pi - 6.28125

import os
NO_CC = bool(os.environ.get("NO_CC"))
LITE = bool(os.environ.get("LITE"))
WL = 1 if LITE else 2
NTAB = 1 if LITE else 8
ENGS = ["sync", "scalar", "vector", "gpsimd", "tensor"]
CENGS = ["scalar", "vector", "gpsimd", "tensor"]


class Sched:
    def __init__(self, nc, es):
        self.nc = nc
        self.es = es
        self.q = {e: [] for e in ENGS}
        self.sem = {}
        self.cnt = {}
        for e in CENGS:
            self.sem[e] = es.enter_context(nc.semaphore("s_" + e))
            self.cnt[e] = 0
        self.waited = {e: {} for e in ENGS}
        self.chans = []
        self.named = {}
        self.tmp_i = {}

    def chan(self, name):
        if name in self.named:
            return self.named[name]
        sem = self.es.enter_context(self.nc.semaphore(name))
        ch = [sem, 0]
        self.chans.append(ch)
        self.named[name] = ch
        return ch

    def tmpchan(self, eng="sync"):
        self.tmp_i[eng] = self.tmp_i.get(eng, 0) + 1
        return self.chan(f"tmp_{eng}{self.tmp_i[eng]}")

    def _waits(self, eng, deps):
        flat = []
        for d in deps:
            if d is None:
                continue
            if isinstance(d, list):
                flat.extend(x for x in d if x is not None)
            else:
                flat.append(d)
        for d in flat:
            sem, val = d
            key = id(sem)
            if self.waited[eng].get(key, 0) >= val:
                continue
            self.waited[eng][key] = val
            self.q[eng].append(lambda e, sem=sem, val=val: e.wait_ge(sem, val))

    def op(self, eng, fn, deps=(), ev=True):
        self._waits(eng, deps)
        if ev:
            self.cnt[eng] += 1
            sem = self.sem[eng]
            self.q[eng].append(lambda e, fn=fn, sem=sem: fn(e).then_inc(sem, 1))
            return (sem, self.cnt[eng])
        self.q[eng].append(lambda e, fn=fn: fn(e))
        return None

    def dma(self, eng, out, in_, chan, deps=()):
        self._waits(eng, deps)
        chan[1] += 16
        sem = chan[0]
        self.q[eng].append(lambda e, out=out, in_=in_, sem=sem: e.dma_start(out=out, in_=in_).then_inc(sem, 16))
        return (sem, chan[1])

    def raw(self, eng, fn, deps=()):
        self._waits(eng, deps)
        self.q[eng].append(fn)

    def wait(self, eng, deps):
        self._waits(eng, deps)

    def barrier(self, extra=()):
        evs = [(self.sem[e], self.cnt[e]) for e in CENGS if self.cnt[e] > 0]
        evs += [(c[0], c[1]) for c in self.chans if c[1] > 0]
        evs += list(extra)
        for e in ENGS:
            self._waits(e, evs)
        self.tmp_i = {}

    def flush(self):
        nc = self.nc
        with nc.Block() as block:
            for e in ENGS:
                if not self.q[e]:
                    continue

                def f(engobj, e=e):
                    for c in self.q[e]:
                        c(engobj)

                getattr(block, e)(f)
        self.q = {e: [] for e in ENGS}


class Ring:
    def __init__(self, items):
        self.items = list(items)
        self.free = [None] * len(self.items)
        self.i = 0

    def get(self):
        j = self.i % len(self.items)
        self.i += 1
        return j, self.items[j], self.free[j]

    def release(self, j, ev):
        self.free[j] = ev


def build_program(stop=None, debug=()):
    nc = bass.Bass("TRN2", target_bir_lowering=False)
    dt = nc.dram_tensor
    x_in = dt("x", [T, D], F32, kind="ExternalInput")
    cvec = dt("cvec", [128, 16], F32, kind="ExternalInput")
    pos = dt("pos", [1, T], I32, kind="ExternalInput")
    w_ada = dt("w_ada", [1, 128, 512] if LITE else [WL, D, 3 * D], F32, kind="ExternalInput")
    b_ada = dt("b_ada", [WL, 3 * D], F32, kind="ExternalInput")
    w_in = dt("w_in", [WL, D, DIN], F32, kind="ExternalInput")
    qn_t = dt("qn_t", [WL, 128, 4], F32, kind="ExternalInput")
    w_q_b = dt("w_q_b", [WL, 512, 1536], F32, kind="ExternalInput")
    kvn_t = dt("kvn_t", [WL, 128, 2], F32, kind="ExternalInput")
    w_kv_b = dt("w_kv_b", [WL, 256, 2048], F32, kind="ExternalInput")
    w_fmix = dt("w_fmix", [WL, 8, 128, 128], F32, kind="ExternalInput")
    w_out = dt("w_out", [WL, D, D], F32, kind="ExternalInput")
    ln_g = dt("ln_g", [WL, D], F32, kind="ExternalInput")
    ln_b = dt("ln_b", [WL, D], F32, kind="ExternalInput")
    tab = dt("tab", [NTAB, SEQ, 512], BF16, kind="ExternalInput")
    cc_in = dt("cc", [128, 2, 128], BF16, kind="ExternalInput")
    sc_in = dt("sc", [128, 2, 128], BF16, kind="ExternalInput")
    invf = dt("invf", [64, 1], F32, kind="ExternalInput")
    ident_in = dt("ident", [128, 128], BF16, kind="ExternalInput")
    out = dt("out", [T, D], F32, kind="ExternalOutput")

    modv = dt("modv", [DEPTH, 3 * D], F32)
    uTokA = [dt(f"uTokA{l}", [1024, 1024], BF16) for l in range(DEPTH)]
    uTokB = [dt(f"uTokB{l}", [1024, 1024], BF16) for l in range(DEPTH)]
    uAllA = [dt(f"uAllA{l}", [2048, 1024], BF16) for l in range(DEPTH)]
    uAllB = [dt(f"uAllB{l}", [2048, 1024], BF16) for l in range(DEPTH)]
    kvT = [dt(f"kvT{l}", [320, T], BF16) for l in range(DEPTH)]
    kvAll = [dt(f"kvAll{l}", [640, T], BF16) for l in range(DEPTH)]
    szT = [dt(f"szT{l}", [2048, T], BF16) for l in range(DEPTH)]
    qT = [dt(f"qT{l}", [NH, 192, T], BF16) for l in range(DEPTH)]
    yT = [dt(f"yT{l}", [2048, T], BF16) for l in range(DEPTH)]
    x1 = dt("x1", [T, D], F32)

    dbg = {}
    for name, shape, dty in debug:
        dbg[name] = dt("dbg_" + name, shape, dty, kind="ExternalOutput")

    PAIRS = [[0, 1]] if os.environ.get('SIM2') else [[0, 1], [2, 3], [4, 5], [6, 7]]

    with ExitStack() as es:
        S = Sched(nc, es)
        cc_sem = es.enter_context(nc.semaphore("cc_sem"))
        cc_cnt = [0]
        ps = es.enter_context(nc.psum_tensor("ps", [128, 8, 512], F32))
        cos2 = es.enter_context(nc.sbuf_tensor("sb_cos2", [64, T], F32))
        sin2 = es.enter_context(nc.sbuf_tensor("sb_sin2", [64, T], F32))
        ident = es.enter_context(nc.sbuf_tensor("sb_ident", [128, 128], BF16))
        ones_bf = es.enter_context(nc.sbuf_tensor("sb_ones_bf", [128, 128], BF16))
        ch_st = S.chan("ch_st")
        cact = es.enter_context(nc.sbuf_tensor("sb_cact", [128, 16], BF16))
        ch_wa = [S.chan("ch_wa0"), S.chan("ch_wa1")]
        ch_bc = [S.chan("ch_bc0"), S.chan("ch_bc1")]
        ch_mc = [S.chan("ch_mc0"), S.chan("ch_mc1")]
        mod_state = {"it": 0, "mfree": [None, None], "e_ca": None}

        def mod_load(l, c0, ncol, wa_ring):
            j, wt, fr = wa_ring.get()
            src = w_ada[l % WL, :, c0:c0 + ncol].rearrange("(kc p) n -> p kc n", p=128)
            e_w = S.dma("gpsimd", wt[:, :, 0:ncol], src, ch_wa[j], deps=[fr])
            return (l, c0, ncol, j, wt, e_w)

        def mod_chunk(l, c0, ncol, wa_ring, bank_ring, badac, modc):
            mod_compute(mod_load(l, c0, ncol, wa_ring), wa_ring, bank_ring, badac, modc)

        def mod_compute(hnd, wa_ring, bank_ring, badac, modc):
            l, c0, ncol, j, wt, e_w = hnd
            mfree = mod_state["mfree"]
            bj, bank, bfree = bank_ring.get()
            i2 = mod_state["it"] % 2
            mod_state["it"] += 1
            e_bc = S.dma("sync", badac[i2][:, 0:ncol], b_ada[l % WL:l % WL + 1, c0:c0 + ncol], ch_bc[i2], deps=[mfree[i2]])
            em = None
            for kc in range(16):
                em = S.op("tensor", lambda t, bank=bank, wt=wt, kc=kc: t.matmul(
                    ps[0:1, bank, 0:ncol], cact[:, kc:kc + 1], wt[:, kc, 0:ncol], start=(kc == 0), stop=(kc == 15)),
                    deps=[e_w, mod_state["e_ca"], bfree] if kc == 0 else [], ev=(kc == 15))
            wa_ring.release(j, em)
            ee = S.op("vector", lambda v, bank=bank, i2=i2: v.tensor_tensor(
                out=modc[i2][0:1, 0:ncol], in0=ps[0:1, bank, 0:ncol], in1=badac[i2][0:1, 0:ncol], op=ALU.add),
                deps=[em, e_bc, mfree[i2]])
            bank_ring.release(bj, ee)
            e_m = S.dma("sync", modv[l:l + 1, c0:c0 + ncol], modc[i2][:, 0:ncol], ch_mc[i2], deps=[ee])
            if "mod" in dbg:
                e_m = S.dma("sync", dbg["mod"][l:l + 1, c0:c0 + ncol], modc[i2][:, 0:ncol], ch_mc[i2], deps=[ee])
            mfree[i2] = e_m

        def collective(src, dst, deps):
            S.wait("gpsimd", deps)
            if NO_CC:
                return None
            cc_cnt[0] += 1

            def f(g, src=src, dst=dst):
                g.collective_compute("AllGather", ALU.bypass, replica_groups=PAIRS,
                                     ins=[src.ap().opt()], outs=[dst.ap().opt()]).then_inc(cc_sem)
            S.q["gpsimd"].append(f)
            return (cc_sem, cc_cnt[0])

        with ExitStack() as p0:
            sb = lambda name, shape, dty: p0.enter_context(nc.sbuf_tensor("p0_" + name, shape, dty))
            cv = sb("cv", [128, 16], F32)
            badac = [sb(f"badac{i}", [1, 512], F32) for i in range(2)]
            modc = [sb(f"modc{i}", [1, 512], F32) for i in range(2)]
            wa = [sb(f"wa{i}", [128, 16, 512], BF16) for i in range(2)]
            posi = sb("posi", [64, T], I32)
            posf = sb("posf", [64, T], F32)
            ang = sb("ang", [64, T], F32)
            ki = sb("ki", [64, T], I32)
            kf = sb("kf", [64, T], F32)
            rr = sb("rr", [64, T], F32)
            tmpa = sb("tmpa", [64, T], F32)
            tmpb = sb("tmpb", [64, T], F32)
            invf_sb = sb("invf_sb", [64, 1], F32)

            e_id = S.dma("sync", ident[:], ident_in[:, :], S.tmpchan())
            e_cv = S.dma("sync", cv[:], cvec[:, :], S.tmpchan())
            e_ps = S.dma("sync", posi[:], pos.ap().partition_broadcast(64), S.tmpchan())
            e_if = S.dma("sync", invf_sb[:], invf[:, :], S.tmpchan())
            e_c0 = S.op("vector", lambda v: v.memset(ones_bf[:], 1.0))
            e_ca = S.op("scalar", lambda a: a.activation(out=cact[:], in_=cv[:], func=AF.Silu), deps=[e_cv])

            e = S.op("vector", lambda v: v.tensor_copy(out=posf[:], in_=posi[:]), deps=[e_ps])
            e = S.op("vector", lambda v: v.tensor_scalar(out=ang[:], in0=posf[:], scalar1=invf_sb[:, 0:1], scalar2=None, op0=ALU.mult), deps=[e, e_if])
            e = S.op("vector", lambda v: v.tensor_scalar(out=ki[:], in0=ang[:], scalar1=1.0 / (2 * PI), scalar2=None, op0=ALU.mult), deps=[e])
            e = S.op("vector", lambda v: v.tensor_copy(out=kf[:], in_=ki[:]), deps=[e])
            e = S.op("vector", lambda v: v.scalar_tensor_tensor(out=tmpa[:], in0=kf[:], scalar=-C1, in1=ang[:], op0=ALU.mult, op1=ALU.add), deps=[e])
            e_r = S.op("vector", lambda v: v.scalar_tensor_tensor(out=rr[:], in0=kf[:], scalar=-C2, in1=tmpa[:], op0=ALU.mult, op1=ALU.add), deps=[e])

            def wrap_sin(src_ap, dst_tab, shift, dep):
                e = S.op("vector", lambda v: v.tensor_scalar(out=tmpa[:], in0=src_ap, scalar1=shift, scalar2=None, op0=ALU.add), deps=[dep])
                e = S.op("vector", lambda v: v.tensor_scalar(out=tmpb[:], in0=tmpa[:], scalar1=PI, scalar2=2 * PI, op0=ALU.is_gt, op1=ALU.mult), deps=[e])
                e = S.op("vector", lambda v: v.tensor_tensor(out=tmpa[:], in0=tmpa[:], in1=tmpb[:], op=ALU.subtract), deps=[e])
                e = S.op("vector", lambda v: v.tensor_scalar(out=tmpb[:], in0=tmpa[:], scalar1=-PI, scalar2=2 * PI, op0=ALU.is_lt, op1=ALU.mult), deps=[e])
                e = S.op("vector", lambda v: v.tensor_tensor(out=tmpa[:], in0=tmpa[:], in1=tmpb[:], op=ALU.add), deps=[e])
                e = S.op("vector", lambda v: v.tensor_scalar(out=tmpa[:], in0=tmpa[:], scalar1=PI, scalar2=-PI, op0=ALU.min, op1=ALU.max), deps=[e])
                e = S.op("scalar", lambda a: a.activation(out=dst_tab, in_=tmpa[:], func=AF.Sin), deps=[e])
                return e

            e_s = wrap_sin(rr[:], sin2[:], 0.0, e_r)
            e_c = wrap_sin(rr[:], cos2[:], PI / 2, e_s)

            wa_ring = Ring(wa)
            bank_ring = Ring([0, 1])
            mod_state["e_ca"] = e_ca
            if LITE:
                e_z = S.op("vector", lambda v: v.memset(modc[0][:], 0.1))
                for l in range(DEPTH):
                    for nb in range(12):
                        S.dma("sync", modv[l:l + 1, nb * 512:(nb + 1) * 512], modc[0][:], ch_st, deps=[e_z])
            else:
                for nb in range(12 if "mod" in dbg else 8):
                    mod_chunk(0, nb * 512, 512, wa_ring, bank_ring, badac, modc)
                if "mod" in dbg:
                    for nb in range(12):
                        mod_chunk(1, nb * 512, 512, wa_ring, bank_ring, badac, modc)
            if "rope" in dbg:
                S.dma("sync", dbg["rope"][0], cos2[:], ch_st, deps=[e_c])
                S.dma("sync", dbg["rope"][1], sin2[:], ch_st, deps=[e_c])
            S.barrier()
            S.flush()

        x_src = x_in
        for l in range(DEPTH):
            lw = l % WL
            if stop == "p0":
                break
            x_dst = x1 if l == 0 else out
            with ExitStack() as pa:
                sbA = lambda name, shape, dty: pa.enter_context(nc.sbuf_tensor(f"pa{l}_" + name, shape, dty))
                hT = sbA("hT", [128, 16, T], BF16)
                with ExitStack() as pa1:
                    sb = lambda name, shape, dty: pa1.enter_context(nc.sbuf_tensor(f"pa1{l}_" + name, shape, dty))
                    xt = [sb(f"xt{i}", [128, D], F32) for i in range(2)]
                    hb = [sb(f"hb{i}", [128, D], BF16) for i in range(2)]
                    sc_rep = sb("sc_rep", [128, D], F32)
                    sh_rep = sb("sh_rep", [128, D], F32)
                    st = [sb(f"st{i}", [128, 4, 6], F32) for i in range(2)]
                    mv = [sb(f"mv{i}", [128, 2], F32) for i in range(2)]
                    sm = [sb(f"sm{i}", [128, 4], F32) for i in range(2)]
                    e_sh = S.dma("sync", sh_rep[:], modv[l:l + 1, 0:D].partition_broadcast(128), S.tmpchan())
                    e_sc = S.dma("sync", sc_rep[:], modv[l:l + 1, D:2 * D].partition_broadcast(128), S.tmpchan())
                    e_sc = S.op("vector", lambda v: v.tensor_scalar(out=sc_rep[:], in0=sc_rep[:], scalar1=1.0, scalar2=None, op0=ALU.add), deps=[e_sc])
                    ch_x = [S.chan(f"ch_x{i}") for i in range(2)]
                    xfree = [None, None]
                    hfree = [None, None]
                    tp_ring = Ring([0, 1, 2, 3])
                    for t in range(NT):
                        i = t % 2
                        e_x = S.dma("sync", xt[i][:], x_src[t * 128:(t + 1) * 128, :], ch_x[i], deps=[xfree[i]])
                        e_st = None
                        for k in range(4):
                            e_st = S.op("vector", lambda v, i=i, k=k: v.bn_stats(out=st[i][:, k, :], in_=xt[i][:, k * 512:(k + 1) * 512]),
                                        deps=[e_x], ev=(k == 3))
                        e_ag = S.op("vector", lambda v, i=i: v.bn_aggr(out=mv[i][:], in_=st[i][:].rearrange("p a b -> p (a b)")), deps=[e_st])
                        e1 = S.op("vector", lambda v, i=i: v.tensor_scalar(out=sm[i][:, 0:1], in0=mv[i][:, 1:2], scalar1=EPS, scalar2=None, op0=ALU.add), deps=[e_ag])
                        e2 = S.op("scalar", lambda a, i=i: a.activation(out=sm[i][:, 1:2], in_=sm[i][:, 0:1], func=AF.Sqrt), deps=[e1])
                        e3 = S.op("vector", lambda v, i=i: v.reciprocal(out=sm[i][:, 2:3], in_=sm[i][:, 1:2]), deps=[e2])
                        e4 = S.op("vector", lambda v, i=i: v.tensor_scalar(out=sm[i][:, 3:4], in0=mv[i][:, 0:1], scalar1=sm[i][:, 2:3], scalar2=-1.0, op0=ALU.mult, op1=ALU.mult), deps=[e3])
                        e5 = S.op("scalar", lambda a, i=i: a.activation(out=xt[i][:], in_=xt[i][:], func=AF.Identity, bias=sm[i][:, 3:4], scale=sm[i][:, 2:3]), deps=[e4])
                        e6 = S.op("vector", lambda v, i=i: v.tensor_tensor(out=xt[i][:], in0=xt[i][:], in1=sc_rep[:], op=ALU.mult), deps=[e5, e_sc])
                        e7 = S.op("vector", lambda v, i=i: v.tensor_tensor(out=hb[i][:], in0=xt[i][:], in1=sh_rep[:], op=ALU.add), deps=[e6, e_sh, hfree[i]])
                        xfree[i] = e7
                        e_tp = None
                        for g in range(4):
                            bj, bank, bfree = tp_ring.get()
                            pview = ps[:, bank, :].bitcast(BF16)
                            for jj in range(4):
                                kc = 4 * g + jj
                                e_tp = S.op("tensor", lambda tt, pview=pview, i=i, kc=kc, jj=jj: tt.transpose(
                                    pview[:, jj * 128:(jj + 1) * 128], hb[i][:, kc * 128:(kc + 1) * 128], ident[:]),
                                    deps=[e7, bfree, e_id] if jj == 0 else [], ev=(jj == 3))
                            e_ev = S.op("scalar", lambda a, pview=pview, g=g, t=t: a.copy(
                                out=hT[:, 4 * g:4 * g + 4, t * 128:(t + 1) * 128],
                                in_=pview[:, 0:512].rearrange("p (a b) -> p a b", a=4)), deps=[e_tp])
                            tp_ring.release(bj, e_ev)
                        hfree[i] = e_tp
                    if f"hT{l}" in dbg:
                        S.barrier()
                        S.dma("sync", dbg[f"hT{l}"].ap().rearrange("(kc p) t -> p kc t", p=128), hT[:], ch_st)
                    S.barrier()
                    S.flush()
                if stop == f"a1_{l}":
                    break
                with ExitStack() as pa2:
                    sb = lambda name, shape, dty: pa2.enter_context(nc.sbuf_tensor(f"pa2{l}_" + name, shape, dty))
                    wbuf = [sb(f"wbuf{i}", [128, 16, 512], BF16) for i in range(2)]
                    cqT = sb("cqT", [128, 4, T], BF16)
                    ckv_raw = sb("ckv_raw", [128, 2, T], F32)
                    rq_rep = sb("rq_rep", [128, T], F32)
                    rkv_rep = sb("rkv_rep", [128, T], F32)
                    sq = [sb(f"sq{i}", [128, 512], BF16) for i in range(2)]
                    evb = [sb(f"evb{i}", [128, 512], BF16) for i in range(4)]
                    tf = [sb(f"tf{i}", [128, 512], F32) for i in range(3)]
                    wq = sb("wq", [128, 4, 1536], BF16)
                    wqrot = sb("wqrot", [128, 4, 8, 64], BF16)
                    wkrot = sb("wkrot", [128, 16, 64], BF16)
                    gq = sb("gq", [128, 4], F32)
                    gkv = sb("gkv", [128, 2], F32)

                    e_gq = e_gkv = e_wq = None
                    if not os.environ.get("NO_G"):
                        e_gq = S.dma("sync", gq[:], qn_t[lw], S.tmpchan())
                        e_gkv = S.dma("sync", gkv[:], kvn_t[lw], S.tmpchan())
                    if not os.environ.get("NO_WQ"):
                        e_wq = S.dma("gpsimd", wq[:], w_q_b[lw].rearrange("(kc p) n -> p kc n", p=128), S.tmpchan("gpsimd"))

                    w_ring = Ring(wbuf)
                    ch_w = [S.chan("ch_w0"), S.chan("ch_w1")]
                    ev_ring = Ring(evb)
                    ch_ev = [S.chan(f"ch_ev{i}") for i in range(4)]
                    bank_ring = Ring([0, 1, 2, 3])
                    sq_ring = Ring(sq)
                    tf_ring = Ring(tf)

                    GROUPS = [("kv", 2560, 320), ("u", 0, 512), ("u", 512, 512), ("cq", 2048, 512),
                              ("z", 1024, 512), ("z", 1536, 512), ("z", 2880, 512), ("z", 3392, 512)]
                    SZROW = {1024: 0, 1536: 512, 2880: 1024, 3392: 1536}
                    loaded = {}

                    def load_group(gi):
                        kind, c0, ncol = GROUPS[gi]
                        j, wt, fr = w_ring.get()
                        src = w_in[lw, :, c0:c0 + ncol].rearrange("(kc p) n -> p kc n", p=128)
                        e = S.dma("gpsimd", wt[:, :, 0:ncol], src, ch_w[j], deps=[fr])
                        loaded[gi] = (j, wt, e)

                    def store(dst, j, ev_t, e_ev, npart=128, eng="gpsimd"):
                        if os.environ.get("NO_STORE"):
                            ev_ring.release(j, e_ev)
                            return e_ev
                        eng = os.environ.get("STORE_ENG", eng)
                        e_st = S.dma(eng, dst, ev_t[0:npart, :], ch_ev[j], deps=[e_ev])
                        ev_ring.release(j, e_st)
                        return e_st

                    def mm_group(bank, lhs_fn, rhs_fn, nk, deps, mpart=128, ncols=512):
                        em = None
                        for kc in range(nk):
                            em = S.op("tensor", lambda t, bank=bank, kc=kc: t.matmul(
                                ps[0:mpart, bank, 0:ncols], lhs_fn(kc), rhs_fn(kc), start=(kc == 0), stop=(kc == nk - 1)),
                                deps=deps if kc == 0 else [], ev=(kc == nk - 1))
                        return em

                    def rms_chunks(wt, e_w, nchunk, gvec, e_g, raw_dst, r_rep, dim, tb, ssq_bank, last_mm):
                        tsl = slice(tb * 512, (tb + 1) * 512)
                        e_ss = None
                        for c in range(nchunk):
                            bj, bank, bfree = bank_ring.get()
                            em = mm_group(bank, lambda kc, c=c: wt[:, kc, c * 128:(c + 1) * 128], lambda kc: hT[:, kc, tsl], 16, [e_w, bfree])
                            sj, sqt, sfree = sq_ring.get()
                            if os.environ.get("NO_SQ"):
                                e_sq = em
                            else:
                                e_sq = S.op("scalar", lambda a, bank=bank, sqt=sqt: a.activation(out=sqt[:], in_=ps[:, bank, :], func=AF.Square), deps=[em, sfree])
                            if os.environ.get("NO_RAW"):
                                e_raw = em
                            else:
                                rm = os.environ.get("RAWMODE", "")
                                if rm == "const":
                                    e_raw = S.op("vector", lambda v, bank=bank, c=c: v.tensor_scalar(
                                        out=raw_dst[:, c, tsl], in0=ps[:, bank, :], scalar1=1.0, scalar2=None, op0=ALU.mult), deps=[em, e_g])
                                elif rm == "nodep":
                                    e_raw = S.op("vector", lambda v, bank=bank, c=c: v.tensor_scalar(
                                        out=raw_dst[:, c, tsl], in0=ps[:, bank, :], scalar1=gvec[:, c:c + 1], scalar2=None, op0=ALU.mult), deps=[em])
                                elif rm == "act":
                                    e_raw = S.op("scalar", lambda a, bank=bank, c=c: a.activation(
                                        out=raw_dst[:, c, tsl], in_=ps[:, bank, :], func=AF.Copy, scale=gvec[:, c:c + 1]), deps=[em, e_g])
                                else:
                                    e_raw = S.op("vector", lambda v, bank=bank, c=c: v.tensor_scalar(
                                        out=raw_dst[:, c, tsl], in0=ps[:, bank, :], scalar1=gvec[:, c:c + 1], scalar2=None, op0=ALU.mult), deps=[em, e_g, e_sq])
                            bank_ring.release(bj, [e_sq, e_raw])
                            if os.environ.get("NO_SSQ"):
                                e_ss = e_sq
                            else:
                                e_ss = S.op("tensor", lambda t, sqt=sqt, c=c: t.matmul(
                                    ps[:, ssq_bank, :], ones_bf[:], sqt[:], start=(c == 0), stop=(c == nchunk - 1)),
                                    deps=[e_sq, ssq_free[ssq_bank]] if c == 0 else [e_sq])
                            sq_ring.release(sj, e_ss)
                            last_mm[0] = em
                        if os.environ.get("NO_RTAIL"):
                            return e_ss, e_raw
                        tj, tft, tfree = tf_ring.get()
                        e1 = S.op("vector", lambda v, tft=tft: v.tensor_scalar(out=tft[:], in0=ps[:, ssq_bank, :], scalar1=1.0 / dim, scalar2=EPS, op0=ALU.mult, op1=ALU.add), deps=[e_ss, tfree])
                        ssq_free[ssq_bank] = e1
                        e2 = S.op("scalar", lambda a, tft=tft: a.activation(out=tft[:], in_=tft[:], func=AF.Sqrt), deps=[e1])
                        e3 = S.op("vector", lambda v, tft=tft: v.reciprocal(out=r_rep[:, tsl], in_=tft[:]), deps=[e2])
                        tf_ring.release(tj, e3)
                        return e3, e_raw

                    def rope_pair(bA, bB, e_mm, r_rep, e_r, tb, dst):
                        tsl = slice(tb * 512, (tb + 1) * 512)
                        t1j, t1, f1 = tf_ring.get()
                        t2j, t2, f2 = tf_ring.get()
                        ea = S.op("vector", lambda v: v.tensor_tensor(out=t1[0:64, :], in0=ps[0:64, bA[1], :], in1=cos2[:, tsl], op=ALU.mult), deps=[e_mm, f1])
                        eb = S.op("vector", lambda v: v.tensor_tensor(out=t2[0:64, :], in0=ps[0:64, bB[1], :], in1=sin2[:, tsl], op=ALU.mult), deps=[e_mm, f2])
                        bank_ring.release(bA[0], ea)
                        bank_ring.release(bB[0], eb)
                        j, ev_t, fr = ev_ring.get()
                        if r_rep is None:
                            ec = S.op("vector", lambda v: v.tensor_tensor(out=ev_t[0:64, :], in0=t1[0:64, :], in1=t2[0:64, :], op=ALU.add), deps=[ea, eb, fr])
                        else:
                            ec0 = S.op("vector", lambda v: v.tensor_tensor(out=t1[0:64, :], in0=t1[0:64, :], in1=t2[0:64, :], op=ALU.add), deps=[ea, eb])
                            ec = S.op("vector", lambda v: v.tensor_tensor(out=ev_t[0:64, :], in0=t1[0:64, :], in1=r_rep[0:64, tsl], op=ALU.mult), deps=[ec0, e_r, fr])
                        tf_ring.release(t1j, ec)
                        tf_ring.release(t2j, ec)
                        return store(dst, j, ev_t, ec, npart=64)

                    ssq_free = {4: None, 5: None}
                    kv_stores, u_stores = [], []
                    load_group(0)
                    A2N = int(os.environ.get('A2N', '99'))
                    GROUPS = GROUPS[:A2N]
                    for gi, (kind, c0, ncol) in enumerate(GROUPS):
                        if gi + 1 < len(GROUPS):
                            load_group(gi + 1)
                        j_w, wt, e_w = loaded[gi]
                        last_mm = [None]
                        if kind == "kv":
                            if os.environ.get("NO_WKROT"):
                                continue
                            e_r1 = S.op("vector", lambda v, wt=wt: v.tensor_scalar(out=wkrot[:, :, 0:32], in0=wt[:, :, 288:320], scalar1=-1.0, scalar2=None, op0=ALU.mult), deps=[e_w])
                            e_r2 = S.op("vector", lambda v, wt=wt: v.tensor_copy(out=wkrot[:, :, 32:64], in_=wt[:, :, 256:288]), deps=[e_w])
                            for tb in range(int(os.environ.get("KV_TB", "4"))):
                                tsl = slice(tb * 512, (tb + 1) * 512)
                                e_r, e_raw = rms_chunks(wt, e_w, 2, gkv, e_gkv, ckv_raw, rkv_rep, 256.0, tb, 4 + tb % 2, last_mm)
                                for c in range(2):
                                    j, ev_t, fr = ev_ring.get()
                                    e_n = S.op("vector", lambda v, c=c, ev_t=ev_t, tsl=tsl: v.tensor_tensor(
                                        out=ev_t[:], in0=ckv_raw[:, c, tsl], in1=rkv_rep[:, tsl], op=ALU.mult), deps=[e_r, e_raw, fr])
                                    kv_stores.append(store(kvT[l][c * 128:(c + 1) * 128, tsl], j, ev_t, e_n))
                                if os.environ.get("NO_KR"):
                                    continue
                                bA = bank_ring.get()
                                emA = mm_group(bA[1], lambda kc, wt=wt: wt[:, kc, 256:320], lambda kc, tsl=tsl: hT[:, kc, tsl], 16, [e_w, bA[2]], mpart=64)
                                bB = bank_ring.get()
                                emB = mm_group(bB[1], lambda kc: wkrot[:, kc, :], lambda kc, tsl=tsl: hT[:, kc, tsl], 16, [e_r1, e_r2, bB[2]], mpart=64)
                                last_mm[0] = emB
                                kv_stores.append(rope_pair(bA, bB, [emA, emB], None, None, tb, kvT[l][256:320, tsl]))
                            e_cc_kv = collective(kvT[l], kvAll[l], kv_stores)
                        elif kind == "u":
                            for t in range(NT):
                                bj, bank, bfree = bank_ring.get()
                                em = mm_group(bank, lambda kc, t=t: hT[:, kc, t * 128:(t + 1) * 128], lambda kc, wt=wt: wt[:, kc, 0:512], 16, [e_w, bfree])
                                last_mm[0] = em
                                j, ev_t, fr = ev_ring.get()
                                e_ev = S.op("scalar", lambda a, bank=bank, ev_t=ev_t: a.copy(out=ev_t[:], in_=ps[:, bank, :]), deps=[em, fr])
                                bank_ring.release(bj, e_ev)
                                dstT = uTokA[l] if t < 8 else uTokB[l]
                                tt = t % 8
                                u_stores.append(store(dstT[tt * 128:(tt + 1) * 128, c0:c0 + 512], j, ev_t, e_ev))
                            if c0 == 512:
                                e_cc_uA = collective(uTokA[l], uAllA[l], u_stores)
                                e_cc_uB = collective(uTokB[l], uAllB[l], u_stores)
                        elif kind == "cq":
                            for tb in range(4):
                                e_rq, e_cq = rms_chunks(wt, e_w, 4, gq, e_gq, cqT, rq_rep, 512.0, tb, 4 + tb % 2, last_mm)
                            e_rq_all, e_cq_all = e_rq, e_cq
                        else:
                            row0 = SZROW[c0]
                            for c in range(4):
                                for tb in range(4):
                                    tsl = slice(tb * 512, (tb + 1) * 512)
                                    bj, bank, bfree = bank_ring.get()
                                    em = mm_group(bank, lambda kc, c=c, wt=wt: wt[:, kc, c * 128:(c + 1) * 128], lambda kc, tsl=tsl: hT[:, kc, tsl], 16, [e_w, bfree])
                                    last_mm[0] = em
                                    j, ev_t, fr = ev_ring.get()
                                    e_ev = S.op("scalar", lambda a, bank=bank, ev_t=ev_t: a.activation(out=ev_t[:], in_=ps[:, bank, :], func=AF.Silu), deps=[em, fr])
                                    bank_ring.release(bj, e_ev)
                                    store(szT[l][row0 + c * 128:row0 + (c + 1) * 128, tsl], j, ev_t, e_ev)
                        w_ring.release(j_w, last_mm[0])

                        if kind == "cq" and not os.environ.get('NO_A3'):
                            e_rots = []
                            for kc in range(4):
                                wqv = wq[:, kc, :].rearrange("p (h d) -> p h d", h=8)
                                e_rots.append(S.op("vector", lambda v, kc=kc, wqv=wqv: v.tensor_scalar(out=wqrot[:, kc, :, 0:32], in0=wqv[:, :, 160:192], scalar1=-1.0, scalar2=None, op0=ALU.mult), deps=[e_wq]))
                                e_rots.append(S.op("vector", lambda v, kc=kc, wqv=wqv: v.tensor_copy(out=wqrot[:, kc, :, 32:64], in_=wqv[:, :, 128:160]), deps=[e_wq]))
                            for h in range(NH):
                                for tb in range(4):
                                    tsl = slice(tb * 512, (tb + 1) * 512)
                                    bj, bank, bfree = bank_ring.get()
                                    em = mm_group(bank, lambda kc, h=h: wq[:, kc, 192 * h:192 * h + 128], lambda kc, tsl=tsl: cqT[:, kc, tsl], 4, [e_wq, e_cq_all, bfree])
                                    j, ev_t, fr = ev_ring.get()
                                    e_ev = S.op("vector", lambda v, bank=bank, ev_t=ev_t, tsl=tsl: v.tensor_tensor(
                                        out=ev_t[:], in0=ps[:, bank, :], in1=rq_rep[:, tsl], op=ALU.mult), deps=[em, e_rq_all, fr])
                                    bank_ring.release(bj, e_ev)
                                    store(qT[l][h, 0:128, tsl], j, ev_t, e_ev)
                                    bA = bank_ring.get()
                                    emA = mm_group(bA[1], lambda kc, h=h: wq[:, kc, 192 * h + 128:192 * h + 192], lambda kc, tsl=tsl: cqT[:, kc, tsl], 4, [e_wq, e_cq_all, bA[2]], mpart=64)
                                    bB = bank_ring.get()
                                    emB = mm_group(bB[1], lambda kc, h=h: wqrot[:, kc, h, :], lambda kc, tsl=tsl: cqT[:, kc, tsl], 4, e_rots + [e_cq_all, bB[2]], mpart=64)
                                    rope_pair(bA, bB, [emA, emB], rq_rep, e_rq_all, tb, qT[l][h, 128:192, tsl])

                    S.barrier(extra=[(cc_sem, cc_cnt[0])])
                    if f"uTok{l}" in dbg:
                        S.dma("sync", dbg[f"uTok{l}"][0:1024, :], uTokA[l][:, :], ch_st)
                        S.dma("sync", dbg[f"uTok{l}"][1024:2048, :], uTokB[l][:, :], ch_st)
                        S.dma("sync", dbg[f"kvT{l}"][:, :], kvT[l][:, :], ch_st)
                        S.dma("sync", dbg[f"szT{l}"][:, :], szT[l][:, :], ch_st)
                        S.dma("sync", dbg[f"qT{l}"].ap().rearrange("h d t -> (h d) t"), qT[l].ap().rearrange("h d t -> (h d) t"), ch_st)
                        S.barrier()
                    S.flush()
            if stop == f"a_{l}":
                break
            with ExitStack() as pb1:
                sb = lambda name, shape, dty: pb1.enter_context(nc.sbuf_tensor(f"pb1{l}_" + name, shape, dty))
                u = sb("u", [128, 32, 1024], BF16)
                tabb = [sb(f"tabb{i}", [128, 32, 512], BF16) for i in range(2)]
                ccs = sb("ccs", [128, 2, 128], BF16)
                scs = sb("scs", [128, 2, 128], BF16)
                wf = sb("wf", [128, 8, 128], F32)
                wfh = sb("wfh", [128, 8, 128], BF16)
                wfl = sb("wfl", [128, 8, 128], BF16)
                G1s = sb("G1s", [128, 8, 128], BF16)
                G2s = sb("G2s", [128, 8, 128], BF16)
                XT = [sb(f"XT{i}", [128, 512], BF16) for i in range(2)]
                szt = [sb(f"szt{i}", [128, 256], BF16) for i in range(3)]
                yev = [sb(f"yev{i}", [128, 256], BF16) for i in range(3)]

                e_uA = S.dma("sync", u[:, 0:16, :], uAllA[l].ap().rearrange("(st p) c -> p st c", p=128), S.tmpchan())
                e_uB = S.dma("sync", u[:, 16:32, :], uAllB[l].ap().rearrange("(st p) c -> p st c", p=128), S.tmpchan())
                e_cc = S.dma("sync", ccs[:], cc_in[:, :, :], S.tmpchan())
                e_sc2 = S.dma("sync", scs[:], sc_in[:, :, :], S.tmpchan())
                e_wf0 = S.dma("sync", wf[:], w_fmix[lw].rearrange("g m d -> m g d"), S.tmpchan())
                e_wfh = S.op("vector", lambda v: v.tensor_copy(out=wfh[:], in_=wf[:]), deps=[e_wf0])
                e_wfl0 = S.op("vector", lambda v: v.tensor_tensor(out=wf[:], in0=wf[:], in1=wfh[:], op=ALU.subtract), deps=[e_wfh])
                e_wf = S.op("vector", lambda v: v.tensor_copy(out=wfl[:], in_=wf[:]), deps=[e_wfl0])
                ch_tab = [S.chan("ch_tab0"), S.chan("ch_tab1")]
                tabfree = [None, None]

                def load_tab(kb):
                    i = kb % 2
                    srcv = tab[kb % NTAB].rearrange("(st p) n -> p st n", p=128)
                    S.dma("sync", tabb[i][:, 0:16, :], srcv[:, 0:16, :], ch_tab[i], deps=[tabfree[i]])
                    return S.dma("sync", tabb[i][:, 16:32, :], srcv[:, 16:32, :], ch_tab[i], deps=[tabfree[i]])

                gbank = Ring([6, 7])
                e_G = []
                for g in range(8):
                    for (tabm, dstG, scl) in ((ccs, G1s, FNORM), (scs, G2s, -FNORM)):
                        bj, bank, bfree = gbank.get()
                        S.op("tensor", lambda t, bank=bank, tabm=tabm, g=g: t.matmul(ps[:, bank, 0:128], tabm[:, 0, :], wfh[:, g, :], start=True, stop=False),
                             deps=[e_cc, e_sc2, e_wf, bfree], ev=False)
                        S.op("tensor", lambda t, bank=bank, tabm=tabm, g=g: t.matmul(ps[:, bank, 0:128], tabm[:, 0, :], wfl[:, g, :], start=False, stop=False), ev=False)
                        em = S.op("tensor", lambda t, bank=bank, tabm=tabm, g=g: t.matmul(ps[:, bank, 0:128], tabm[:, 1, :], wfh[:, g, :], start=False, stop=True))
                        ee = S.op("scalar", lambda a, bank=bank, dstG=dstG, g=g, scl=scl: a.activation(out=dstG[:, g, :], in_=ps[:, bank, 0:128], func=AF.Copy, scale=scl), deps=[em])
                        gbank.release(bj, ee)
                        e_G.append(ee)

                mbank = Ring([0, 1, 2])
                cbank = Ring([4, 5])
                xt_ring = Ring(XT)
                szt_ring = Ring(szt)
                yev_ring = Ring(yev)
                ch_szt = [S.chan(f"ch_szt{i}") for i in range(3)]
                ch_yev = [S.chan(f"ch_yev{i}") for i in range(3)]

                def channel_stage(kb, g, xj, xtt, e_x, sj, sztt, e_sz):
                    bj, bank, bfree = cbank.get()
                    S.op("tensor", lambda t: t.matmul(ps[:, bank, 0:256], G1s[:, g, :], xtt[:, 0:256], start=True, stop=False),
                         deps=[e_x, bfree] + e_G, ev=False)
                    em = S.op("tensor", lambda t: t.matmul(ps[:, bank, 0:256], G2s[:, g, :], xtt[:, 256:512], start=False, stop=True))
                    xt_ring.release(xj, em)
                    yj, yt, yfree = yev_ring.get()
                    ee = S.op("vector", lambda v: v.tensor_tensor(out=yt[:], in0=ps[:, bank, 0:256], in1=sztt[:], op=ALU.mult), deps=[em, e_sz, yfree])
                    cbank.release(bj, ee)
                    szt_ring.release(sj, ee)
                    e_st = S.dma("gpsimd", yT[l][g * 128:(g + 1) * 128, kb * 256:(kb + 1) * 256], yt[:], ch_yev[yj], deps=[ee])
                    yev_ring.release(yj, e_st)

                pending = None
                e_tab = load_tab(0)
                for kb in range(8):
                    i = kb % 2
                    e_tab_cur = e_tab
                    if kb + 1 < 8:
                        e_tab = load_tab(kb + 1)
                    em = None
                    for g in range(8):
                        sj, sztt, sfree = szt_ring.get()
                        e_sz = S.dma("sync", sztt[:], szT[l][g * 128:(g + 1) * 128, kb * 256:(kb + 1) * 256], ch_szt[sj], deps=[sfree])
                        bj, bank, bfree = mbank.get()
                        for st_ in range(32):
                            em = S.op("tensor", lambda t, bank=bank, st_=st_, g=g, i=i: t.matmul(
                                ps[:, bank, :], u[:, st_, g * 128:(g + 1) * 128], tabb[i][:, st_, :], start=(st_ == 0), stop=(st_ == 31)),
                                deps=[e_uA, e_uB, e_tab_cur, bfree] if st_ == 0 else [], ev=(st_ == 31))
                        xj, xtt, xfree = xt_ring.get()
                        e_x = S.op("scalar", lambda a, bank=bank, xtt=xtt: a.copy(out=xtt[:], in_=ps[:, bank, :]), deps=[em, xfree])
                        mbank.release(bj, e_x)
                        if pending is not None:
                            channel_stage(*pending)
                        pending = (kb, g, xj, xtt, e_x, sj, sztt, e_sz)
                    tabfree[i] = em
                channel_stage(*pending)
                S.barrier()
                S.flush()
            if stop == f"b1_{l}":
                if f"yT{l}" in dbg:
                    S.dma("sync", dbg[f"yT{l}"][0:1024, :], yT[l][0:1024, :], ch_st)
                break
            pbw = ExitStack()
            wo = pbw.enter_context(nc.sbuf_tensor(f"pbw{l}_wo", [128, 16, D], BF16))
            e_wo = []
            for cb in range(4):
                e_wo.append(S.dma("gpsimd", wo[:, :, cb * 512:(cb + 1) * 512], w_out[lw, :, cb * 512:(cb + 1) * 512].rearrange("(kc p) n -> p kc n", p=128), S.chan(f"ch_wo{cb}")))
            with ExitStack() as pb2:
                sb = lambda name, shape, dty: pb2.enter_context(nc.sbuf_tensor(f"pb2{l}_" + name, shape, dty))
                do_mod1 = (l == 0 and not LITE and "mod" not in dbg)
                if do_mod1:
                    wa2 = [sb(f"wa2_{i}", [128, 16, 256], BF16) for i in range(2)]
                    badac2 = [sb(f"badac2_{i}", [1, 512], F32) for i in range(2)]
                    modc2 = [sb(f"modc2_{i}", [1, 512], F32) for i in range(2)]
                    wa2_ring = Ring(wa2)
                    mod_pend = [None]
                ckvT = sb("ckvT", [128, 2, SEQ], BF16)
                krT = sb("krT", [128, SEQ], BF16)
                wkv = sb("wkv", [128, 2, 2048], BF16)
                kT = [sb(f"kT{i}", [128, SEQ], BF16) for i in range(2)]
                vS = [sb(f"vS{i}", [128, 32, 128], BF16) for i in range(2)]
                qn = [sb(f"qn{i}", [128, T], BF16) for i in range(2)]
                qr = [sb(f"qr{i}", [128, T], BF16) for i in range(2)]
                pT = [sb(f"pT{i}", [128, 512], BF16) for i in range(4)]
                rden = [sb(f"rden{i}", [128, 512], F32) for i in range(2)]
                sza = [sb(f"sza{i}", [128, 512], BF16) for i in range(2)]
                yev = [sb(f"yev{i}", [128, 512], BF16) for i in range(2)]

                e_kvl = [S.op("vector", lambda v: v.memset(krT[64:128, :], 0.0)),
                         S.op("vector", lambda v: v.memset(qr[0][64:128, :], 0.0)),
                         S.op("vector", lambda v: v.memset(qr[1][64:128, :], 0.0))]
                for r in range(2):
                    for c in range(2):
                        e_kvl.append(S.dma("sync", ckvT[:, c, r * T:(r + 1) * T], kvAll[l][320 * r + 128 * c:320 * r + 128 * (c + 1), :], S.tmpchan()))
                    e_kvl.append(S.dma("sync", krT[0:64, r * T:(r + 1) * T], kvAll[l][320 * r + 256:320 * r + 320, :], S.tmpchan()))
                e_wkv = S.dma("gpsimd", wkv[:], w_kv_b[lw].rearrange("(kc p) n -> p kc n", p=128), S.tmpchan("gpsimd"))

                ch_q = [S.chan("ch_q0"), S.chan("ch_q1")]
                ch_sza = [S.chan("ch_sza0"), S.chan("ch_sza1")]
                ch_ya = [S.chan("ch_ya0"), S.chan("ch_ya1")]
                qfree = [None, None]
                kvfree = [None, None]
                sbank = Ring([0, 1, 2])
                obank = Ring([3, 4])
                dbank = Ring([5])
                bbank = Ring([6, 7])
                pt_ring = Ring(pT)
                sza_ring = Ring(sza)
                yev_ring = Ring(yev)
                fin_i = [0]
                kb_ev = {}
                q_ev = {}

                def build_kv(h):
                    i = h % 2
                    S.dma("sync", qn[i][:], qT[l][h, 0:128, :], ch_q[i], deps=[qfree[i]])
                    q_ev[h] = S.dma("sync", qr[i][0:64, :], qT[l][h, 128:192, :], ch_q[i], deps=[qfree[i]])
                    e_kb = []
                    for kblk in range(8):
                        bj, bank, bfree = bbank.get()
                        ksl = slice(kblk * 512, (kblk + 1) * 512)
                        for kc in range(2):
                            em = S.op("tensor", lambda t, bank=bank, kc=kc, h=h, ksl=ksl: t.matmul(
                                ps[:, bank, :], wkv[:, kc, 256 * h:256 * h + 128], ckvT[:, kc, ksl], start=(kc == 0), stop=(kc == 1)),
                                deps=[e_wkv, bfree] + e_kvl if kc == 0 else [], ev=(kc == 1))
                        ee = S.op("vector", lambda v, bank=bank, i=i, ksl=ksl: v.tensor_copy(out=kT[i][:, ksl], in_=ps[:, bank, :]), deps=[em, kvfree[i]])
                        bbank.release(bj, ee)
                        e_kb.append(ee)
                    for grp in range(8):
                        bj, bank, bfree = bbank.get()
                        for jj in range(4):
                            kt = 4 * grp + jj
                            for kc in range(2):
                                em = S.op("tensor", lambda t, bank=bank, kc=kc, h=h, kt=kt, jj=jj: t.matmul(
                                    ps[:, bank, jj * 128:(jj + 1) * 128], ckvT[:, kc, kt * 128:(kt + 1) * 128],
                                    wkv[:, kc, 256 * h + 128:256 * h + 256], start=(kc == 0), stop=(kc == 1)),
                                    deps=[e_wkv, bfree] + e_kvl if (jj == 0 and kc == 0) else [], ev=(jj == 3 and kc == 1))
                        ee = S.op("vector", lambda v, bank=bank, i=i, grp=grp: v.tensor_copy(
                            out=vS[i][:, 4 * grp:4 * grp + 4, :], in_=ps[:, bank, :].rearrange("p (a b) -> p a b", a=4)), deps=[em, kvfree[i]])
                        bbank.release(bj, ee)
                        e_kb.append(ee)
                    kb_ev[h] = e_kb

                build_kv(0)
                for h in range(NH):
                    i = h % 2
                    e_q = q_ev[h]
                    e_kb = kb_ev[h]
                    last_pv = None
                    for qb in range(4):
                        qsl = slice(qb * 512, (qb + 1) * 512)
                        zj, szat, zfree = sza_ring.get()
                        e_sza = S.dma("sync", szat[:], szT[l][1024 + 128 * h:1024 + 128 * (h + 1), qsl], ch_sza[zj], deps=[zfree])
                        oj, ob, ofree = obank.get()
                        dj, db, dfree = dbank.get()
                        s_ev = {}

                        def s_mm(kt, i=i, qsl=qsl):
                            bj, bank, bfree = sbank.get()
                            S.op("tensor", lambda t: t.matmul(ps[:, bank, :], kT[i][:, kt * 128:(kt + 1) * 128], qn[i][:, qsl], start=True, stop=False),
                                 deps=[e_q, bfree] + e_kb, ev=False)
                            em = S.op("tensor", lambda t: t.matmul(ps[:, bank, :], krT[:, kt * 128:(kt + 1) * 128], qr[i][:, qsl], start=False, stop=True))
                            s_ev[kt] = (bj, bank, em)

                        s_mm(0)
                        s_mm(1)
                        for kt in range(32):
                            bj, bank, em = s_ev.pop(kt)
                            pj, ptt, pfree = pt_ring.get()
                            e_p = S.op("scalar", lambda a, bank=bank, ptt=ptt: a.activation(out=ptt[:], in_=ps[:, bank, :], func=AF.Exp, scale=SM_SCALE), deps=[em, pfree])
                            sbank.release(bj, e_p)
                            if kt + 2 < 32:
                                s_mm(kt + 2)
                            S.op("tensor", lambda t, kt=kt, ptt=ptt, ob=ob, i=i: t.matmul(ps[:, ob, :], vS[i][:, kt, :], ptt[:], start=(kt == 0), stop=(kt == 31)),
                                 deps=[e_p, ofree] if kt == 0 else [e_p], ev=False)
                            last_pv = S.op("tensor", lambda t, kt=kt, ptt=ptt, db=db: t.matmul(ps[:, db, :], ones_bf[:], ptt[:], start=(kt == 0), stop=(kt == 31)),
                                           deps=[dfree] if kt == 0 else [])
                            pt_ring.release(pj, last_pv)
                        fi = fin_i[0] % 2
                        fin_i[0] += 1
                        e0 = S.op("vector", lambda v, fi=fi, db=db: v.tensor_copy(out=rden[fi][:], in_=ps[:, db, :]), deps=[last_pv])
                        dbank.release(dj, e0)
                        e1 = S.op("vector", lambda v, fi=fi: v.reciprocal(out=rden[fi][:], in_=rden[fi][:]), deps=[e0])
                        e2 = S.op("vector", lambda v, fi=fi, ob=ob: v.tensor_tensor(out=rden[fi][:], in0=ps[:, ob, :], in1=rden[fi][:], op=ALU.mult), deps=[e1])
                        obank.release(oj, e2)
                        yj, yt, yfree = yev_ring.get()
                        e3 = S.op("vector", lambda v, fi=fi, yt=yt, szat=szat: v.tensor_tensor(out=yt[:], in0=rden[fi][:], in1=szat[:], op=ALU.mult), deps=[e2, e_sza, yfree])
                        sza_ring.release(zj, e3)
                        e_st = S.dma("gpsimd", yT[l][1024 + 128 * h:1024 + 128 * (h + 1), qsl], yt[:], ch_ya[yj], deps=[e3])
                        yev_ring.release(yj, e_st)
                        if qb == 1 and h + 1 < NH:
                            build_kv(h + 1)
                        if do_mod1:
                            slot = 4 * h + qb
                            mod_job = lambda k: (1, k * 256) if k < 24 else (0, 4096 + (k - 24) * 256)
                            if slot == 0:
                                mod_pend[0] = mod_load(mod_job(0)[0], mod_job(0)[1], 256, wa2_ring)
                            cur = mod_pend[0]
                            if slot + 1 < 32:
                                mod_pend[0] = mod_load(mod_job(slot + 1)[0], mod_job(slot + 1)[1], 256, wa2_ring)
                            mod_compute(cur, wa2_ring, bbank, badac2, modc2)
                    qfree[i] = last_pv
                    kvfree[i] = last_pv
                S.barrier()
                S.flush()
            if stop == f"b2_{l}":
                if f"yT{l}" in dbg:
                    S.dma("sync", dbg[f"yT{l}"][:, :], yT[l][:, :], ch_st)
                break
            with ExitStack() as pb3:
                sb = lambda name, shape, dty: pb3.enter_context(nc.sbuf_tensor(f"pb3{l}_" + name, shape, dty))
                yTs = sb("yTs", [128, 16, T], BF16)
                gate_rep = sb("gate_rep", [128, D], F32)
                g_rep = sb("g_rep", [128, D], F32)
                b_rep = sb("b_rep", [128, D], F32)
                xt = [sb(f"xt{i}", [128, D], F32) for i in range(2)]
                vv = [sb(f"vv{i}", [128, D], F32) for i in range(2)]
                st = [sb(f"st{i}", [128, 4, 6], F32) for i in range(2)]
                mv = [sb(f"mv{i}", [128, 2], F32) for i in range(2)]
                sm = [sb(f"sm{i}", [128, 4], F32) for i in range(2)]
                e_y = []
                for q4 in range(4):
                    e_y.append(S.dma("sync", yTs[:, 4 * q4:4 * q4 + 4, :], yT[l][512 * q4:512 * (q4 + 1), :].rearrange("(kc p) t -> p kc t", p=128), S.tmpchan()))
                e_gate = S.dma("sync", gate_rep[:], modv[l:l + 1, 2 * D:3 * D].partition_broadcast(128), S.tmpchan())
                e_g = S.dma("sync", g_rep[:], ln_g[lw:lw + 1, :].partition_broadcast(128), S.tmpchan())
                e_b = S.dma("sync", b_rep[:], ln_b[lw:lw + 1, :].partition_broadcast(128), S.tmpchan())
                ch_x = [S.chan(f"ch_x{i}") for i in range(2)]
                ch_o = [S.chan(f"ch_o{i}") for i in range(2)]
                xfree = [None, None]
                vfree = [None, None]
                obank = Ring(list(range(8)))
                for t in range(NT):
                    i = t % 2
                    e_x = S.dma("sync", xt[i][:], x_src[t * 128:(t + 1) * 128, :], ch_x[i], deps=[xfree[i]])
                    e_gm = []
                    for cb in range(4):
                        bj, bank, bfree = obank.get()
                        csl = slice(cb * 512, (cb + 1) * 512)
                        for kc in range(16):
                            em = S.op("tensor", lambda tt, bank=bank, kc=kc, t=t, csl=csl: tt.matmul(
                                ps[:, bank, :], yTs[:, kc, t * 128:(t + 1) * 128], wo[:, kc, csl], start=(kc == 0), stop=(kc == 15)),
                                deps=e_y + [e_wo[cb], bfree] if kc == 0 else [], ev=(kc == 15))
                        ee = S.op("vector", lambda v, bank=bank, i=i, csl=csl: v.tensor_tensor(out=vv[i][:, csl], in0=ps[:, bank, :], in1=gate_rep[:, csl], op=ALU.mult),
                                  deps=[em, e_gate, vfree[i]])
                        obank.release(bj, ee)
                        e_gm.append(ee)
                    e_v = S.op("vector", lambda v, i=i: v.scalar_tensor_tensor(out=vv[i][:], in0=xt[i][:], scalar=ALPHA, in1=vv[i][:], op0=ALU.mult, op1=ALU.add), deps=e_gm + [e_x])
                    xfree[i] = e_v
                    e_st = None
                    for k in range(4):
                        e_st = S.op("vector", lambda v, i=i, k=k: v.bn_stats(out=st[i][:, k, :], in_=vv[i][:, k * 512:(k + 1) * 512]), deps=[e_v], ev=(k == 3))
                    e_ag = S.op("vector", lambda v, i=i: v.bn_aggr(out=mv[i][:], in_=st[i][:].rearrange("p a b -> p (a b)")), deps=[e_st])
                    e1 = S.op("vector", lambda v, i=i: v.tensor_scalar(out=sm[i][:, 0:1], in0=mv[i][:, 1:2], scalar1=EPS, scalar2=None, op0=ALU.add), deps=[e_ag])
                    e2 = S.op("scalar", lambda a, i=i: a.activation(out=sm[i][:, 1:2], in_=sm[i][:, 0:1], func=AF.Sqrt), deps=[e1])
                    e3 = S.op("vector", lambda v, i=i: v.reciprocal(out=sm[i][:, 2:3], in_=sm[i][:, 1:2]), deps=[e2])
                    e4 = S.op("vector", lambda v, i=i: v.tensor_scalar(out=sm[i][:, 3:4], in0=mv[i][:, 0:1], scalar1=sm[i][:, 2:3], scalar2=-1.0, op0=ALU.mult, op1=ALU.mult), deps=[e3])
                    e5 = S.op("scalar", lambda a, i=i: a.activation(out=vv[i][:], in_=vv[i][:], func=AF.Identity, bias=sm[i][:, 3:4], scale=sm[i][:, 2:3]), deps=[e4])
                    e6 = S.op("vector", lambda v, i=i: v.tensor_tensor(out=vv[i][:], in0=vv[i][:], in1=g_rep[:], op=ALU.mult), deps=[e5, e_g])
                    e7 = S.op("vector", lambda v, i=i: v.tensor_tensor(out=vv[i][:], in0=vv[i][:], in1=b_rep[:], op=ALU.add), deps=[e6, e_b])
                    e_o = S.dma("gpsimd", x_dst[t * 128:(t + 1) * 128, :], vv[i][:], ch_o[i], deps=[e7])
                    vfree[i] = e_o
                S.barrier()
                S.flush()
            pbw.close()
            if stop == f"b3_{l}":
                if "x1" in dbg:
                    S.dma("sync", dbg["x1"][:, :], x1[:, :], ch_st)
                break
            x_src = x1

        S.barrier()
        S.flush()
    return nc


_PROGRAM_CACHE = {}


def _host_consts():
    m = np.arange(128, dtype=np.float64)
    angc = 2 * np.pi * np.outer(m, m) / 128.0
    def hilo(a):
        hi = a.astype(np.float32).astype(ml_dtypes.bfloat16)
        lo = (a - hi.astype(np.float64)).astype(np.float32).astype(ml_dtypes.bfloat16)
        return np.ascontiguousarray(np.stack([hi, lo], axis=1))
    cc = hilo(np.cos(angc))
    sc = hilo(np.sin(angc))
    invf = (10000.0 ** (-np.arange(0, 64, 2, dtype=np.float32) / np.float32(64))).astype(np.float32)
    invf2 = np.concatenate([invf, invf]).reshape(64, 1).astype(np.float32)
    ident = np.eye(128, dtype=np.float32).astype(ml_dtypes.bfloat16)
    return cc, sc, invf2, ident


def _token_lists():
    return [np.arange(0, 2048), np.arange(2048, 4096)]


def _tables(tok_lists):
    s_order = np.concatenate([tok_lists[0][:1024], tok_lists[1][:1024], tok_lists[0][1024:], tok_lists[1][1024:]])
    tabs = []
    for r in range(2):
        k = tok_lists[r].astype(np.int64)
        ph = (np.outer(s_order.astype(np.int64), k) % SEQ).astype(np.float64) * (2 * np.pi / SEQ)
        cosm = np.cos(ph).reshape(SEQ, 8, 256)
        sinm = np.sin(ph).reshape(SEQ, 8, 256)
        tb = np.concatenate([cosm, sinm], axis=2).transpose(1, 0, 2)
        tabs.append(np.ascontiguousarray(tb).astype(ml_dtypes.bfloat16))
    return tabs


def make_in_maps(x, c, positions, w_ada, b_ada, w_in, q_norm, w_q_b, kv_norm, w_kv_b, w_fmix, w_out, ln_g, ln_b):
    tl = _token_lists()
    tabs = _tables(tl)
    cc, sc, invf2, ident = _host_consts()
    f32 = lambda a: np.ascontiguousarray(np.asarray(a, dtype=np.float32))
    shared = {
        "w_ada": f32(w_ada), "b_ada": f32(b_ada), "w_in": f32(w_in), "w_q_b": f32(w_q_b),
        "w_kv_b": f32(w_kv_b), "w_fmix": f32(w_fmix), "w_out": f32(w_out), "ln_g": f32(ln_g), "ln_b": f32(ln_b),
        "qn_t": f32(np.asarray(q_norm).reshape(DEPTH, 4, 128).transpose(0, 2, 1)),
        "kvn_t": f32(np.asarray(kv_norm).reshape(DEPTH, 2, 128).transpose(0, 2, 1)),
        "cc": cc, "sc": sc, "invf": invf2, "ident": ident,
    }
    if LITE:
        for k in ["w_ada", "b_ada", "w_in", "w_q_b", "w_kv_b", "w_fmix", "w_out", "ln_g", "ln_b", "qn_t", "kvn_t"]:
            shared[k] = np.ascontiguousarray(shared[k][:1])
        shared["w_ada"] = np.ascontiguousarray(shared["w_ada"][:1, :128, :512])
        tabs = [np.ascontiguousarray(t[:1]) for t in tabs]
    in_maps = []
    for core in range(8):
        b, r = core // 2, core % 2
        m = dict(shared)
        m["x"] = f32(np.asarray(x)[b][tl[r]])
        m["cvec"] = f32(np.asarray(c)[b].reshape(16, 128).T)
        m["pos"] = np.ascontiguousarray(np.asarray(positions)[b][tl[r]].astype(np.int32).reshape(1, T))
        m["tab"] = tabs[r]
        in_maps.append(m)
    return in_maps, tl


def kernel(x, c, positions, w_ada, b_ada, w_in, q_norm, w_q_b, kv_norm, w_kv_b, w_fmix, w_out, ln_g, ln_b):
    in_maps, tl = make_in_maps(x, c, positions, w_ada, b_ada, w_in, q_norm, w_q_b, kv_norm, w_kv_b,
                               w_fmix, w_out, ln_g, ln_b)
    if "full" not in _PROGRAM_CACHE:
        _PROGRAM_CACHE["full"] = build_program()
    nc = _PROGRAM_CACHE["full"]
    res = run_bass_kernel_spmd(nc, in_maps, core_ids=list(range(8)))
    outp = np.empty((4, SEQ, D), dtype=np.float32)
    for core in range(8):
        b, r = core // 2, core % 2
        outp[b][tl[r]] = res.results[core]["out"]
    return outp
```
